# Optimizing a Trainium2 kernel written in Bass

```python
import math
import jax, jax.numpy as jnp
from jax import lax
import numpy as np

D_MODEL = 2048
BATCH = 4
SEQ = 4096
DEPTH = 4

N_A = DEPTH // 2
N_B = DEPTH - N_A
GDN_QK_HEADS = 16
GDN_V_HEADS = 32
GDN_HEAD_DIM = 128
GDN_CONV = 4
GDN_CHUNK = 64
GDN_QK_DIM = GDN_QK_HEADS * GDN_HEAD_DIM
GDN_V_DIM = GDN_V_HEADS * GDN_HEAD_DIM
GDN_CONV_DIM = 2 * GDN_QK_DIM + GDN_V_DIM
GDN_IN_DIM = GDN_CONV_DIM + GDN_V_DIM + 2 * GDN_V_HEADS
DIFF_HEADS = D_MODEL // 256
DIFF_QK_DIM = 128
DIFF_V_DIM = 2 * DIFF_QK_DIM
DIFF_Q_WIDTH = DIFF_HEADS * 2 * DIFF_QK_DIM
DIFF_KV_WIDTH = DIFF_Q_WIDTH + DIFF_HEADS * DIFF_V_DIM
Q_BLOCK = 128
D_FF = 4 * D_MODEL
PLE_DIM = 256
EPS = 1e-6

kernel_name = "yoco_gdn_diffattn_hybrid"


def rmsnorm(x, w, eps=EPS):
    xf = x.astype(jnp.float32)
    y = xf * lax.rsqrt(jnp.mean(xf * xf, axis=-1, keepdims=True) + eps)
    return (y * w.astype(jnp.float32)).astype(x.dtype)


def l2norm(x, eps=1e-6):
    xf = x.astype(jnp.float32)
    return xf * lax.rsqrt(jnp.sum(xf * xf, axis=-1, keepdims=True) + eps)


def causal_depthwise_conv(x, w):
    return lax.conv_general_dilated(
        x, w[:, None, :].astype(x.dtype), window_strides=(1,),
        padding=[(GDN_CONV - 1, 0)],
        dimension_numbers=('NWC', 'WIO', 'NWC'),
        feature_group_count=x.shape[-1])


def gated_delta_rule(q, k, v, g, beta):
    B, T, H, dk = q.shape
    dv = v.shape[-1]
    C = GDN_CHUNK
    N = T // C

    def to_chunks(a):
        return a.reshape(B, N, C, H, a.shape[-1]).transpose(1, 0, 3, 2, 4)

    q, k, v = to_chunks(q), to_chunks(k), to_chunks(v)
    g = g.reshape(B, N, C, H).transpose(1, 0, 3, 2)
    beta = beta.reshape(B, N, C, H).transpose(1, 0, 3, 2)
    gc = jnp.cumsum(g, axis=-1)
    causal = jnp.tril(jnp.ones((C, C), dtype=bool))
    strict = jnp.tril(jnp.ones((C, C), dtype=bool), k=-1)
    L = jnp.exp(jnp.where(causal, gc[..., :, None] - gc[..., None, :], -jnp.inf))
    kb = k * beta[..., None]
    A = jnp.where(strict, jnp.einsum('nbhid,nbhjd->nbhij', kb, k) * L, 0.0)
    eye = jnp.eye(C, dtype=jnp.float32)
    rhs = jnp.concatenate([v * beta[..., None], kb * jnp.exp(gc)[..., None]], axis=-1)
    sol = lax.linalg.triangular_solve(A + eye, rhs, left_side=True, lower=True, unit_diagonal=True)
    u, w = sol[..., :dv], sol[..., dv:]
    qk = jnp.where(causal, jnp.einsum('nbhid,nbhjd->nbhij', q, k) * L, 0.0)
    q_dec = q * jnp.exp(gc)[..., None]
    k_dec = k * jnp.exp(gc[..., -1:] - gc)[..., None]
    g_last = jnp.exp(gc[..., -1])

    def step(S, xs):
        qd, kd, u_c, w_c, qk_c, gl = xs
        v_new = u_c - jnp.einsum('bhcd,bhde->bhce', w_c, S)
        o = jnp.einsum('bhcd,bhde->bhce', qd, S) + jnp.einsum('bhij,bhje->bhie', qk_c, v_new)
        S = S * gl[..., None, None] + jnp.einsum('bhcd,bhce->bhde', kd, v_new)
        return S, o

    S0 = jnp.zeros((B, H, dk, dv), jnp.float32)
    _, o = lax.scan(step, S0, (q_dec, k_dec, u, w, qk, g_last))
    return o.transpose(1, 0, 3, 2, 4).reshape(B, T, H, dv)


def gated_deltanet(xn, w_in, conv_w, a_log, dt_bias, norm_w, w_out):
    B, T, _ = xn.shape
    proj = xn @ w_in
    s1 = GDN_CONV_DIM
    s2 = s1 + GDN_V_DIM
    s3 = s2 + GDN_V_HEADS
    qkv, z, b, a = proj[..., :s1], proj[..., s1:s2], proj[..., s2:s3], proj[..., s3:]
    qkv = jax.nn.silu(causal_depthwise_conv(qkv, conv_w))
    q = qkv[..., :GDN_QK_DIM].reshape(B, T, GDN_QK_HEADS, GDN_HEAD_DIM)
    k = qkv[..., GDN_QK_DIM:2 * GDN_QK_DIM].reshape(B, T, GDN_QK_HEADS, GDN_HEAD_DIM)
    v = qkv[..., 2 * GDN_QK_DIM:].reshape(B, T, GDN_V_HEADS, GDN_HEAD_DIM).astype(jnp.float32)
    rep = GDN_V_HEADS // GDN_QK_HEADS
    q = jnp.repeat(l2norm(q), rep, axis=2) * (GDN_HEAD_DIM ** -0.5)
    k = jnp.repeat(l2norm(k), rep, axis=2)
    beta = jax.nn.sigmoid(b.astype(jnp.float32))
    g = -jnp.exp(a_log.astype(jnp.float32)) * jax.nn.softplus(a.astype(jnp.float32) + dt_bias.astype(jnp.float32))
    o = gated_delta_rule(q, k, v, g, beta)
    o = rmsnorm(o, norm_w) * jax.nn.silu(z.reshape(B, T, GDN_V_HEADS, GDN_HEAD_DIM).astype(jnp.float32))
    return o.reshape(B, T, GDN_V_DIM).astype(xn.dtype) @ w_out


def alibi_slopes(n_heads):
    return jnp.exp2(-8.0 * jnp.arange(1, n_heads + 1, dtype=jnp.float32) / n_heads)


def diff_attention(xn, k, v, w_q, lq1, lk1, lq2, lk2, subln_w, w_o, layer_idx):
    B, T, _ = xn.shape
    q = (xn @ w_q).reshape(B, T, DIFF_HEADS, 2, DIFF_QK_DIM) * (DIFF_QK_DIM ** -0.5)
    lam_init = 0.8 - 0.6 * math.exp(-0.3 * layer_idx)
    lam = (jnp.exp(jnp.sum(lq1.astype(jnp.float32) * lk1.astype(jnp.float32)))
           - jnp.exp(jnp.sum(lq2.astype(jnp.float32) * lk2.astype(jnp.float32))) + lam_init)
    slopes = alibi_slopes(DIFF_HEADS)
    key_pos = jnp.arange(T)

    def block(i):
        start = i * Q_BLOCK
        qb = lax.dynamic_slice_in_dim(q, start, Q_BLOCK, axis=1)
        scores = jnp.einsum('bqhcd,bkhcd->bhcqk', qb, k, preferred_element_type=jnp.float32)
        dist = (start + jnp.arange(Q_BLOCK))[:, None] - key_pos[None, :]
        bias = jnp.where(dist >= 0, -slopes[:, None, None] * dist.astype(jnp.float32), -jnp.inf)
        probs = jax.nn.softmax(scores + bias[None, :, None], axis=-1)
        attn = probs[:, :, 0] - lam * probs[:, :, 1]
        return jnp.einsum('bhqk,bkhe->bqhe', attn.astype(v.dtype), v)

    o = lax.map(block, jnp.arange(T // Q_BLOCK))
    o = o.transpose(1, 0, 2, 3, 4).reshape(B, T, DIFF_HEADS, DIFF_V_DIM)
    o = rmsnorm(o, subln_w) * (1.0 - lam_init)
    return o.reshape(B, T, DIFF_HEADS * DIFF_V_DIM) @ w_o


def setup_inputs(seed: int = 0) -> dict:
    key = jax.random.key(seed)
    ks = jax.random.split(key, 26)
    f32 = jnp.float32

    def dense(k, shape, fan_in):
        return jax.random.normal(k, shape, f32) * (fan_in ** -0.5)

    def gain(k, shape):
        return 1.0 + 0.01 * jax.random.normal(k, shape, f32)

    x = jax.random.normal(ks[0], (BATCH, SEQ, D_MODEL), f32)
    p = jax.random.normal(ks[1], (DEPTH, BATCH, SEQ, PLE_DIM), f32)
    norm_mix = gain(ks[2], (DEPTH, D_MODEL))
    norm_mlp = gain(ks[3], (DEPTH, D_MODEL))
    norm_ple = gain(ks[4], (DEPTH, D_MODEL))
    gdn_w_in = dense(ks[5], (N_A, D_MODEL, GDN_IN_DIM), D_MODEL)
    gdn_conv_w = jax.random.normal(ks[6], (N_A, GDN_CONV, GDN_CONV_DIM), f32) * (GDN_CONV ** -0.5)
    gdn_a_log = jnp.log(jax.random.uniform(ks[7], (N_A, GDN_V_HEADS), f32, 1.0, 16.0))
    dt = jnp.exp(jax.random.uniform(ks[8], (N_A, GDN_V_HEADS), f32, math.log(1e-3), math.log(1e-1)))
    gdn_dt_bias = dt + jnp.log(-jnp.expm1(-dt))
    gdn_norm_w = gain(ks[9], (N_A, GDN_HEAD_DIM))
    gdn_w_out = dense(ks[10], (N_A, GDN_V_DIM, D_MODEL), GDN_V_DIM)
    kv_norm = gain(ks[11], (D_MODEL,))
    w_kv = dense(ks[12], (D_MODEL, DIFF_KV_WIDTH), D_MODEL)
    diff_w_q = dense(ks[13], (N_B, D_MODEL, DIFF_Q_WIDTH), D_MODEL)
    diff_lambda_q1 = 0.1 * jax.random.normal(ks[14], (N_B, DIFF_QK_DIM), f32)
    diff_lambda_k1 = 0.1 * jax.random.normal(ks[15], (N_B, DIFF_QK_DIM), f32)
    diff_lambda_q2 = 0.1 * jax.random.normal(ks[16], (N_B, DIFF_QK_DIM), f32)
    diff_lambda_k2 = 0.1 * jax.random.normal(ks[17], (N_B, DIFF_QK_DIM), f32)
    diff_subln_w = gain(ks[18], (N_B, DIFF_V_DIM))
    diff_w_o = dense(ks[19], (N_B, DIFF_HEADS * DIFF_V_DIM, D_MODEL), DIFF_HEADS * DIFF_V_DIM)
    mlp_w1 = dense(ks[20], (DEPTH, D_MODEL, D_FF), D_MODEL)
    mlp_w2 = dense(ks[21], (DEPTH, D_FF, D_MODEL), D_FF)
    ple_w_proj = dense(ks[22], (DEPTH, PLE_DIM, D_MODEL), PLE_DIM)
    ple_w_gate = dense(ks[23], (DEPTH, D_MODEL, D_MODEL), D_MODEL)
    final_norm = gain(ks[24], (D_MODEL,))
    return {
        'x': x, 'p': p, 'norm_mix': norm_mix, 'norm_mlp': norm_mlp, 'norm_ple': norm_ple,
        'gdn_w_in': gdn_w_in, 'gdn_conv_w': gdn_conv_w, 'gdn_a_log': gdn_a_log,
        'gdn_dt_bias': gdn_dt_bias, 'gdn_norm_w': gdn_norm_w, 'gdn_w_out': gdn_w_out,
        'kv_norm': kv_norm, 'w_kv': w_kv, 'diff_w_q': diff_w_q,
        'diff_lambda_q1': diff_lambda_q1, 'diff_lambda_k1': diff_lambda_k1,
        'diff_lambda_q2': diff_lambda_q2, 'diff_lambda_k2': diff_lambda_k2,
        'diff_subln_w': diff_subln_w, 'diff_w_o': diff_w_o,
        'mlp_w1': mlp_w1, 'mlp_w2': mlp_w2, 'ple_w_proj': ple_w_proj, 'ple_w_gate': ple_w_gate,
        'final_norm': final_norm,
    }


def reference(x, p, norm_mix, norm_mlp, norm_ple, gdn_w_in, gdn_conv_w, gdn_a_log, gdn_dt_bias,
              gdn_norm_w, gdn_w_out, kv_norm, w_kv, diff_w_q, diff_lambda_q1, diff_lambda_k1,
              diff_lambda_q2, diff_lambda_k2, diff_subln_w, diff_w_o, mlp_w1, mlp_w2,
              ple_w_proj, ple_w_gate, final_norm):
    B, T, _ = x.shape
    h = x
    k_shared = None
    v_shared = None
    for i in range(DEPTH):
        xn = rmsnorm(h, norm_mix[i])
        if i < N_A:
            h = h + gated_deltanet(xn, gdn_w_in[i], gdn_conv_w[i], gdn_a_log[i], gdn_dt_bias[i],
                                   gdn_norm_w[i], gdn_w_out[i])
        else:
            j = i - N_A
            h = h + diff_attention(xn, k_shared, v_shared, diff_w_q[j], diff_lambda_q1[j],
                                   diff_lambda_k1[j], diff_lambda_q2[j], diff_lambda_k2[j],
                                   diff_subln_w[j], diff_w_o[j], i)
        hn = rmsnorm(h, norm_mlp[i])
        h = h + jnp.square(jax.nn.relu(hn @ mlp_w1[i])) @ mlp_w2[i]
        gate = jax.nn.sigmoid(rmsnorm(h, norm_ple[i]) @ ple_w_gate[i])
        h = h + (p[i] @ ple_w_proj[i]) * gate
        if i == N_A - 1:
            kv = rmsnorm(h, kv_norm) @ w_kv
            k_shared = kv[..., :DIFF_Q_WIDTH].reshape(B, T, DIFF_HEADS, 2, DIFF_QK_DIM)
            v_shared = kv[..., DIFF_Q_WIDTH:].reshape(B, T, DIFF_HEADS, DIFF_V_DIM)
    return rmsnorm(h, final_norm)
```

```python
import numpy as np
import concourse.bass as bass
import concourse.mybir as mybir
from concourse.bass_utils import run_bass_kernel_spmd

F32 = mybir.dt.float32
BF16 = mybir.dt.bfloat16
AF = mybir.ActivationFunctionType
ALU = mybir.AluOpType
AX = mybir.AxisListType

ENGS = ("pe", "act", "dve", "pool", "sp")
DMA_RING = {"sp": 12, "pool": 12, "act": 6, "cc": 4}


class Op:
    __slots__ = ("eng", "fn", "reads", "writes", "dma", "waits", "signal", "sem", "count", "idx", "inc")

    def __init__(self, eng, fn, reads, writes, dma, inc=16):
        self.eng, self.fn, self.reads, self.writes, self.dma = eng, fn, reads, writes, dma
        self.inc = inc if dma else 1
        self.waits = []
        self.signal = False
        self.sem = None
        self.count = 0


def _key(r):
    return r if isinstance(r, (str, tuple, int)) else r.name


class Prog:
    def __init__(self, kb, same_engine_sync=True):
        self.kb = kb
        self.nc = kb.nc
        self.ops = []
        self.same_engine_sync = same_engine_sync

    def add(self, eng, fn, reads=(), writes=(), dma=False, inc=16):
        op = Op(eng, fn, tuple(_key(r) for r in reads), tuple(_key(w) for w in writes), dma, inc)
        op.idx = len(self.ops)
        self.ops.append(op)
        return op

    def pe(self, fn, reads=(), writes=()):
        return self.add("pe", fn, reads, writes)

    def act(self, fn, reads=(), writes=()):
        return self.add("act", fn, reads, writes)

    def dve(self, fn, reads=(), writes=()):
        return self.add("dve", fn, reads, writes)

    def pool(self, fn, reads=(), writes=()):
        return self.add("pool", fn, reads, writes)

    def dma(self, q, out, in_, reads=(), writes=(), **kw):
        return self.add(q, lambda e: e.dma_start(out=out, in_=in_, **kw), reads, writes, dma=True)

    def _schedule(self):
        last_write = {}
        readers = {}
        deps_of = []
        for op in self.ops:
            deps = set()
            for r in op.reads:
                lw = last_write.get(r)
                if lw is not None:
                    deps.add(lw)
            for w in op.writes:
                lw = last_write.get(w)
                if lw is not None:
                    deps.add(lw)
                for rd in readers.get(w, ()):
                    deps.add(rd)
            for r in op.reads:
                readers.setdefault(r, []).append(op.idx)
            for w in op.writes:
                last_write[w] = op.idx
                readers[w] = []
            deps.discard(op.idx)
            deps_of.append(deps)

        for op in self.ops:
            for j in deps_of[op.idx]:
                d = self.ops[j]
                if d.dma:
                    continue
                if d.eng == op.eng and not op.dma:
                    if op.eng == "pe" or not self.same_engine_sync:
                        continue
                d.signal = True

        eng_count = {e: 0 for e in ENGS}
        dma_n = {q: 0 for q in DMA_RING}
        persist = getattr(self.kb, "persist", None)
        if persist is None:
            persist = self.kb.persist = {}
        dma_slot_count = dict(persist)
        dma_prev = {}
        for op in self.ops:
            if op.dma:
                q = op.eng if op.inc != 1 else "cc"
                slot = dma_n[q] % DMA_RING[q]
                dma_n[q] += 1
                key = ("dma", q, slot)
                prev = dma_prev.get(key)
                if prev is not None:
                    deps_of[op.idx].add(prev)
                dma_slot_count[key] = dma_slot_count.get(key, 0) + op.inc
                op.sem, op.count, op.signal = key, dma_slot_count[key], True
                dma_prev[key] = op.idx
            elif op.signal:
                eng_count[op.eng] += 1
                op.sem, op.count = ("eng", op.eng), eng_count[op.eng]
        assert max(list(eng_count.values()) + list(dma_slot_count.values()) + [0]) < 60000, eng_count

        known = {e: {} for e in ENGS}
        snap = [None] * len(self.ops)
        for op in self.ops:
            kn = known[op.eng]
            need = {}
            for j in deps_of[op.idx]:
                d = self.ops[j]
                if not d.signal:
                    continue
                if d.eng == op.eng and not d.dma and not op.dma:
                    if op.eng == "pe" or not self.same_engine_sync:
                        continue
                if kn.get(d.sem, 0) >= d.count:
                    continue
                if need.get(d.sem, (0, None))[0] < d.count:
                    need[d.sem] = (d.count, j)
            for sem, (cnt, j) in need.items():
                op.waits.append((sem, cnt))
                if kn.get(sem, 0) < cnt:
                    kn[sem] = cnt
                sj = snap[j]
                if sj:
                    for s2, c2 in sj.items():
                        if kn.get(s2, 0) < c2:
                            kn[s2] = c2
            snap[op.idx] = dict(kn) if op.signal else None
        self.final_dma = {k: v for k, v in dma_slot_count.items() if v != persist.get(k, 0) or k[1] not in ("pool", "cc")}
        for k, v in dma_slot_count.items():
            if k[1] in ("pool", "cc"):
                persist[k] = v
        return sorted({op.sem for op in self.ops if op.signal}, key=str)

    def emit(self):
        nc = self.nc
        if not self.ops:
            return
        used = self._schedule()
        sem = self.kb.sem
        for key in used:
            if key not in sem:
                sem[key] = nc.alloc_semaphore("s_" + "_".join(str(k) for k in key))
        with nc.Block() as cb:
            def clr(e):
                for key in used:
                    if not (key[0] == "dma" and key[1] in ("pool", "cc")):
                        e.sem_clear(sem[key])
            cb.gpsimd(clr)
        by_eng = {e: [] for e in ENGS}
        for op in self.ops:
            by_eng[op.eng].append(op)
        final_dma = self.final_dma
        with nc.Block() as block:
            def section(ename):
                def body(e):
                    for op in by_eng[ename]:
                        for s, c in op.waits:
                            e.wait_ge(sem[s], c)
                        ins = op.fn(e)
                        if op.signal:
                            ins.then_inc(sem[op.sem], op.inc)
                    if ename == "sp":
                        for key, cnt in final_dma.items():
                            e.wait_ge(sem[key], cnt)
                return body

            block.tensor(section("pe"))
            block.scalar(section("act"))
            block.vector(section("dve"))
            block.gpsimd(section("pool"))
            block.sync(section("sp"))
        self.ops = []


D = 2048
T = 4096
TO = 2048
NT = TO // 128
KD = D // 128
FF = 8192
DEPTH = 4
N_A = 2
EPS = 1e-6
HL = 16
HQ = 8
NCOLS_IN = 6176
NVC = 13 * 16 + 2
BIG = 30000.0
import os
DEBUG_CUT = int(os.environ.get('MK_DEBUG_CUT', '0'))
SUBCUT = int(os.environ.get('MK_SUBCUT', '9'))


class KB:
    def __init__(self):
        self.nc = bass.Bass("TRN2", target_bir_lowering=False)
        self.sem = {}
        self.uid = 0
        self.rr = {}

    def din(self, name, shape, dt=F32):
        return self.nc.dram_tensor(name, list(shape), dt, kind="ExternalInput").ap()

    def dscr(self, name, shape, dt):
        return self.nc.dram_tensor(name, list(shape), dt, kind="Internal").ap()

    def sb(self, st, name, shape, dt):
        self.uid += 1
        return st.enter_context(self.nc.sbuf_tensor(f"{name}_{self.uid}", list(shape), dt))

    def ps(self, st, name, shape, dt=F32):
        self.uid += 1
        return st.enter_context(self.nc.psum_tensor(f"{name}_{self.uid}", list(shape), dt))

    def prog(self):
        return Prog(self)


class Ring:
    def __init__(self, bufs):
        self.bufs = bufs
        self.i = 0

    def next(self):
        b = self.bufs[self.i % len(self.bufs)]
        self.i += 1
        return b


def w_view(w2d):
    return w2d.rearrange("(k p) n -> p k n", p=128)


def hk(ht):
    return [(ht.name, d) for d in range(4)]


def norm_T(kb, P, st, C, hts, nvcol, xT, ps_tr):
    n = len(hts)
    ss = kb.sb(st, "ss", [128, n], F32)
    sd = kb.sb(st, "sd", [128, n], F32)
    rstd = kb.sb(st, "rstd", [128, n], F32)
    junk = kb.sb(st, "junk", [128, D], BF16)
    xs = [kb.sb(st, "xs", [128, D], BF16) for _ in range(4)]
    NV, identb = C["NV"], C["identb"]
    P.dve(lambda e: e.memset(ss[:], 0.0), writes=[ss])
    for t, h in enumerate(hts):
        P.act(lambda e, h=h, t=t: e.activation(out=junk[:], in_=h[:], func=AF.Square, accum_out=ss[:, t:t + 1]),
              reads=hk(h) + [ss], writes=[junk, (ss.name, t)])
    P.act(lambda e: e.activation(out=sd[:], in_=ss[:], func=AF.Sqrt, bias=C["eps"][:, 0:1], scale=1.0 / D),
          reads=[ss] + [(ss.name, t) for t in range(n)], writes=[sd])
    P.dve(lambda e: e.reciprocal(out=rstd[:], in_=sd[:]), reads=[sd], writes=[rstd])
    cnt = 0
    for g in range(n // 4):
        for j in range(4):
            t = g * 4 + j
            P.act(lambda e, t=t, j=j: e.activation(out=xs[j][:], in_=hts[t][:], func=AF.Copy, scale=rstd[:, t:t + 1]),
                  reads=hk(hts[t]) + [rstd], writes=[xs[j]])
        for k2 in range(KD // 2):
            pt = ps_tr.next()
            for kk in range(2):
                k = k2 * 2 + kk
                for j in range(4):
                    c0 = kk * 512 + j * 128
                    P.pe(lambda e, pt=pt, j=j, k=k, c0=c0: e.transpose(out=pt[:, c0:c0 + 128], in_=xs[j][:, k * 128:(k + 1) * 128],
                                                                     identity=identb[:]),
                         reads=[xs[j], identb], writes=[pt])
            for kk in range(2):
                k = k2 * 2 + kk
                dst = xT[:, k, g * 512:(g + 1) * 512]
                src = pt[:, kk * 512:(kk + 1) * 512]
                if cnt % 2 == 0:
                    P.dve(lambda e, dst=dst, src=src, k=k: e.tensor_scalar(out=dst, in0=src, scalar1=NV[:, nvcol + k:nvcol + k + 1],
                                                                         scalar2=None, op0=ALU.mult),
                          reads=[NV], writes=[pt, (xT.name, k, g)])
                else:
                    P.act(lambda e, dst=dst, src=src, k=k: e.activation(out=dst, in_=src, func=AF.Copy,
                                                                      scale=NV[:, nvcol + k:nvcol + k + 1]),
                          reads=[NV], writes=[pt, (xT.name, k, g)])
            cnt += 1


def tr_ring(kb, st, nbank=2):
    return Ring([kb.ps(st, "ptr", [128, 1024], BF16) for _ in range(nbank)])


def xT_keys(xT, g):
    return [(xT.name, k, g) for k in range(KD)]


def mlp_pass(kb, C, H, w1, w2, nvcol):
    import contextlib
    for tb in range(int(os.environ.get("MK_NTB", "2"))):
        with contextlib.ExitStack() as st:
            HT = [kb.sb(st, "ht", [128, D], F32) for _ in range(8)]
            hnT = kb.sb(st, "hnT", [128, KD, 1024], BF16)
            with contextlib.ExitStack() as st1:
                P = kb.prog()
                for t in range(8):
                    r0 = (tb * 8 + t) * 128
                    P.dma("sp", HT[t][:], H[r0:r0 + 128, :], writes=hk(HT[t]))
                norm_T(kb, P, st1, C, HT, nvcol, hnT, tr_ring(kb, st1))
                P.emit()
            if DEBUG_CUT == 1 or (tb == 1 and os.environ.get("MK_SKIPB2")):
                continue
            with contextlib.ExitStack() as st2:
                P = kb.prog()
                gT = [kb.sb(st2, "gT", [128, 4, 1024], BF16) for _ in range(2)]
                w1b = [kb.sb(st2, "w1b", [128, KD, 512], BF16) for _ in range(2)]
                w2b = [kb.sb(st2, "w2b", [128, 4, D], BF16) for _ in range(2)]
                rt = Ring([kb.sb(st2, "rt", [128, 512], F32) for _ in range(2)])
                psu = Ring([kb.ps(st2, "psu", [128, 512]) for _ in range(3)])
                psy = Ring([kb.ps(st2, "psy", [128, 512]) for _ in range(3)])
                w1v = w_view(w1)

                def loadw(fb):
                    b = fb % 2
                    P.dma("pool", w1b[b][:], w1v[:, :, fb * 512:(fb + 1) * 512], writes=[w1b[b]])
                    if DEBUG_CUT != 2:
                        P.dma("pool", w2b[b][:], w2[fb * 512:(fb + 1) * 512, :].rearrange("(c p) n -> p c n", p=128), writes=[w2b[b]])

                def stage1(fb):
                    b = fb % 2
                    for fc in range(4):
                        for tg in range(2):
                            pu = psu.next()
                            for k in range(KD if SUBCUT >= 0 else 0):
                                P.pe(lambda e, pu=pu, k=k, fc=fc, tg=tg: e.matmul(pu[:], lhsT=w1b[b][:, k, fc * 128:(fc + 1) * 128],
                                                                                 rhs=hnT[:, k, tg * 512:(tg + 1) * 512],
                                                                                 start=(k == 0), stop=(k == KD - 1)),
                                     reads=[w1b[b], (hnT.name, k, tg)], writes=[pu])
                            r = rt.next()
                            if SUBCUT >= 1:
                                P.act(lambda e, pu=pu, r=r: e.activation(out=r[:], in_=pu[:], func=AF.Relu), reads=[], writes=[pu, r])
                            if SUBCUT >= 2:
                                P.act(lambda e, r=r, fc=fc, tg=tg: e.activation(out=gT[b][:, fc, tg * 512:(tg + 1) * 512], in_=r[:], func=AF.Square),
                                      reads=[r], writes=[(gT[b].name, fc, tg)])

                def stage2(fb):
                    b = fb % 2
                    if DEBUG_CUT == 2:
                        return
                    for t in range(8):
                        for db in range(4):
                            py = psy.next()
                            for fc in range(4):
                                P.pe(lambda e, py=py, fc=fc, t=t, db=db: e.matmul(py[:], lhsT=gT[b][:, fc, t * 128:(t + 1) * 128],
                                                                                 rhs=w2b[b][:, fc, db * 512:(db + 1) * 512],
                                                                                 start=(fc == 0), stop=(fc == 3)),
                                     reads=[w2b[b], (gT[b].name, fc, t // 4)], writes=[py])
                            hs = HT[t][:, db * 512:(db + 1) * 512]
                            P.dve(lambda e, hs=hs, py=py: e.tensor_tensor(out=hs, in0=hs, in1=py[:], op=ALU.add),
                                  reads=[(HT[t].name, db)], writes=[py, (HT[t].name, db)])

                NFB = int(os.environ.get("MK_NFB", "16"))
                loadw(0)
                stage1(0)
                for fb in range(NFB):
                    if fb + 1 < NFB:
                        loadw(fb + 1)
                        stage1(fb + 1)
                    stage2(fb)
                for t in range(8):
                    r0 = (tb * 8 + t) * 128
                    P.dma("sp", H[r0:r0 + 128, :], HT[t][:], reads=hk(HT[t]))
                P.emit()


def ple_pass(kb, C, H, p_l, wple, wgate, nvcol):
    import contextlib
    for tb in range(2):
        with contextlib.ExitStack() as st:
            HT = [kb.sb(st, "ht", [128, D], F32) for _ in range(8)]
            hpT = kb.sb(st, "hpT", [128, KD, 1024], BF16)
            pT = kb.sb(st, "pT", [128, 2, 1024], BF16)
            with contextlib.ExitStack() as st1:
                P = kb.prog()
                for t in range(8):
                    r0 = (tb * 8 + t) * 128
                    P.dma("sp", HT[t][:], H[r0:r0 + 128, :], writes=hk(HT[t]))
                trr = tr_ring(kb, st1)
                norm_T(kb, P, st1, C, HT, nvcol, hpT, trr)
                pf = Ring([kb.sb(st1, "pf", [128, 256], F32) for _ in range(2)])
                pb = Ring([kb.sb(st1, "pb", [128, 256], BF16) for _ in range(2)])
                for t in range(8):
                    r0 = (tb * 8 + t) * 128
                    f, b_ = pf.next(), pb.next()
                    P.dma("sp", f[:], p_l[r0:r0 + 128, :], writes=[f])
                    P.act(lambda e, f=f, b_=b_: e.activation(out=b_[:], in_=f[:], func=AF.Copy), reads=[f], writes=[b_])
                    pt = trr.next()
                    for c in range(2):
                        P.pe(lambda e, pt=pt, c=c, b_=b_: e.transpose(out=pt[:, c * 128:(c + 1) * 128],
                                                                    in_=b_[:, c * 128:(c + 1) * 128], identity=C["identb"][:]),
                             reads=[b_, C["identb"]], writes=[pt])
                    for c in range(2):
                        P.dve(lambda e, pt=pt, c=c, t=t: e.tensor_copy(out=pT[:, c, t * 128:(t + 1) * 128],
                                                                     in_=pt[:, c * 128:(c + 1) * 128]),
                              reads=[], writes=[pt, (pT.name, c, t)])
                P.emit()
            with contextlib.ExitStack() as st2:
                P = kb.prog()
                wg = [kb.sb(st2, "wg", [128, KD, 512], BF16) for _ in range(2)]
                wp = kb.sb(st2, "wp", [128, 2, D], BF16)
                gs = Ring([kb.sb(st2, "gs", [128, 512], F32) for _ in range(2)])
                tm = Ring([kb.sb(st2, "tm", [128, 512], F32) for _ in range(2)])
                psg = Ring([kb.ps(st2, "psg", [128, 512]) for _ in range(3)])
                psp = Ring([kb.ps(st2, "psp", [128, 512]) for _ in range(3)])
                P.dma("pool", wp[:], wple.rearrange("(c p) n -> p c n", p=128), writes=[wp])
                wgv = w_view(wgate)
                P.dma("pool", wg[0][:], wgv[:, :, 0:512], writes=[wg[0]])
                for db in range(4):
                    b = db % 2
                    if db + 1 < 4:
                        P.dma("pool", wg[1 - b][:], wgv[:, :, (db + 1) * 512:(db + 2) * 512], writes=[wg[1 - b]])
                    for t in range(8):
                        pg, pp = psg.next(), psp.next()
                        for k in range(KD):
                            P.pe(lambda e, pg=pg, k=k, t=t, b=b: e.matmul(pg[:], lhsT=hpT[:, k, t * 128:(t + 1) * 128], rhs=wg[b][:, k, :],
                                                                        start=(k == 0), stop=(k == KD - 1)),
                                 reads=[wg[b], (hpT.name, k, t // 4)], writes=[pg])
                        for c in range(2):
                            P.pe(lambda e, pp=pp, c=c, t=t, db=db: e.matmul(pp[:], lhsT=pT[:, c, t * 128:(t + 1) * 128],
                                                                          rhs=wp[:, c, db * 512:(db + 1) * 512], start=(c == 0), stop=(c == 1)),
                                 reads=[wp, (pT.name, c, t)], writes=[pp])
                        g_, m_ = gs.next(), tm.next()
                        P.act(lambda e, g_=g_, pg=pg: e.activation(out=g_[:], in_=pg[:], func=AF.Sigmoid), reads=[], writes=[pg, g_])
                        P.dve(lambda e, m_=m_, pp=pp, g_=g_: e.tensor_tensor(out=m_[:], in0=pp[:], in1=g_[:], op=ALU.mult),
                              reads=[g_], writes=[pp, m_])
                        hs = HT[t][:, db * 512:(db + 1) * 512]
                        P.dve(lambda e, hs=hs, m_=m_: e.tensor_tensor(out=hs, in0=hs, in1=m_[:], op=ALU.add),
                               reads=[m_, (HT[t].name, db)], writes=[(HT[t].name, db)])
                for t in range(8):
                    r0 = (tb * 8 + t) * 128
                    P.dma("sp", H[r0:r0 + 128, :], HT[t][:], reads=hk(HT[t]))
                P.emit()


def final_pass(kb, C, H, out, fnorm):
    import contextlib
    with contextlib.ExitStack() as st:
        P = kb.prog()
        FN = kb.sb(st, "FN", [128, D], F32)
        P.dma("sp", FN[:], fnorm, writes=[FN])
        hts = Ring([kb.sb(st, "ht", [128, D], F32) for _ in range(3)])
        ots = Ring([kb.sb(st, "ot", [128, D], F32) for _ in range(2)])
        junk = kb.sb(st, "junk", [128, D], BF16)
        ss = kb.sb(st, "ss", [128, NT], F32)
        sd = kb.sb(st, "sd", [128, NT], F32)
        rs = kb.sb(st, "rs", [128, NT], F32)
        P.dve(lambda e: e.memset(ss[:], 0.0), writes=[ss])
        for t in range(NT):
            h, o = hts.next(), ots.next()
            P.dma("sp", h[:], H[t * 128:(t + 1) * 128, :], writes=[h])
            P.act(lambda e, h=h, t=t: e.activation(out=junk[:], in_=h[:], func=AF.Square, accum_out=ss[:, t:t + 1]),
                  reads=[h, ss], writes=[junk, (ss.name, t)])
            P.act(lambda e, t=t: e.activation(out=sd[:, t:t + 1], in_=ss[:, t:t + 1], func=AF.Sqrt, bias=C["eps"][:, 0:1], scale=1.0 / D),
                  reads=[(ss.name, t)], writes=[(sd.name, t)])
            P.dve(lambda e, t=t: e.reciprocal(out=rs[:, t:t + 1], in_=sd[:, t:t + 1]), reads=[(sd.name, t)], writes=[(rs.name, t)])
            P.dve(lambda e, h=h, o=o, t=t: e.scalar_tensor_tensor(out=o[:], in0=h[:], scalar=rs[:, t:t + 1], in1=FN[:],
                                                                op0=ALU.mult, op1=ALU.mult),
                  reads=[h, (rs.name, t), FN], writes=[o])
            P.dma("sp", out[t * 128:(t + 1) * 128, :], o[:], reads=[o])
        P.emit()


def build(stages):
    import contextlib
    kb = KB()
    nc = kb.nc
    x_own = kb.din("x_own", [TO, D])
    out = nc.dram_tensor("out", [TO, D], F32, kind="ExternalOutput").ap()
    H = kb.dscr("H", [TO, D], F32)
    cf = kb.din("cf", [128, 5 * 128])
    nv = kb.din("nv", [128, NVC])
    used = {"x_own", "cf", "nv"}
    ins = {}

    def inp(name, shape):
        if name not in ins:
            ins[name] = kb.din(name, shape)
            used.add(name)
        return ins[name]

    with contextlib.ExitStack() as gst:
        C = {}
        CF = kb.sb(gst, "CF", [128, 5, 128], F32)
        C["CF"] = CF
        C["identb"] = kb.sb(gst, "identb", [128, 128], BF16)
        C["onesb"] = kb.sb(gst, "onesb", [128, 128], BF16)
        C["NV"] = kb.sb(gst, "NV", [128, NVC], F32)
        C["eps"] = kb.sb(gst, "eps", [128, 1], F32)
        P = kb.prog()
        P.dma("sp", CF[:], cf.rearrange("p (a b) -> p a b", a=5), writes=[CF])
        P.dma("sp", C["NV"][:], nv, writes=[C["NV"]])
        P.dve(lambda e: e.tensor_copy(out=C["identb"][:], in_=CF[:, 0, :]), reads=[CF], writes=[C["identb"]])
        P.dve(lambda e: e.tensor_copy(out=C["onesb"][:], in_=CF[:, 4, :]), reads=[CF], writes=[C["onesb"]])
        P.dve(lambda e: e.memset(C["eps"][:], EPS), writes=[C["eps"]])
        for i in range(4):
            P.dma("sp", H[i * 512:(i + 1) * 512, :], x_own[i * 512:(i + 1) * 512, :], writes=[("H", i)])
        P.emit()

        C["one"] = kb.sb(gst, "one", [128, 1], F32)
        P = kb.prog()
        P.dve(lambda e: e.memset(C["one"][:], 1.0), writes=[C["one"]])
        P.emit()
        G = None
        S_ = {}

        def gdn_setup():
            nonlocal G
            if G is not None:
                return
            G = {}
            G["CW"] = kb.sb(gst, "CW", [128, 2, 32, 4], F32)
            G["AB"] = kb.sb(gst, "AB", [128, 2, 2, 16], F32)
            G["BG"] = kb.sb(gst, "BG", [128, 32, 32], F32)
            G["NEA"] = kb.sb(gst, "NEA", [128, 16], F32)
            G["SEL16"] = kb.sb(gst, "SEL16", [16, 16, 128], F32)
            G["SELC"] = kb.sb(gst, "SELC", [128, 2], F32)
            P = kb.prog()
            P.dma("sp", G["CW"][:], inp("convw", [128, 256]).rearrange("p (l c j) -> p l c j", l=2, c=32), writes=[G["CW"]])
            P.dma("sp", G["AB"][:], inp("ab", [128, 64]).rearrange("p (l a h) -> p l a h", l=2, a=2), writes=[G["AB"]])
            P.dma("sp", G["SEL16"][:], inp("sel16", [16, 2048]).rearrange("p (h m) -> p h m", h=16), writes=[G["SEL16"]])
            P.dma("sp", G["SELC"][:], inp("selc", [128, 2]), writes=[G["SELC"]])
            G["lvlmask"] = inp("lvlmask", [128, 7680])
            P.emit()
            S_["XNT_own"] = kb.dscr("XNT_own", [2048, TO], BF16)
            S_["XNT_all"] = kb.dscr("XNT_all", [8, 512, TO], BF16)
            S_["QKVZ"] = kb.dscr("QKVZ", [48, 128, T], BF16)
            S_["OT_lo"] = kb.dscr("OT_lo", [2048, TO], BF16)
            S_["OT_hi"] = kb.dscr("OT_hi", [2048, TO], BF16)
            S_["G_lo"] = kb.dscr("G_lo", [8, 512, TO], BF16)
            S_["G_hi"] = kb.dscr("G_hi", [8, 512, TO], BF16)

        A = {}

        def att_setup():
            if A:
                return
            A["LAMV"] = kb.sb(gst, "LAMV", [128, 2, 4, 128], F32)
            A["SUBW"] = kb.sb(gst, "SUBW", [128, 2, 256], F32)
            A["SW"] = kb.sb(gst, "SW", [128, 256], F32)
            A["nlam"] = kb.sb(gst, "nlam", [128, 1], F32)
            A["l2t"] = kb.sb(gst, "l2t", [128, 128], F32)
            A["lsum"] = kb.sb(gst, "lsum", [128, 2], F32)
            A["BT"] = kb.sb(gst, "BT", [128, 8, 32], F32)
            A["MA"] = kb.sb(gst, "MA", [128, 128], BF16)
            A["MB"] = kb.sb(gst, "MB", [128, 128], BF16)
            P = kb.prog()
            P.dma("sp", A["LAMV"][:], inp("lamv", [128, 1024]).rearrange("p (l a d) -> p l a d", l=2, a=4), writes=[A["LAMV"]])
            P.dma("sp", A["SUBW"][:], inp("sublnw", [128, 512]).rearrange("p (l d) -> p l d", l=2), writes=[A["SUBW"]])
            P.dma("sp", A["BT"][:], inp("bt", [128, 256]).rearrange("p (h d) -> p h d", h=8), writes=[A["BT"]])
            mab = inp("mab", [128, 256])
            P.dma("pool", A["MA"][:], mab[:, 0:128], writes=[A["MA"]])
            P.dma("pool", A["MB"][:], mab[:, 128:256], writes=[A["MB"]])
            P.emit()
            S_["KT_own"] = kb.dscr("KT_own", [2048, TO], BF16)
            S_["KT_all"] = kb.dscr("KT_all", [8, 512, TO], BF16)
            S_["V_own"] = kb.dscr("V_own", [TO, 2048], BF16)
            S_["V_all"] = kb.dscr("V_all", [8, 512, 2048], BF16)
            S_["QT"] = kb.dscr("QT", [16, 128, TO], BF16)

        for stg in stages:
            kind, li = stg[:-1], int(stg[-1]) if stg[-1].isdigit() else None
            if kind == "gdn":
                gdn_setup()
                S_["out"] = out
                gi = {"w_in": [inp(f"w_in_{l}", [D, NCOLS_IN]) if l == li else None for l in range(2)],
                      "w_out": [inp(f"w_out_{l}", [4096, D]) if l == li else None for l in range(2)]}
                gdn_layer(kb, C, G, li, H, S_, gi)
            elif stg == "kv":
                att_setup()
                kv_pass(kb, C, H, inp("w_kv", [D, 4096]), 12 * 16, S_["KT_own"], S_["V_own"])
                allgather(kb, S_["KT_own"], S_["KT_all"])
                allgather(kb, S_["V_own"], S_["V_all"])
            elif kind == "att":
                att_setup()
                attn_layer(kb, C, A, li, H, S_, inp(f"w_q_{li - 2}", [D, D]), inp(f"w_o_{li - 2}", [D, D]), li * 16)
            elif kind == "mlp":
                w1 = inp(f"mlp_w1_{li}", [D, FF])
                w2 = inp(f"mlp_w2_{li}", [FF, D])
                mlp_pass(kb, C, H, w1, w2, (4 + li) * 16)
            elif kind == "ple":
                p_own = inp("p_own", [DEPTH, TO, 256])
                wple = inp(f"ple_w_proj_{li}", [256, D])
                wgate = inp(f"ple_w_gate_{li}", [D, D])
                ple_pass(kb, C, H, p_own[li], wple, wgate, (8 + li) * 16)
            elif stg == "final":
                final_pass(kb, C, H, out, inp("fnorm", [128, D]))
            elif stg == "outh":
                P = kb.prog()
                for i in range(4):
                    P.dma("sp", out[i * 512:(i + 1) * 512, :], H[i * 512:(i + 1) * 512, :])
                P.emit()
            else:
                raise ValueError(stg)
    return nc, sorted(used)


def _fm(v):
    return np.ascontiguousarray(np.asarray(v, np.float32).reshape(16, 128).T)


def host_consts():
    i = np.arange(128)
    ident = np.eye(128, dtype=np.float32)
    U = (i[:, None] <= i[None, :]).astype(np.float32)
    maskl = np.where(i[:, None] > i[None, :], 0.0, BIG).astype(np.float32)
    masklt = np.where(i[None, :] >= i[:, None], 0.0, BIG).astype(np.float32)
    ones = np.ones((128, 128), np.float32)
    return np.concatenate([ident, U, maskl, masklt, ones], axis=1)


def prep_inputs(inputs, used):
    g = {k: np.asarray(v) for k, v in inputs.items()}
    cf = host_consts()
    nvl = [_fm(g["norm_mix"][i]) for i in range(4)] + [_fm(g["norm_mlp"][i]) for i in range(4)] + \
          [_fm(g["norm_ple"][i]) for i in range(4)] + [_fm(g["kv_norm"])]
    nvl.append(np.ascontiguousarray(g["gdn_norm_w"].astype(np.float32).T))
    nv = np.ascontiguousarray(np.concatenate(nvl, axis=1))
    fnorm = np.ascontiguousarray(np.broadcast_to(g["final_norm"].astype(np.float32)[None, :], (128, D)))
    maps = []
    for c in range(8):
        b, s = c // 2, c % 2
        m = {}
        m["x_own"] = np.ascontiguousarray(g["x"][b, s * TO:(s + 1) * TO])
        m["p_own"] = np.ascontiguousarray(g["p"][:, b, s * TO:(s + 1) * TO, :])
        m["cf"] = cf
        m["nv"] = nv
        m["fnorm"] = fnorm
        pp = np.arange(128, dtype=np.float32)
        bt = np.zeros((128, 8, 32), np.float32)
        for hh in range(8):
            slope = 2.0 ** -(hh + 1)
            for dd in range(32):
                dg = dd - 16 + 16 * s
                bt[:, hh, dd] = slope * (pp - 127.0 - 128.0 * dg) if dg >= 0 else -BIG
        m["bt"] = bt.reshape(128, 256)
        tri = (pp[None, :] >= pp[:, None]).astype(np.float32)
        mab = np.zeros((128, 256), np.float32)
        mab[:, 0:128] = tri if s == 0 else 1.0
        mab[:, 128:256] = tri if s == 1 else 0.0
        m["mab"] = mab
        lamv = np.stack([np.stack([g["diff_lambda_q1"][jj], g["diff_lambda_k1"][jj], g["diff_lambda_q2"][jj], g["diff_lambda_k2"][jj]]) for jj in range(2)])
        m["lamv"] = np.ascontiguousarray(np.broadcast_to(lamv.astype(np.float32).reshape(1, 1024), (128, 1024)))
        m["sublnw"] = np.ascontiguousarray(np.broadcast_to(g["diff_subln_w"].astype(np.float32).reshape(1, 512), (128, 512)))
        m["w_kv"] = g["w_kv"]
        for jj in range(2):
            m[f"w_q_{jj}"] = g["diff_w_q"][jj]
            m[f"w_o_{jj}"] = g["diff_w_o"][jj]
        sel16 = np.zeros((16, 16, 128), np.float32)
        for hh in range(16):
            sel16[hh, hh, :] = 1.0
        m["sel16"] = sel16.reshape(16, 2048)
        ii = np.arange(128)
        lm = np.zeros((128, 15, 4, 128), np.float32)
        for l in range(7):
            bsz = 2 ** l
            mk = (((ii[:, None] // bsz) % 2 == 1) & ((ii[None, :] // bsz) == (ii[:, None] // bsz) - 1)).astype(np.float32)
            lm[:, l, :, :] = mk[:, None, :]
            lm[:, 7 + l, :, :] = mk.T[:, None, :]
        lm[:, 14, :, :] = np.eye(128, dtype=np.float32)[:, None, :]
        m["lvlmask"] = lm.reshape(128, 7680)
        selc = np.zeros((128, 2), np.float32)
        selc[:, s] = 1.0
        m["selc"] = selc
        cw = np.zeros((128, 2, 32, 4), np.float32)
        ab = np.zeros((128, 2, 2, 16), np.float32)
        for l in range(2):
            if f"w_in_{l}" in used:
                wi = g["gdn_w_in"][l]
                m[f"w_in_{l}"] = np.ascontiguousarray(np.concatenate([
                    wi[:, s * 1024:(s + 1) * 1024], wi[:, 2048 + s * 1024:2048 + (s + 1) * 1024],
                    wi[:, 4096 + s * 2048:4096 + (s + 1) * 2048], wi[:, 8192 + s * 2048:8192 + (s + 1) * 2048],
                    wi[:, 12288 + s * 16:12288 + (s + 1) * 16], wi[:, 12320 + s * 16:12320 + (s + 1) * 16]], axis=1))
            if f"w_out_{l}" in used:
                m[f"w_out_{l}"] = g["gdn_w_out"][l]
            cwl = g["gdn_conv_w"][l]
            own = np.concatenate([cwl[:, s * 1024:(s + 1) * 1024], cwl[:, 2048 + s * 1024:2048 + (s + 1) * 1024],
                                  cwl[:, 4096 + s * 2048:4096 + (s + 1) * 2048]], axis=1)
            cw[:, l] = own.reshape(4, 32, 128).transpose(2, 1, 0)
            ab[:, l, 0, :] = g["gdn_a_log"][l, s * 16:(s + 1) * 16][None, :]
            ab[:, l, 1, :] = g["gdn_dt_bias"][l, s * 16:(s + 1) * 16][None, :]
        m["convw"] = cw.reshape(128, 256)
        m["ab"] = ab.reshape(128, 64)
        for k in ("mlp_w1", "mlp_w2", "ple_w_proj", "ple_w_gate"):
            for li in range(DEPTH):
                if f"{k}_{li}" in used:
                    m[f"{k}_{li}"] = g[k][li]
        maps.append({k: v for k, v in m.items() if k in used})
    return maps


ALL_STAGES = ["gdn0", "mlp0", "ple0", "gdn1", "mlp1", "ple1", "kv", "att2", "mlp2", "ple2", "att3", "mlp3", "ple3", "final"]
_CACHE = {}


def run_stages(inputs, stages):
    key = tuple(stages)
    if key not in _CACHE:
        _CACHE[key] = build(stages)
    nc, used = _CACHE[key]
    maps = prep_inputs(inputs, set(used))
    res = run_bass_kernel_spmd(nc, maps, core_ids=list(range(8)))
    outs = [r["out"] for r in res.results]
    full = np.empty((4, T, D), np.float32)
    for c in range(8):
        full[c // 2, (c % 2) * TO:(c % 2 + 1) * TO] = outs[c]
    return full


def kernel(**inputs):
    return run_stages(inputs, ALL_STAGES)


def gdn_norm_gather(kb, C, H, nvcol, XNT_own, XNT_all):
    import contextlib
    xv = XNT_own.rearrange("(k p) t -> p k t", p=128)
    for tb in range(2):
        with contextlib.ExitStack() as st:
            HT = [kb.sb(st, "ht", [128, D], F32) for _ in range(8)]
            xT = kb.sb(st, "xT", [128, KD, 1024], BF16)
            P = kb.prog()
            for t in range(8):
                r0 = (tb * 8 + t) * 128
                P.dma("sp", HT[t][:], H[r0:r0 + 128, :], writes=hk(HT[t]))
            norm_T(kb, P, st, C, HT, nvcol, xT, tr_ring(kb, st))
            for kh in range(2):
                P.dma("sp", xv[:, kh * 8:(kh + 1) * 8, tb * 1024:(tb + 1) * 1024], xT[:, kh * 8:(kh + 1) * 8, :],
                      reads=[(xT.name, k, g) for k in range(kh * 8, kh * 8 + 8) for g in range(2)])
            P.emit()
    allgather(kb, XNT_own, XNT_all)


PAIRS = [[0, 1], [2, 3], [4, 5], [6, 7]]


def allgather(kb, src, dst, nch=8):
    R = src.shape[0]
    rc = R // nch
    P = kb.prog()
    for j in range(nch):
        P.add("pool", lambda e, j=j: e.collective_compute("AllGather", ALU.bypass, replica_groups=PAIRS,
                                                        ins=[src[j * rc:(j + 1) * rc, :].opt()], outs=[dst[j].opt()]),
              dma=True, inc=1)
    P.emit()


def gdn_inproj(kb, C, G, li, XNT_all, w_in, QKVZ):
    import contextlib
    with contextlib.ExitStack() as st:
        XT = kb.sb(st, "XT", [128, KD, T], BF16)
        wb = [kb.sb(st, "wb", [128, KD, 256], BF16) for _ in range(2)]
        obuf = [kb.sb(st, "ob", [128, T], BF16) for _ in range(2)]
        xbuf = [kb.sb(st, "xb", [128, 515], F32) for _ in range(2)]
        cacc = Ring([kb.sb(st, "ca", [128, 512], F32) for _ in range(2)])
        sl = Ring([kb.sb(st, "sl", [128, 512], F32) for _ in range(2)])
        sq = Ring([kb.sb(st, "sq", [128, 512], BF16) for _ in range(2)])
        sd = Ring([kb.sb(st, "sd", [128, 512], F32) for _ in range(2)])
        ri = Ring([kb.sb(st, "ri", [128, 512], F32) for _ in range(2)])
        wba = kb.sb(st, "wba", [128, KD, 32], BF16)
        xa = kb.sb(st, "xa", [128, 16], F32)
        ea = kb.sb(st, "ea", [128, 16], F32)
        sp_ = kb.sb(st, "sp", [128, 16], F32)
        psa = Ring([kb.ps(st, "psa", [128, 512]) for _ in range(3)])
        pss = Ring([kb.ps(st, "pss", [128, 512]) for _ in range(2)])
        psb = Ring([kb.ps(st, "psb", [128, 512]) for _ in range(2)])
        CW, AB, BG, NEA = G["CW"], G["AB"], G["BG"], G["NEA"]
        onesb, eps = C["onesb"], C["eps"]
        P = kb.prog()
        for r in range(2):
            for j in range(8):
                P.dma("sp", XT[:, 2 * j:2 * j + 2, r * TO:(r + 1) * TO],
                      XNT_all[j, r * 256:(r + 1) * 256, :].rearrange("(k p) t -> p k t", p=128),
                      writes=[(XT.name, r, j)])
        P.act(lambda e: e.activation(out=NEA[:], in_=AB[:, li, 0, :], func=AF.Exp), reads=[AB], writes=[NEA])
        P.dve(lambda e: e.tensor_scalar(out=NEA[:], in0=NEA[:], scalar1=-1.0, scalar2=None, op0=ALU.mult), reads=[NEA], writes=[NEA])
        wv = w_view(w_in)
        P.dma("pool", wba[:], wv[:, :, 6144:6176], writes=[wba])

        def loadw(wbk):
            P.dma("pool", wb[wbk % 2][:], wv[:, :, wbk * 256:(wbk + 1) * 256], writes=[wb[wbk % 2]])

        loadw(0)
        for wbk in range(24):
            if wbk + 1 < 24:
                loadw(wbk + 1)
            b = wbk % 2
            for cti in range(2):
                ct = wbk * 2 + cti
                typ = "q" if ct < 8 else "k" if ct < 16 else "v" if ct < 32 else "z"
                ob = obuf[ct % 2]
                for blk in range(8):
                    pa = psa.next()
                    osl = ob[:, blk * 512:(blk + 1) * 512]
                    okey = (ob.name, blk)
                    for k in range(KD):
                        P.pe(lambda e, pa=pa, k=k, cti=cti, blk=blk, b=b: e.matmul(
                            pa[:], lhsT=wb[b][:, k, cti * 128:(cti + 1) * 128], rhs=XT[:, k, blk * 512:(blk + 1) * 512],
                            start=(k == 0), stop=(k == KD - 1)),
                            reads=[wb[b], (XT.name, blk // 4, k // 2)], writes=[pa])
                    if typ == "z":
                        P.act(lambda e, pa=pa, osl=osl: e.activation(out=osl, in_=pa[:], func=AF.Silu), writes=[pa, okey])
                        continue
                    xb_, prev = xbuf[blk % 2], xbuf[(blk + 1) % 2]
                    if blk == 0:
                        P.dve(lambda e, xb_=xb_: e.memset(xb_[:, 0:3], 0.0), writes=[(xb_.name, "h")])
                    else:
                        P.dve(lambda e, xb_=xb_, prev=prev: e.tensor_copy(out=xb_[:, 0:3], in_=prev[:, 512:515]),
                              reads=[prev], writes=[(xb_.name, "h")])
                    P.act(lambda e, pa=pa, xb_=xb_: e.activation(out=xb_[:, 3:515], in_=pa[:], func=AF.Copy), writes=[pa, xb_])
                    ca = cacc.next()
                    P.dve(lambda e, ca=ca, xb_=xb_, ct=ct: e.tensor_scalar(out=ca[:], in0=xb_[:, 3:515], scalar1=CW[:, li, ct, 3:4],
                                                                        scalar2=None, op0=ALU.mult),
                          reads=[xb_, CW], writes=[ca])
                    for j in (2, 1, 0):
                        P.dve(lambda e, ca=ca, xb_=xb_, ct=ct, j=j: e.scalar_tensor_tensor(
                            out=ca[:], in0=xb_[:, j:j + 512], scalar=CW[:, li, ct, j:j + 1], in1=ca[:], op0=ALU.mult, op1=ALU.add),
                            reads=[xb_, (xb_.name, "h"), CW], writes=[ca])
                    if typ == "v":
                        P.act(lambda e, ca=ca, osl=osl: e.activation(out=osl, in_=ca[:], func=AF.Silu), reads=[ca], writes=[okey])
                        continue
                    s_, q_, d_, r_ = sl.next(), sq.next(), sd.next(), ri.next()
                    P.act(lambda e, ca=ca, s_=s_: e.activation(out=s_[:], in_=ca[:], func=AF.Silu), reads=[ca], writes=[s_])
                    P.act(lambda e, s_=s_, q_=q_: e.activation(out=q_[:], in_=s_[:], func=AF.Square), reads=[s_], writes=[q_])
                    ps_ = pss.next()
                    P.pe(lambda e, ps_=ps_, q_=q_: e.matmul(ps_[:], lhsT=onesb[:], rhs=q_[:], start=True, stop=True),
                         reads=[q_, onesb], writes=[ps_])
                    P.act(lambda e, ps_=ps_, d_=d_: e.activation(out=d_[:], in_=ps_[:], func=AF.Sqrt, bias=eps[:, 0:1], scale=1.0),
                          reads=[eps], writes=[ps_, d_])
                    P.dve(lambda e, d_=d_, r_=r_: e.reciprocal(out=r_[:], in_=d_[:]), reads=[d_], writes=[r_])
                    qs = (128.0 ** -0.5) if typ == "q" else 1.0
                    P.dve(lambda e, s_=s_, r_=r_, osl=osl, qs=qs: e.scalar_tensor_tensor(out=osl, in0=s_[:], scalar=qs, in1=r_[:],
                                                                                     op0=ALU.mult, op1=ALU.mult),
                          reads=[s_, r_], writes=[okey])
                P.dma("sp", QKVZ[ct], ob[:], reads=[(ob.name, blk) for blk in range(8)])
        for tt in range(32):
            pb = psb.next()
            for k in range(KD):
                P.pe(lambda e, pb=pb, k=k, tt=tt: e.matmul(pb[:, 0:32], lhsT=XT[:, k, tt * 128:(tt + 1) * 128], rhs=wba[:, k, :],
                                                        start=(k == 0), stop=(k == KD - 1)),
                     reads=[wba, (XT.name, tt // 16, k // 2)], writes=[pb])
            P.act(lambda e, pb=pb, tt=tt: e.activation(out=BG[:, tt, 0:16], in_=pb[:, 0:16], func=AF.Sigmoid), writes=[pb, (BG.name, tt, 0)])
            P.dve(lambda e, pb=pb: e.tensor_tensor(out=xa[:], in0=pb[:, 16:32], in1=AB[:, li, 1, :], op=ALU.add), reads=[AB], writes=[pb, xa])
            P.act(lambda e: e.activation(out=ea[:], in_=xa[:], func=AF.Exp), reads=[xa], writes=[ea])
            P.act(lambda e: e.activation(out=sp_[:], in_=ea[:], func=AF.Ln, bias=C["one"][:, 0:1], scale=1.0), reads=[ea, C["one"]], writes=[sp_])
            P.dve(lambda e, tt=tt: e.tensor_tensor(out=BG[:, tt, 16:32], in0=sp_[:], in1=NEA[:], op=ALU.mult),
                  reads=[sp_, NEA], writes=[(BG.name, tt, 1)])
        P.emit()


def gdn_scan(kb, C, G, li, QKVZ, OT_lo, OT_hi):
    import contextlib
    CF, identb, onesb, NV, eps = C["CF"], C["identb"], C["onesb"], C["NV"], C["eps"]
    identf, U, MASKL, MASKLT = CF[:, 0, :], CF[:, 1, :], CF[:, 2, :], CF[:, 3, :]
    onesf = CF[:, 4, :]
    BG, SEL16 = G["BG"], G["SEL16"]
    qkvz_v = QKVZ.rearrange("c p t -> p c t")
    with contextlib.ExitStack() as st:
        S = kb.sb(st, "S", [128, HL, 128], F32)
        Sbf = kb.sb(st, "Sbf", [128, HL, 128], BF16)
        inb = [kb.sb(st, "inb", [128, 48, 512], BF16) for _ in range(1)]
        otst = [kb.sb(st, "otst", [128, HL, 512], BF16) for _ in range(1)]
        tk = {n: kb.sb(st, n, [128, 16], F32) for n in ("gc", "egc", "bege", "edl", "negb", "glb")}
        gcT = kb.sb(st, "gcT", [16, 128], F32)

        def gt(name, dt, depth=2):
            return Ring([kb.sb(st, name, [128, 4, 128], dt) for _ in range(depth)])

        Lm, LT, ER = gt("Lm", F32), gt("LT", F32), gt("ER", F32)
        tmpA, tmpB = gt("tmpA", F32), gt("tmpB", F32)
        Yb = [gt("Y0", BF16)]
        Pb = [gt("P0", BF16)]
        ymr, pmr, w1r, w2r, tcr, ttr = gt("YM", BF16), gt("PM", BF16), gt("W1", BF16), gt("W2", BF16), gt("Tc", BF16), gt("TTc", BF16)
        t1, rr, vnew, MT, qd, kdec = gt("t1", F32), gt("rr", BF16), gt("vnew", BF16), gt("MT", BF16), gt("qd", BF16), gt("kdec", BF16)
        osb, osq, sdn, rsn, onn = gt("osb", F32), gt("osq", BF16), gt("sdn", F32), gt("rsn", F32), gt("onn", F32)
        psf = Ring([kb.ps(st, "psf", [128, 4, 128]) for _ in range(6)])
        psh = Ring([kb.ps(st, "psh", [128, 8, 128], BF16) for _ in range(2)])

        G["LM"] = kb.sb(st, "LM", [128, 15, 4, 128], BF16)
        P = kb.prog()
        P.dma("pool", G["LM"][:], G["lvlmask"].rearrange("p (l a m) -> p l a m", l=15, a=4), writes=[G["LM"]])
        P.dve(lambda e: e.memset(S[:], 0.0), writes=[S])
        P.dve(lambda e: e.memset(Sbf[:], 0.0), writes=[Sbf])
        P.emit()

        for sc in range(8):
            P = kb.prog()
            ib = inb[0]
            ot = otst[0]
            for q4 in range(4):
                P.dma("sp", ib[:, q4 * 12:(q4 + 1) * 12, :], qkvz_v[:, q4 * 12:(q4 + 1) * 12, sc * 512:(sc + 1) * 512],
                      writes=[(ib.name, q4)])
            ibk = [(ib.name, q4) for q4 in range(4)]
            def chunk(cl, sc=sc, P=P):
                n = sc * 4 + cl
                cs = slice(cl * 128, (cl + 1) * 128)
                qT = lambda hq: ib[:, hq, cs]
                kT = lambda hq: ib[:, 8 + hq, cs]
                vT = lambda h: ib[:, 16 + h, cs]
                zT = lambda h: ib[:, 32 + h, cs]
                beta, g_ = BG[:, n, 0:16], BG[:, n, 16:32]
                bgk = [(BG.name, n, 0), (BG.name, n, 1)]
                pg = psf.next()
                P.pe(lambda e, pg=pg, g_=g_: e.matmul(pg[:, 0, 0:16], lhsT=U, rhs=g_, start=True, stop=True), reads=[CF] + bgk, writes=[pg])
                P.pe(lambda e, pg=pg, g_=g_: e.matmul(pg[:, 1, 0:16], lhsT=onesf, rhs=g_, start=True, stop=True), reads=[CF] + bgk, writes=[pg])
                P.dve(lambda e, pg=pg: e.tensor_copy(out=tk["gc"][:], in_=pg[:, 0, 0:16]), writes=[pg, tk["gc"]])
                P.dve(lambda e, pg=pg: e.tensor_copy(out=tk["glb"][:], in_=pg[:, 1, 0:16]), writes=[pg, tk["glb"]])
                P.act(lambda e: e.activation(out=tk["egc"][:], in_=tk["gc"][:], func=AF.Exp), reads=[tk["gc"]], writes=[tk["egc"]])
                P.dve(lambda e, beta=beta: e.tensor_tensor(out=tk["bege"][:], in0=tk["egc"][:], in1=beta, op=ALU.mult),
                      reads=[tk["egc"]] + bgk, writes=[tk["bege"]])
                P.dve(lambda e: e.tensor_tensor(out=tk["edl"][:], in0=tk["glb"][:], in1=tk["gc"][:], op=ALU.subtract),
                      reads=[tk["glb"], tk["gc"]], writes=[tk["edl"]])
                P.act(lambda e: e.activation(out=tk["edl"][:], in_=tk["edl"][:], func=AF.Exp), reads=[tk["edl"]], writes=[tk["edl"]])
                P.dve(lambda e, beta=beta: e.tensor_scalar(out=tk["negb"][:], in0=beta, scalar1=-1.0, scalar2=None, op0=ALU.mult),
                      reads=bgk, writes=[tk["negb"]])
                pt_ = psf.next()
                P.pe(lambda e, pt_=pt_: e.transpose(out=pt_[0:16, 0, :], in_=tk["gc"][:], identity=identf), reads=[tk["gc"], CF], writes=[pt_])
                P.dve(lambda e, pt_=pt_: e.tensor_copy(out=gcT[:], in_=pt_[0:16, 0, :]), writes=[pt_, gcT])

                def group(g4):
                    hs = [g4 * 4 + i for i in range(4)]
                    hqs = [g4 * 2, g4 * 2 + 1]
                    Lm_, LT_, ER_, tA, tB = Lm.next(), LT.next(), ER.next(), tmpA.next(), tmpB.next()
                    pgr = psf.next()
                    for i, h in enumerate(hs):
                        P.pe(lambda e, pgr=pgr, i=i, h=h: e.matmul(pgr[:, i, :], lhsT=SEL16[:, h, :], rhs=gcT[:], start=True, stop=True),
                             reads=[SEL16, gcT], writes=[pgr])
                    for i, h in enumerate(hs):
                        gcol = tk["gc"][:, h:h + 1]
                        P.dve(lambda e, pgr=pgr, i=i, gcol=gcol, tA=tA: e.scalar_tensor_tensor(
                            out=tA[:, i, :], in0=pgr[:, i, :], scalar=gcol, in1=MASKL, op0=ALU.subtract, op1=ALU.add),
                            reads=[tk["gc"], CF], writes=[pgr, (tA.name, i)])
                        P.dve(lambda e, pgr=pgr, i=i, gcol=gcol, tB=tB: e.scalar_tensor_tensor(
                            out=tB[:, i, :], in0=pgr[:, i, :], scalar=gcol, in1=MASKLT, op0=ALU.subtract, op1=ALU.subtract),
                            reads=[tk["gc"], CF], writes=[pgr, (tB.name, i)])
                    P.act(lambda e, pgr=pgr, ER_=ER_: e.activation(out=ER_[:], in_=pgr[:], func=AF.Exp), writes=[pgr] + [(ER_.name, i) for i in range(4)])
                    P.act(lambda e, tA=tA, Lm_=Lm_: e.activation(out=Lm_[:], in_=tA[:], func=AF.Exp, scale=-1.0),
                          reads=[(tA.name, i) for i in range(4)], writes=[(Lm_.name, i) for i in range(4)])
                    P.act(lambda e, tB=tB, LT_=LT_: e.activation(out=LT_[:], in_=tB[:], func=AF.Exp),
                          reads=[(tB.name, i) for i in range(4)], writes=[(LT_.name, i) for i in range(4)])
                    pkk = psf.next()
                    for a, hq in enumerate(hqs):
                        P.pe(lambda e, pkk=pkk, a=a, hq=hq: e.matmul(pkk[:, a, :], lhsT=kT(hq), rhs=kT(hq), start=True, stop=True),
                             reads=ibk, writes=[pkk])
                        P.pe(lambda e, pkk=pkk, a=a, hq=hq: e.matmul(pkk[:, 2 + a, :], lhsT=kT(hq), rhs=qT(hq), start=True, stop=True),
                             reads=ibk, writes=[pkk])
                    Y, Pm = Yb[0].next(), Pb[0].next()
                    MT_ = MT.next()
                    for i, h in enumerate(hs):
                        P.dve(lambda e, pkk=pkk, i=i, h=h, Y=Y, Lm_=Lm_: e.scalar_tensor_tensor(
                            out=Y[:, i, :], in0=pkk[:, i // 2, :], scalar=tk["negb"][:, h:h + 1], in1=Lm_[:, i, :], op0=ALU.mult, op1=ALU.mult),
                            reads=[tk["negb"], (Lm_.name, i)], writes=[pkk, (Y.name, i)])
                        P.dve(lambda e, pkk=pkk, i=i, MT_=MT_, LT_=LT_: e.tensor_tensor(
                            out=MT_[:, i, :], in0=pkk[:, 2 + i // 2, :], in1=LT_[:, i, :], op=ALU.mult),
                            reads=[(LT_.name, i)], writes=[pkk, (MT_.name, i)])
                    ph = psh.next()
                    for i in range(4):
                        P.pe(lambda e, ph=ph, i=i, Y=Y: e.transpose(out=ph[:, i, :], in_=Y[:, i, :], identity=identb[:]),
                             reads=[(Y.name, i), identb], writes=[ph])
                    P.act(lambda e, ph=ph, Pm=Pm: e.activation(out=Pm[:], in_=ph[:, 0:4, :], func=AF.Copy),
                          writes=[ph] + [(Pm.name, i) for i in range(4)])
                    k4_ = lambda t_: [(t_.name, i) for i in range(4)]
                    LMt = G["LM"]
                    Tc = TTc = None
                    for l in range(7):
                        YM, PM = ymr.next(), pmr.next()
                        P.dve(lambda e, YM=YM, Y=Y, l=l: e.tensor_tensor(out=YM[:], in0=Y[:], in1=LMt[:, l, :, :], op=ALU.mult),
                              reads=k4_(Y) + [LMt], writes=k4_(YM))
                        P.dve(lambda e, PM=PM, Pm=Pm, l=l: e.tensor_tensor(out=PM[:], in0=Pm[:], in1=LMt[:, 7 + l, :, :], op=ALU.mult),
                              reads=k4_(Pm) + [LMt], writes=k4_(PM))
                        Tn, TTn = tcr.next(), ttr.next()
                        if l == 0:
                            P.dve(lambda e, TTn=TTn, PM=PM: e.tensor_tensor(out=TTn[:], in0=PM[:], in1=LMt[:, 14, :, :], op=ALU.add),
                                  reads=k4_(PM) + [LMt], writes=k4_(TTn))
                            P.dve(lambda e, Tn=Tn, YM=YM: e.tensor_tensor(out=Tn[:], in0=YM[:], in1=LMt[:, 14, :, :], op=ALU.add),
                                  reads=k4_(YM) + [LMt], writes=k4_(Tn))
                        else:
                            W1 = w1r.next()
                            pw = psf.next()
                            for i in range(4):
                                P.pe(lambda e, pw=pw, i=i, YM=YM, TTc=TTc: e.matmul(pw[:, i, :], lhsT=YM[:, i, :], rhs=TTc[:, i, :], start=True, stop=True),
                                     reads=[(YM.name, i), (TTc.name, i)], writes=[pw])
                            P.act(lambda e, pw=pw, W1=W1: e.activation(out=W1[:], in_=pw[:], func=AF.Copy), writes=[pw] + k4_(W1))
                            pt2 = psf.next()
                            for i in range(4):
                                P.pe(lambda e, pt2=pt2, i=i, Tc=Tc, W1=W1: e.matmul(pt2[:, i, :], lhsT=Tc[:, i, :], rhs=W1[:, i, :], start=True, stop=True),
                                     reads=[(Tc.name, i), (W1.name, i)], writes=[pt2])
                            P.dve(lambda e, pt2=pt2, TTn=TTn, TTc=TTc: e.tensor_tensor(out=TTn[:], in0=pt2[:], in1=TTc[:], op=ALU.add),
                                  reads=k4_(TTc), writes=[pt2] + k4_(TTn))
                            if l < 6:
                                W2 = w2r.next()
                                pw2 = psf.next()
                                for i in range(4):
                                    P.pe(lambda e, pw2=pw2, i=i, PM=PM, Tc=Tc: e.matmul(pw2[:, i, :], lhsT=PM[:, i, :], rhs=Tc[:, i, :], start=True, stop=True),
                                         reads=[(PM.name, i), (Tc.name, i)], writes=[pw2])
                                P.act(lambda e, pw2=pw2, W2=W2: e.activation(out=W2[:], in_=pw2[:], func=AF.Copy), writes=[pw2] + k4_(W2))
                                pt3 = psf.next()
                                for i in range(4):
                                    P.pe(lambda e, pt3=pt3, i=i, TTc=TTc, W2=W2: e.matmul(pt3[:, i, :], lhsT=TTc[:, i, :], rhs=W2[:, i, :], start=True, stop=True),
                                         reads=[(TTc.name, i), (W2.name, i)], writes=[pt3])
                                P.dve(lambda e, pt3=pt3, Tn=Tn, Tc=Tc: e.tensor_tensor(out=Tn[:], in0=pt3[:], in1=Tc[:], op=ALU.add),
                                      reads=k4_(Tc), writes=[pt3] + k4_(Tn))
                        Tc, TTc = Tn, TTn
                    R = TTc
                    TT = R
                    pks = psf.next()
                    for i, h in enumerate(hs):
                        P.pe(lambda e, pks=pks, i=i, h=h: e.matmul(pks[:, i, :], lhsT=kT(h // 2), rhs=Sbf[:, h, :], start=True, stop=True),
                             reads=ibk + [(Sbf.name, h)], writes=[pks])
                    pv = psh.next()
                    for i, h in enumerate(hs):
                        P.pe(lambda e, pv=pv, i=i, h=h: e.transpose(out=pv[:, i, :], in_=vT(h), identity=identb[:]), reads=ibk + [identb], writes=[pv])
                    for a, hq in enumerate(hqs):
                        P.pe(lambda e, pv=pv, a=a, hq=hq: e.transpose(out=pv[:, 4 + a, :], in_=kT(hq), identity=identb[:]), reads=ibk + [identb], writes=[pv])
                    t1_, rr_, vn_, qd_, kd_ = t1.next(), rr.next(), vnew.next(), qd.next(), kdec.next()
                    for i, h in enumerate(hs):
                        P.dve(lambda e, pks=pks, i=i, h=h, t1_=t1_: e.tensor_scalar(out=t1_[:, i, :], in0=pks[:, i, :], scalar1=tk["bege"][:, h:h + 1],
                                                                                 scalar2=None, op0=ALU.mult),
                              reads=[tk["bege"]], writes=[pks, (t1_.name, i)])
                        P.dve(lambda e, pv=pv, i=i, h=h, t1_=t1_, rr_=rr_: e.scalar_tensor_tensor(
                            out=rr_[:, i, :], in0=pv[:, i, :], scalar=BG[:, n, h:h + 1], in1=t1_[:, i, :], op0=ALU.mult, op1=ALU.subtract),
                            reads=bgk + [(t1_.name, i)], writes=[pv, (rr_.name, i)])
                        P.act(lambda e, pv=pv, i=i, h=h, kd_=kd_: e.activation(out=kd_[:, i, :], in_=pv[:, 4 + i // 2, :], func=AF.Copy,
                                                                             scale=tk["edl"][:, h:h + 1]),
                              reads=[tk["edl"]], writes=[pv, (kd_.name, i)])
                    pvn = psf.next()
                    for i in range(4):
                        P.pe(lambda e, pvn=pvn, i=i, TT=TT, rr_=rr_: e.matmul(pvn[:, i, :], lhsT=TT[:, i, :], rhs=rr_[:, i, :], start=True, stop=True),
                             reads=[(TT.name, i), (rr_.name, i)], writes=[pvn])
                    P.act(lambda e, pvn=pvn, vn_=vn_: e.activation(out=vn_[:], in_=pvn[:], func=AF.Copy),
                          writes=[pvn] + [(vn_.name, i) for i in range(4)])
                    for i, h in enumerate(hs):
                        P.dve(lambda e, i=i, h=h, qd_=qd_, ER_=ER_: e.tensor_tensor(out=qd_[:, i, :], in0=qT(h // 2), in1=ER_[:, i, :], op=ALU.mult),
                              reads=ibk + [(ER_.name, i)], writes=[(qd_.name, i)])
                    po = psf.next()
                    for i, h in enumerate(hs):
                        P.pe(lambda e, po=po, i=i, h=h, qd_=qd_: e.matmul(po[:, i, :], lhsT=Sbf[:, h, :], rhs=qd_[:, i, :], start=True, stop=False),
                             reads=[(Sbf.name, h), (qd_.name, i)], writes=[po])
                        P.pe(lambda e, po=po, i=i, vn_=vn_, MT_=MT_: e.matmul(po[:, i, :], lhsT=vn_[:, i, :], rhs=MT_[:, i, :], start=False, stop=True),
                             reads=[(vn_.name, i), (MT_.name, i)], writes=[po])
                    pds = psf.next()
                    for i in range(4):
                        P.pe(lambda e, pds=pds, i=i, kd_=kd_, vn_=vn_: e.matmul(pds[:, i, :], lhsT=kd_[:, i, :], rhs=vn_[:, i, :], start=True, stop=True),
                             reads=[(kd_.name, i), (vn_.name, i)], writes=[pds])
                    for i, h in enumerate(hs):
                        P.dve(lambda e, pds=pds, i=i, h=h, ER_=ER_: e.scalar_tensor_tensor(
                            out=S[:, h, :], in0=S[:, h, :], scalar=ER_[:, i, 127:128], in1=pds[:, i, :], op0=ALU.mult, op1=ALU.add),
                            reads=[(ER_.name, i), (S.name, h)], writes=[pds, (S.name, h)])
                        P.act(lambda e, h=h: e.activation(out=Sbf[:, h, :], in_=S[:, h, :], func=AF.Copy), reads=[(S.name, h)], writes=[(Sbf.name, h)])
                    ob_, oq_, sd_, rs_, on_ = osb.next(), osq.next(), sdn.next(), rsn.next(), onn.next()
                    k4 = lambda t_: [(t_.name, i) for i in range(4)]
                    P.act(lambda e, po=po, ob_=ob_: e.activation(out=ob_[:], in_=po[:], func=AF.Copy), writes=[po] + k4(ob_))
                    P.act(lambda e, ob_=ob_, oq_=oq_: e.activation(out=oq_[:], in_=ob_[:], func=AF.Square), reads=k4(ob_), writes=k4(oq_))
                    pss = psf.next()
                    for i in range(4):
                        P.pe(lambda e, pss=pss, i=i, oq_=oq_: e.matmul(pss[:, i, :], lhsT=onesb[:], rhs=oq_[:, i, :], start=True, stop=True),
                             reads=[onesb, (oq_.name, i)], writes=[pss])
                    P.act(lambda e, pss=pss, sd_=sd_: e.activation(out=sd_[:], in_=pss[:], func=AF.Sqrt, bias=eps[:, 0:1], scale=1.0 / 128),
                          reads=[eps], writes=[pss] + k4(sd_))
                    P.dve(lambda e, sd_=sd_, rs_=rs_: e.reciprocal(out=rs_[:], in_=sd_[:]), reads=k4(sd_), writes=k4(rs_))
                    P.dve(lambda e, ob_=ob_, rs_=rs_, on_=on_: e.tensor_tensor(out=on_[:], in0=ob_[:], in1=rs_[:], op=ALU.mult),
                          reads=k4(ob_) + k4(rs_), writes=k4(on_))
                    for i, h in enumerate(hs):
                        P.dve(lambda e, i=i, h=h, on_=on_: e.scalar_tensor_tensor(
                            out=ot[:, h, cs], in0=on_[:, i, :], scalar=NV[:, 208 + li:209 + li], in1=zT(h), op0=ALU.mult, op1=ALU.mult),
                            reads=[(on_.name, i), NV] + ibk, writes=[(ot.name, h, cl)])
                for g4 in range(4):
                    group(g4)

            for cl in range(4):
                chunk(cl)
            dst = OT_lo if sc < 4 else OT_hi
            dv = dst.rearrange("(h e) t -> e h t", e=128)
            P.dma("sp", dv[:, :, (sc % 4) * 512:(sc % 4 + 1) * 512], ot[:],
                  reads=[(ot.name, h, cl) for h in range(HL) for cl in range(4)])
            P.emit()


def gdn_outproj(kb, C, G, H, G_lo, G_hi, w_out):
    import contextlib
    SELC = G["SELC"]
    wv = w_out.rearrange("(c p) n -> p c n", p=128)
    for tb in range(4):
        with contextlib.ExitStack() as st:
            HT = [kb.sb(st, "ht", [128, D], F32) for _ in range(4)]
            oa = kb.sb(st, "oa", [128, 32, 512], BF16)
            ob = kb.sb(st, "ob", [128, 32, 512], BF16)
            wo = [kb.sb(st, "wo", [128, 32, 512], BF16) for _ in range(2)]
            psy = Ring([kb.ps(st, "psy", [128, 512]) for _ in range(4)])
            P = kb.prog()
            for t in range(4):
                r0 = (tb * 4 + t) * 128
                P.dma("sp", HT[t][:], H[r0:r0 + 128, :], writes=hk(HT[t]))
            for r in range(2):
                for j in range(8):
                    c0 = r * 16 + 2 * j
                    P.dma("sp", oa[:, c0:c0 + 2, :], G_lo[j, r * 256:(r + 1) * 256, :].rearrange("(k p) t -> p k t", p=128)[:, :, tb * 512:(tb + 1) * 512],
                          writes=[(oa.name, r, j)])
                    P.dma("sp", ob[:, c0:c0 + 2, :], G_hi[j, r * 256:(r + 1) * 256, :].rearrange("(k p) t -> p k t", p=128)[:, :, tb * 512:(tb + 1) * 512],
                          writes=[(ob.name, r, j)])
            for c2 in range(2):
                sl_ = slice(c2 * 16, (c2 + 1) * 16)
                P.dve(lambda e, sl_=sl_: e.tensor_scalar(out=oa[:, sl_, :], in0=oa[:, sl_, :], scalar1=SELC[:, 0:1], scalar2=None, op0=ALU.mult),
                      reads=[SELC] + [(oa.name, c2, j) for j in range(8)], writes=[(oa.name, c2)])
                P.dve(lambda e, sl_=sl_: e.scalar_tensor_tensor(out=oa[:, sl_, :], in0=ob[:, sl_, :], scalar=SELC[:, 1:2], in1=oa[:, sl_, :],
                                                              op0=ALU.mult, op1=ALU.add),
                      reads=[SELC] + [(ob.name, c2, j) for j in range(8)], writes=[(oa.name, c2)])
            P.dma("pool", wo[0][:], wv[:, :, 0:512], writes=[wo[0]])
            for db in range(4):
                b = db % 2
                if db + 1 < 4:
                    P.dma("pool", wo[1 - b][:], wv[:, :, (db + 1) * 512:(db + 2) * 512], writes=[wo[1 - b]])
                for t in range(4):
                    py = psy.next()
                    for c in range(32):
                        P.pe(lambda e, py=py, c=c, t=t, b=b: e.matmul(py[:], lhsT=oa[:, c, t * 128:(t + 1) * 128], rhs=wo[b][:, c, :],
                                                                    start=(c == 0), stop=(c == 31)),
                             reads=[wo[b], (oa.name, c // 16)], writes=[py])
                    hs_ = HT[t][:, db * 512:(db + 1) * 512]
                    P.dve(lambda e, hs_=hs_, py=py: e.tensor_tensor(out=hs_, in0=hs_, in1=py[:], op=ALU.add),
                          reads=[(HT[t].name, db)], writes=[py, (HT[t].name, db)])
            for t in range(4):
                r0 = (tb * 4 + t) * 128
                P.dma("sp", H[r0:r0 + 128, :], HT[t][:], reads=hk(HT[t]))
            P.emit()


def gdn_layer(kb, C, G, li, H, S_, ins):
    gdn_norm_gather(kb, C, H, li * 16, S_["XNT_own"], S_["XNT_all"])
    if DEBUG_CUT == 11:
        return
    gdn_inproj(kb, C, G, li, S_["XNT_all"], ins["w_in"][li], S_["QKVZ"])
    if DEBUG_CUT == 12:
        import contextlib
        ov = S_["out"].rearrange("(a b) c -> a (b c)", b=2)
        with contextlib.ExitStack() as st:
            tb_ = kb.sb(st, "dbb", [128, T], BF16)
            tf_ = kb.sb(st, "dbf", [128, T], F32)
            P = kb.prog()
            for i, ct in enumerate((0, 8, 16, 32, 7, 15, 31, 47)):
                P.dma("sp", tb_[:], S_["QKVZ"][ct], writes=[tb_])
                P.dve(lambda e: e.tensor_copy(out=tf_[:], in_=tb_[:]), reads=[tb_], writes=[tf_])
                P.dma("sp", ov[i * 128:(i + 1) * 128, :], tf_[:], reads=[tf_])
            P.dma("sp", ov[1024 - 128:1024, 0:1024], G["BG"][:].rearrange("p a b -> p (a b)"), reads=[])
            P.emit()
        return
    gdn_scan(kb, C, G, li, S_["QKVZ"], S_["OT_lo"], S_["OT_hi"])
    if DEBUG_CUT == 13:
        import contextlib
        with contextlib.ExitStack() as st:
            tb_ = kb.sb(st, "dbb", [128, TO], BF16)
            tf_ = kb.sb(st, "dbf", [128, TO], F32)
            P = kb.prog()
            for i in range(16):
                P.dma("sp", tb_[:], S_["OT_lo"][i * 128:(i + 1) * 128, :], writes=[tb_])
                P.dve(lambda e: e.tensor_copy(out=tf_[:], in_=tb_[:]), reads=[tb_], writes=[tf_])
                P.dma("sp", S_["out"][i * 128:(i + 1) * 128, :], tf_[:], reads=[tf_])
            P.emit()
        return
    allgather(kb, S_["OT_lo"], S_["G_lo"])
    allgather(kb, S_["OT_hi"], S_["G_hi"])
    gdn_outproj(kb, C, G, H, S_["G_lo"], S_["G_hi"], ins["w_out"][li])


def kv_pass(kb, C, H, w_kv, nvcol, KT_own, V_own):
    import contextlib
    wv = w_view(w_kv)
    for tb in range(2):
        with contextlib.ExitStack() as st:
            HT = [kb.sb(st, "ht", [128, D], F32) for _ in range(8)]
            xT = kb.sb(st, "xT", [128, KD, 1024], BF16)
            with contextlib.ExitStack() as st1:
                P = kb.prog()
                for t in range(8):
                    r0 = (tb * 8 + t) * 128
                    P.dma("sp", HT[t][:], H[r0:r0 + 128, :], writes=hk(HT[t]))
                norm_T(kb, P, st1, C, HT, nvcol, xT, tr_ring(kb, st1))
                P.emit()
            with contextlib.ExitStack() as st2:
                P = kb.prog()
                wb = [kb.sb(st2, "wkv", [128, KD, 512], BF16) for _ in range(2)]
                kst = Ring([kb.sb(st2, "kst", [128, 1024], BF16) for _ in range(2)])
                vst = Ring([kb.sb(st2, "vst", [128, 512], BF16) for _ in range(3)])
                psa = Ring([kb.ps(st2, "psa", [128, 512]) for _ in range(4)])
                P.dma("pool", wb[0][:], wv[:, :, 0:512], writes=[wb[0]])
                for blk in range(8):
                    b = blk % 2
                    if blk + 1 < 8:
                        P.dma("pool", wb[1 - b][:], wv[:, :, (blk + 1) * 512:(blk + 2) * 512], writes=[wb[1 - b]])
                    if blk < 4:
                        for ci in range(4):
                            hc = blk * 4 + ci
                            ks = kst.next()
                            for tg in range(2):
                                pa = psa.next()
                                for k in range(KD):
                                    P.pe(lambda e, pa=pa, k=k, ci=ci, tg=tg, b=b: e.matmul(pa[:], lhsT=wb[b][:, k, ci * 128:(ci + 1) * 128],
                                                                                         rhs=xT[:, k, tg * 512:(tg + 1) * 512],
                                                                                         start=(k == 0), stop=(k == KD - 1)),
                                         reads=[wb[b], (xT.name, k, tg)], writes=[pa])
                                P.act(lambda e, pa=pa, ks=ks, tg=tg: e.activation(out=ks[:, tg * 512:(tg + 1) * 512], in_=pa[:], func=AF.Copy),
                                      writes=[pa, (ks.name, tg)])
                            P.dma("sp", KT_own[hc * 128:(hc + 1) * 128, tb * 1024:(tb + 1) * 1024], ks[:], reads=[(ks.name, 0), (ks.name, 1)])
                    else:
                        db = blk - 4
                        for t in range(8):
                            pa = psa.next()
                            vs = vst.next()
                            for k in range(KD):
                                P.pe(lambda e, pa=pa, k=k, t=t, b=b: e.matmul(pa[:], lhsT=xT[:, k, t * 128:(t + 1) * 128], rhs=wb[b][:, k, :],
                                                                            start=(k == 0), stop=(k == KD - 1)),
                                     reads=[wb[b], (xT.name, k, t // 4)], writes=[pa])
                            P.act(lambda e, pa=pa, vs=vs: e.activation(out=vs[:], in_=pa[:], func=AF.Copy), writes=[pa, vs])
                            r0 = (tb * 8 + t) * 128
                            P.dma("sp", V_own[r0:r0 + 128, db * 512:(db + 1) * 512], vs[:], reads=[vs])
                P.emit()


def attn_layer(kb, C, A, li, H, S_, w_q, w_o, nvcol):
    import contextlib, math
    j = li - N_A
    lam_init = 0.8 - 0.6 * math.exp(-0.3 * li)
    QT, KT_all, V_all = S_["QT"], S_["KT_all"], S_["V_all"]
    identb, eps = C["identb"], C["eps"]
    P = kb.prog()
    LAMV, nlam, SW, SUBW = A["LAMV"], A["nlam"], A["SW"], A["SUBW"]
    l2t, lsum = A["l2t"], A["lsum"]
    P.dve(lambda e: e.memset(lsum[:], 0.0), writes=[lsum])
    for a in range(2):
        P.dve(lambda e, a=a: e.tensor_tensor(out=l2t[:], in0=LAMV[:, j, 2 * a, :], in1=LAMV[:, j, 2 * a + 1, :], op=ALU.mult),
              reads=[LAMV], writes=[l2t])
        P.act(lambda e, a=a: e.activation(out=l2t[:], in_=l2t[:], func=AF.Copy, accum_out=lsum[:, a:a + 1]), reads=[l2t, lsum], writes=[l2t, (lsum.name, a)])
    P.act(lambda e: e.activation(out=lsum[:], in_=lsum[:], func=AF.Exp), reads=[lsum, (lsum.name, 0), (lsum.name, 1)], writes=[lsum])
    P.dve(lambda e: e.tensor_tensor(out=nlam[:], in0=lsum[:, 1:2], in1=lsum[:, 0:1], op=ALU.subtract), reads=[lsum], writes=[nlam])
    P.dve(lambda e: e.tensor_scalar(out=nlam[:], in0=nlam[:], scalar1=-lam_init, scalar2=None, op0=ALU.add), reads=[nlam], writes=[nlam])
    P.dve(lambda e: e.tensor_scalar(out=SW[:], in0=SUBW[:, j, :], scalar1=1.0 - lam_init, scalar2=None, op0=ALU.mult), reads=[SUBW], writes=[SW])
    P.emit()
    wqv = w_view(w_q)
    for tb in range(2):
        with contextlib.ExitStack() as st:
            HT = [kb.sb(st, "ht", [128, D], F32) for _ in range(8)]
            xT = kb.sb(st, "xT", [128, KD, 1024], BF16)
            with contextlib.ExitStack() as st1:
                P = kb.prog()
                for t in range(8):
                    r0 = (tb * 8 + t) * 128
                    P.dma("sp", HT[t][:], H[r0:r0 + 128, :], writes=hk(HT[t]))
                norm_T(kb, P, st1, C, HT, nvcol, xT, tr_ring(kb, st1))
                P.emit()
            with contextlib.ExitStack() as st2:
                P = kb.prog()
                wb = [kb.sb(st2, "wq", [128, KD, 512], BF16) for _ in range(2)]
                qst = Ring([kb.sb(st2, "qst", [128, 1024], BF16) for _ in range(2)])
                psa = Ring([kb.ps(st2, "psa", [128, 512]) for _ in range(4)])
                P.dma("pool", wb[0][:], wqv[:, :, 0:512], writes=[wb[0]])
                for blk in range(4):
                    b = blk % 2
                    if blk + 1 < 4:
                        P.dma("pool", wb[1 - b][:], wqv[:, :, (blk + 1) * 512:(blk + 2) * 512], writes=[wb[1 - b]])
                    for ci in range(4):
                        hm = blk * 4 + ci
                        qs = qst.next()
                        for tg in range(2):
                            pa = psa.next()
                            for k in range(KD):
                                P.pe(lambda e, pa=pa, k=k, ci=ci, tg=tg, b=b: e.matmul(pa[:], lhsT=wb[b][:, k, ci * 128:(ci + 1) * 128],
                                                                                     rhs=xT[:, k, tg * 512:(tg + 1) * 512],
                                                                                     start=(k == 0), stop=(k == KD - 1)),
                                     reads=[wb[b], (xT.name, k, tg)], writes=[pa])
                            P.act(lambda e, pa=pa, qs=qs, tg=tg: e.activation(out=qs[:, tg * 512:(tg + 1) * 512], in_=pa[:], func=AF.Copy,
                                                                            scale=128.0 ** -0.5),
                                  writes=[pa, (qs.name, tg)])
                        P.dma("sp", QT[hm][:, tb * 1024:(tb + 1) * 1024], qs[:], reads=[(qs.name, 0), (qs.name, 1)])
                P.emit()
    with contextlib.ExitStack() as sto:
        OATT = kb.sb(sto, "OATT", [128, NT, D], BF16)
        with contextlib.ExitStack() as st:
            KTh = [kb.sb(st, "KTh", [128, 2, T], BF16) for _ in range(2)]
            Vh = [kb.sb(st, "Vh", [128, 32, 257], BF16) for _ in range(2)]
            QTh = [kb.sb(st, "QTh", [128, 2, TO], BF16) for _ in range(2)]
            pT = Ring([kb.sb(st, "pT", [128, 128], BF16) for _ in range(6)])
            o1 = Ring([kb.sb(st, "o1", [128, 256], F32) for _ in range(2)])
            oo = Ring([kb.sb(st, "oo", [128, 256], F32) for _ in range(2)])
            jk = kb.sb(st, "jk", [128, 256], BF16)
            sm = Ring([kb.sb(st, "sm", [128, 8], F32) for _ in range(4)])
            sring = Ring([kb.ps(st, "pss", [128, 512]) for _ in range(3)])
            av = [[kb.ps(st, "av", [128, 512]) for _ in range(2)] for _ in range(2)]
            BT, MA, MB = A["BT"], A["MA"], A["MB"]
            P = kb.prog()
            for b in range(2):
                P.dve(lambda e, b=b: e.memset(Vh[b][:, :, 256:257], 1.0), writes=[(Vh[b].name, "one")])
            P.emit()
            for h in range(8):
                b = h % 2
                P = kb.prog()
                for m in range(2):
                    for r in range(2):
                        P.dma("sp", KTh[b][:, m, r * TO:(r + 1) * TO], KT_all[h, r * 256 + m * 128:r * 256 + (m + 1) * 128, :], writes=[(KTh[b].name, m, r)])
                    P.dma("sp", QTh[b][:, m, :], QT[2 * h + m], writes=[(QTh[b].name, m)])
                for r in range(2):
                    for jj in range(8):
                        kt0 = r * 16 + 2 * jj
                        P.dma("sp", Vh[b][:, kt0:kt0 + 2, 0:256],
                              V_all[jj, r * 256:(r + 1) * 256, h * 256:(h + 1) * 256].rearrange("(a p) e -> p a e", p=128),
                              writes=[(Vh[b].name, kt0 // 2)])

                def qblock(qb, h=h, b=b, P=P):
                    i0 = 2 * qb
                    nkt = 18 + 2 * qb
                    for kt in range(nkt):
                        for m in range(2):
                            ps = sring.next()
                            P.pe(lambda e, ps=ps, m=m, kt=kt: e.matmul(ps[:, 0:256], lhsT=KTh[b][:, m, kt * 128:(kt + 1) * 128],
                                                                     rhs=QTh[b][:, m, i0 * 128:(i0 + 2) * 128], start=True, stop=True),
                                 reads=[(KTh[b].name, m, kt // 16), (QTh[b].name, m)], writes=[ps])
                            for jj in range(2):
                                i = i0 + jj
                                if kt > 16 + i:
                                    continue
                                p_ = pT.next()
                                dd = i - kt + 16
                                P.act(lambda e, ps=ps, p_=p_, jj=jj, dd=dd: e.activation(out=p_[:], in_=ps[:, jj * 128:(jj + 1) * 128], func=AF.Exp,
                                                                                       bias=BT[:, h, dd:dd + 1], scale=1.0),
                                      reads=[BT], writes=[ps, p_])
                                if kt == i:
                                    P.dve(lambda e, p_=p_: e.tensor_tensor(out=p_[:], in0=p_[:], in1=MA[:], op=ALU.mult), reads=[MA], writes=[p_])
                                if kt == 16 + i:
                                    P.dve(lambda e, p_=p_: e.tensor_tensor(out=p_[:], in0=p_[:], in1=MB[:], op=ALU.mult), reads=[MB], writes=[p_])
                                acc = av[jj][m]
                                P.pe(lambda e, acc=acc, p_=p_, kt=kt, i=i: e.matmul(acc[:, 0:257], lhsT=p_[:], rhs=Vh[b][:, kt, :],
                                                                                  start=(kt == 0), stop=(kt == 16 + i)),
                                     reads=[p_, (Vh[b].name, kt // 2), (Vh[b].name, "one")], writes=[acc])
                    for jj in range(2):
                        i = i0 + jj
                        s_ = sm.next()
                        a0, a1 = av[jj][0], av[jj][1]
                        o1_, oo_ = o1.next(), oo.next()
                        P.dve(lambda e, s_=s_, a0=a0: e.reciprocal(out=s_[:, 0:1], in_=a0[:, 256:257]), writes=[a0, (s_.name, 0)])
                        P.dve(lambda e, s_=s_, a1=a1: e.reciprocal(out=s_[:, 1:2], in_=a1[:, 256:257]), writes=[a1, (s_.name, 1)])
                        P.dve(lambda e, s_=s_: e.tensor_tensor(out=s_[:, 2:3], in0=s_[:, 1:2], in1=nlam[:], op=ALU.mult),
                              reads=[(s_.name, 1), nlam], writes=[(s_.name, 2)])
                        P.dve(lambda e, s_=s_, a0=a0, o1_=o1_: e.tensor_scalar(out=o1_[:], in0=a0[:, 0:256], scalar1=s_[:, 0:1], scalar2=None, op0=ALU.mult),
                              reads=[(s_.name, 0)], writes=[a0, o1_])
                        P.dve(lambda e, s_=s_, a1=a1, o1_=o1_, oo_=oo_: e.scalar_tensor_tensor(out=oo_[:], in0=a1[:, 0:256], scalar=s_[:, 2:3], in1=o1_[:],
                                                                                             op0=ALU.mult, op1=ALU.add),
                              reads=[(s_.name, 2), o1_], writes=[a1, oo_])
                        P.dve(lambda e, s_=s_: e.memset(s_[:, 3:4], 0.0), writes=[(s_.name, 3)])
                        P.act(lambda e, s_=s_, oo_=oo_: e.activation(out=jk[:], in_=oo_[:], func=AF.Square, accum_out=s_[:, 3:4]),
                              reads=[oo_, (s_.name, 3)], writes=[jk, (s_.name, 4)])
                        P.act(lambda e, s_=s_: e.activation(out=s_[:, 5:6], in_=s_[:, 3:4], func=AF.Sqrt, bias=eps[:, 0:1], scale=1.0 / 256),
                              reads=[(s_.name, 4), eps], writes=[(s_.name, 5)])
                        P.dve(lambda e, s_=s_: e.reciprocal(out=s_[:, 6:7], in_=s_[:, 5:6]), reads=[(s_.name, 5)], writes=[(s_.name, 6)])
                        P.dve(lambda e, s_=s_, oo_=oo_, i=i: e.scalar_tensor_tensor(out=OATT[:, i, h * 256:(h + 1) * 256], in0=oo_[:], scalar=s_[:, 6:7],
                                                                                  in1=SW[:], op0=ALU.mult, op1=ALU.mult),
                              reads=[oo_, (s_.name, 6), SW], writes=[(OATT.name, i, h)])

                for qb in range(8):
                    qblock(qb)
                P.emit()
        wov = w_view(w_o)
        for tb in range(4):
            with contextlib.ExitStack() as st:
                HT = [kb.sb(st, "ht", [128, D], F32) for _ in range(4)]
                oT = kb.sb(st, "oT", [128, KD, 512], BF16)
                wo = [kb.sb(st, "wo", [128, KD, 512], BF16) for _ in range(2)]
                trr = Ring([kb.ps(st, "ptr3", [128, 8, 128], BF16) for _ in range(2)])
                psy = Ring([kb.ps(st, "psy", [128, 512]) for _ in range(4)])
                P = kb.prog()
                for t in range(4):
                    r0 = (tb * 4 + t) * 128
                    P.dma("sp", HT[t][:], H[r0:r0 + 128, :], writes=hk(HT[t]))
                for t in range(4):
                    i = tb * 4 + t
                    for c2 in range(2):
                        pt = trr.next()
                        for cc in range(8):
                            c = c2 * 8 + cc
                            P.pe(lambda e, pt=pt, cc=cc, c=c, i=i: e.transpose(out=pt[:, cc, :], in_=OATT[:, i, c * 128:(c + 1) * 128],
                                                                             identity=identb[:]),
                                 reads=[(OATT.name, i, c // 2), identb], writes=[pt])
                        P.act(lambda e, pt=pt, c2=c2, t=t: e.activation(out=oT[:, c2 * 8:(c2 + 1) * 8, t * 128:(t + 1) * 128],
                                                                      in_=pt[:], func=AF.Copy),
                              writes=[pt, (oT.name, c2, t)])
                P.dma("pool", wo[0][:], wov[:, :, 0:512], writes=[wo[0]])
                for db in range(4):
                    b = db % 2
                    if db + 1 < 4:
                        P.dma("pool", wo[1 - b][:], wov[:, :, (db + 1) * 512:(db + 2) * 512], writes=[wo[1 - b]])
                    for t in range(4):
                        py = psy.next()
                        for c in range(KD):
                            P.pe(lambda e, py=py, c=c, t=t, b=b: e.matmul(py[:], lhsT=oT[:, c, t * 128:(t + 1) * 128], rhs=wo[b][:, c, :],
                                                                        start=(c == 0), stop=(c == KD - 1)),
                                 reads=[wo[b], (oT.name, c // 8, t)], writes=[py])
                        hs_ = HT[t][:, db * 512:(db + 1) * 512]
                        P.dve(lambda e, hs_=hs_, py=py: e.tensor_tensor(out=hs_, in0=hs_, in1=py[:], op=ALU.add),
                              reads=[(HT[t].name, db)], writes=[py, (HT[t].name, db)])
                for t in range(4):
                    r0 = (tb * 4 + t) * 128
                    P.dma("sp", H[r0:r0 + 128, :], HT[t][:], reads=hk(HT[t]))
                P.emit()
```

```python
import numpy as np
import concourse.bass as bass
import concourse.mybir as mybir
from concourse.bass_utils import run_bass_kernel_spmd

F32 = mybir.dt.float32
BF16 = mybir.dt.bfloat16
AF = mybir.ActivationFunctionType
ALU = mybir.AluOpType
AX = mybir.AxisListType

ENGS = ("pe", "act", "dve", "pool", "sp")
DMA_RING = {"sp": 12, "pool": 12, "act": 6, "cc": 4}


class Op:
    __slots__ = ("eng", "fn", "reads", "writes", "dma", "waits", "signal", "sem", "count", "idx", "inc")

    def __init__(self, eng, fn, reads, writes, dma, inc=16):
        self.eng, self.fn, self.reads, self.writes, self.dma = eng, fn, reads, writes, dma
        self.inc = inc if dma else 1
        self.waits = []
        self.signal = False
        self.sem = None
        self.count = 0


def _key(r):
    return r if isinstance(r, (str, tuple, int)) else r.name


class Prog:
    def __init__(self, kb, same_engine_sync=True):
        self.kb = kb
        self.nc = kb.nc
        self.ops = []
        self.same_engine_sync = same_engine_sync

    def add(self, eng, fn, reads=(), writes=(), dma=False, inc=16):
        op = Op(eng, fn, tuple(_key(r) for r in reads), tuple(_key(w) for w in writes), dma, inc)
        op.idx = len(self.ops)
        self.ops.append(op)
        return op

    def pe(self, fn, reads=(), writes=()):
        return self.add("pe", fn, reads, writes)

    def act(self, fn, reads=(), writes=()):
        return self.add("act", fn, reads, writes)

    def dve(self, fn, reads=(), writes=()):
        return self.add("dve", fn, reads, writes)

    def pool(self, fn, reads=(), writes=()):
        return self.add("pool", fn, reads, writes)

    def dma(self, q, out, in_, reads=(), writes=(), **kw):
        return self.add(q, lambda e: e.dma_start(out=out, in_=in_, **kw), reads, writes, dma=True)

    def _schedule(self):
        for k_, op_ in enumerate(self.ops):
            op_.idx = k_
        last_write = {}
        readers = {}
        deps_of = []
        for op in self.ops:
            deps = set()
            for r in op.reads:
                lw = last_write.get(r)
                if lw is not None:
                    deps.add(lw)
            for w in op.writes:
                lw = last_write.get(w)
                if lw is not None:
                    deps.add(lw)
                for rd in readers.get(w, ()):
                    deps.add(rd)
            for r in op.reads:
                readers.setdefault(r, []).append(op.idx)
            for w in op.writes:
                last_write[w] = op.idx
                readers[w] = []
            deps.discard(op.idx)
            deps_of.append(deps)

        for op in self.ops:
            for j in deps_of[op.idx]:
                d = self.ops[j]
                if d.dma:
                    continue
                if d.eng == op.eng and not op.dma:
                    if op.eng == "pe" or not self.same_engine_sync:
                        continue
                d.signal = True

        eng_count = {e: 0 for e in ENGS}
        dma_n = {q: 0 for q in DMA_RING}
        persist = getattr(self.kb, "persist", None)
        if persist is None:
            persist = self.kb.persist = {}
        dma_slot_count = dict(persist)
        dma_prev = {}
        for op in self.ops:
            if op.dma:
                q = op.eng if op.inc != 1 else "cc"
                slot = dma_n[q] % DMA_RING[q]
                dma_n[q] += 1
                key = ("dma", q, slot)
                prev = dma_prev.get(key)
                if prev is not None:
                    deps_of[op.idx].add(prev)
                dma_slot_count[key] = dma_slot_count.get(key, 0) + op.inc
                op.sem, op.count, op.signal = key, dma_slot_count[key], True
                dma_prev[key] = op.idx
            elif op.signal:
                eng_count[op.eng] += 1
                op.sem, op.count = ("eng", op.eng), eng_count[op.eng]
        assert max(list(eng_count.values()) + list(dma_slot_count.values()) + [0]) < 60000, eng_count

        known = {e: {} for e in ENGS}
        snap = [None] * len(self.ops)
        for op in self.ops:
            kn = known[op.eng]
            need = {}
            for j in deps_of[op.idx]:
                d = self.ops[j]
                if not d.signal:
                    continue
                if d.eng == op.eng and not d.dma and not op.dma:
                    if op.eng == "pe" or not self.same_engine_sync:
                        continue
                if kn.get(d.sem, 0) >= d.count:
                    continue
                if need.get(d.sem, (0, None))[0] < d.count:
                    need[d.sem] = (d.count, j)
            for sem, (cnt, j) in need.items():
                op.waits.append((sem, cnt))
                if kn.get(sem, 0) < cnt:
                    kn[sem] = cnt
                sj = snap[j]
                if sj:
                    for s2, c2 in sj.items():
                        if kn.get(s2, 0) < c2:
                            kn[s2] = c2
            snap[op.idx] = dict(kn) if op.signal else None
        self.final_dma = {k: v for k, v in dma_slot_count.items() if v != persist.get(k, 0) or k[1] not in ("pool", "cc")}
        for k, v in dma_slot_count.items():
            if k[1] in ("pool", "cc"):
                persist[k] = v
        return sorted({op.sem for op in self.ops if op.signal}, key=str)

    def emit(self):
        nc = self.nc
        if not self.ops:
            return
        used = self._schedule()
        sem = self.kb.sem
        for key in used:
            if key not in sem:
                sem[key] = nc.alloc_semaphore("s_" + "_".join(str(k) for k in key))
        with nc.Block() as cb:
            def clr(e):
                for key in used:
                    if not (key[0] == "dma" and key[1] in ("pool", "cc")):
                        e.sem_clear(sem[key])
            cb.gpsimd(clr)
        by_eng = {e: [] for e in ENGS}
        for op in self.ops:
            by_eng[op.eng].append(op)
        final_dma = self.final_dma
        with nc.Block() as block:
            def section(ename):
                def body(e):
                    for op in by_eng[ename]:
                        for s, c in op.waits:
                            e.wait_ge(sem[s], c)
                        ins = op.fn(e)
                        if op.signal:
                            ins.then_inc(sem[op.sem], op.inc)
                    if ename == "sp":
                        for key, cnt in final_dma.items():
                            e.wait_ge(sem[key], cnt)
                return body

            block.tensor(section("pe"))
            block.scalar(section("act"))
            block.vector(section("dve"))
            block.gpsimd(section("pool"))
            block.sync(section("sp"))
        self.ops = []


D = 2048
T = 4096
TO = 2048
NT = TO // 128
KD = D // 128
FF = 8192
DEPTH = 4
N_A = 2
EPS = 1e-6
HL = 16
HQ = 8
NCOLS_IN = 6176
NVC = 13 * 16 + 2
BIG = 30000.0
import os
DEBUG_CUT = int(os.environ.get('MK_DEBUG_CUT', '0'))
SUBCUT = int(os.environ.get('MK_SUBCUT', '9'))


class KB:
    def __init__(self):
        self.nc = bass.Bass("TRN2", target_bir_lowering=False)
        self.sem = {}
        self.uid = 0
        self.rr = {}

    def din(self, name, shape, dt=F32):
        return self.nc.dram_tensor(name, list(shape), dt, kind="ExternalInput").ap()

    def dscr(self, name, shape, dt):
        return self.nc.dram_tensor(name, list(shape), dt, kind="Internal").ap()

    def sb(self, st, name, shape, dt):
        self.uid += 1
        return st.enter_context(self.nc.sbuf_tensor(f"{name}_{self.uid}", list(shape), dt))

    def ps(self, st, name, shape, dt=F32):
        self.uid += 1
        return st.enter_context(self.nc.psum_tensor(f"{name}_{self.uid}", list(shape), dt))

    def prog(self):
        return Prog(self)


PAR = [0]


class PRing:
    def __init__(self, a, b):
        self.r = (a, b)

    def next(self):
        return self.r[PAR[0]].next()


class Ring:
    def __init__(self, bufs):
        self.bufs = bufs
        self.i = 0

    def next(self):
        b = self.bufs[self.i % len(self.bufs)]
        self.i += 1
        return b


def w_view(w2d):
    return w2d.rearrange("(k p) n -> p k n", p=128)


def hk(ht):
    return [(ht.name, d) for d in range(4)]


def norm_T(kb, P, st, C, hts, nvcol, xT, ps_tr):
    n = len(hts)
    ss = kb.sb(st, "ss", [128, n], F32)
    sd = kb.sb(st, "sd", [128, n], F32)
    rstd = kb.sb(st, "rstd", [128, n], F32)
    junk = kb.sb(st, "junk", [128, D], BF16)
    xs = [kb.sb(st, "xs", [128, D], BF16) for _ in range(4)]
    NV, identb = C["NV"], C["identb"]
    P.dve(lambda e: e.memset(ss[:], 0.0), writes=[ss])
    for t, h in enumerate(hts):
        P.act(lambda e, h=h, t=t: e.activation(out=junk[:], in_=h[:], func=AF.Square, accum_out=ss[:, t:t + 1]),
              reads=hk(h) + [ss], writes=[junk, (ss.name, t)])
    P.act(lambda e: e.activation(out=sd[:], in_=ss[:], func=AF.Sqrt, bias=C["eps"][:, 0:1], scale=1.0 / D),
          reads=[ss] + [(ss.name, t) for t in range(n)], writes=[sd])
    P.dve(lambda e: e.reciprocal(out=rstd[:], in_=sd[:]), reads=[sd], writes=[rstd])
    cnt = 0
    for g in range(n // 4):
        for j in range(4):
            t = g * 4 + j
            P.act(lambda e, t=t, j=j: e.activation(out=xs[j][:], in_=hts[t][:], func=AF.Copy, scale=rstd[:, t:t + 1]),
                  reads=hk(hts[t]) + [rstd], writes=[xs[j]])
        for k2 in range(KD // 2):
            pt = ps_tr.next()
            for kk in range(2):
                k = k2 * 2 + kk
                for j in range(4):
                    c0 = kk * 512 + j * 128
                    P.pe(lambda e, pt=pt, j=j, k=k, c0=c0: e.transpose(out=pt[:, c0:c0 + 128], in_=xs[j][:, k * 128:(k + 1) * 128],
                                                                     identity=identb[:]),
                         reads=[xs[j], identb], writes=[pt])
            for kk in range(2):
                k = k2 * 2 + kk
                dst = xT[:, k, g * 512:(g + 1) * 512]
                src = pt[:, kk * 512:(kk + 1) * 512]
                if cnt % 2 == 0:
                    P.dve(lambda e, dst=dst, src=src, k=k: e.tensor_scalar(out=dst, in0=src, scalar1=NV[:, nvcol + k:nvcol + k + 1],
                                                                         scalar2=None, op0=ALU.mult),
                          reads=[NV], writes=[pt, (xT.name, k, g)])
                else:
                    P.act(lambda e, dst=dst, src=src, k=k: e.activation(out=dst, in_=src, func=AF.Copy,
                                                                      scale=NV[:, nvcol + k:nvcol + k + 1]),
                          reads=[NV], writes=[pt, (xT.name, k, g)])
            cnt += 1


def tr_ring(kb, st, nbank=2):
    return Ring([kb.ps(st, "ptr", [128, 1024], BF16) for _ in range(nbank)])


def xT_keys(xT, g):
    return [(xT.name, k, g) for k in range(KD)]


def mlp_pass(kb, C, H, w1, w2, nvcol):
    import contextlib
    for tb in range(int(os.environ.get("MK_NTB", "2"))):
        with contextlib.ExitStack() as st:
            HT = [kb.sb(st, "ht", [128, D], F32) for _ in range(8)]
            hnT = kb.sb(st, "hnT", [128, KD, 1024], BF16)
            with contextlib.ExitStack() as st1:
                P = kb.prog()
                for t in range(8):
                    r0 = (tb * 8 + t) * 128
                    P.dma("sp", HT[t][:], H[r0:r0 + 128, :], writes=hk(HT[t]))
                norm_T(kb, P, st1, C, HT, nvcol, hnT, tr_ring(kb, st1))
                P.emit()
            if DEBUG_CUT == 1 or (tb == 1 and os.environ.get("MK_SKIPB2")):
                continue
            with contextlib.ExitStack() as st2:
                P = kb.prog()
                gT = [kb.sb(st2, "gT", [128, 4, 1024], BF16) for _ in range(2)]
                w1b = [kb.sb(st2, "w1b", [128, KD, 512], BF16) for _ in range(2)]
                w2b = [kb.sb(st2, "w2b", [128, 4, D], BF16) for _ in range(2)]
                rt = Ring([kb.sb(st2, "rt", [128, 512], F32) for _ in range(2)])
                psu = Ring([kb.ps(st2, "psu", [128, 512]) for _ in range(3)])
                psy = Ring([kb.ps(st2, "psy", [128, 512]) for _ in range(3)])
                w1v = w_view(w1)

                def loadw(fb):
                    b = fb % 2
                    P.dma("pool", w1b[b][:], w1v[:, :, fb * 512:(fb + 1) * 512], writes=[w1b[b]])
                    if DEBUG_CUT != 2:
                        P.dma("pool", w2b[b][:], w2[fb * 512:(fb + 1) * 512, :].rearrange("(c p) n -> p c n", p=128), writes=[w2b[b]])

                def stage1(fb):
                    b = fb % 2
                    for fc in range(4):
                        for tg in range(2):
                            pu = psu.next()
                            for k in range(KD if SUBCUT >= 0 else 0):
                                P.pe(lambda e, pu=pu, k=k, fc=fc, tg=tg: e.matmul(pu[:], lhsT=w1b[b][:, k, fc * 128:(fc + 1) * 128],
                                                                                 rhs=hnT[:, k, tg * 512:(tg + 1) * 512],
                                                                                 start=(k == 0), stop=(k == KD - 1)),
                                     reads=[w1b[b], (hnT.name, k, tg)], writes=[pu])
                            r = rt.next()
                            if SUBCUT >= 1:
                                P.act(lambda e, pu=pu, r=r: e.activation(out=r[:], in_=pu[:], func=AF.Relu), reads=[], writes=[pu, r])
                            if SUBCUT >= 2:
                                P.act(lambda e, r=r, fc=fc, tg=tg: e.activation(out=gT[b][:, fc, tg * 512:(tg + 1) * 512], in_=r[:], func=AF.Square),
                                      reads=[r], writes=[(gT[b].name, fc, tg)])

                def stage2(fb):
                    b = fb % 2
                    if DEBUG_CUT == 2:
                        return
                    for t in range(8):
                        for db in range(4):
                            py = psy.next()
                            for fc in range(4):
                                P.pe(lambda e, py=py, fc=fc, t=t, db=db: e.matmul(py[:], lhsT=gT[b][:, fc, t * 128:(t + 1) * 128],
                                                                                 rhs=w2b[b][:, fc, db * 512:(db + 1) * 512],
                                                                                 start=(fc == 0), stop=(fc == 3)),
                                     reads=[w2b[b], (gT[b].name, fc, t // 4)], writes=[py])
                            hs = HT[t][:, db * 512:(db + 1) * 512]
                            P.dve(lambda e, hs=hs, py=py: e.tensor_tensor(out=hs, in0=hs, in1=py[:], op=ALU.add),
                                  reads=[(HT[t].name, db)], writes=[py, (HT[t].name, db)])

                NFB = int(os.environ.get("MK_NFB", "16"))
                loadw(0)
                stage1(0)
                for fb in range(NFB):
                    if fb + 1 < NFB:
                        loadw(fb + 1)
                        stage1(fb + 1)
                    stage2(fb)
                for t in range(8):
                    r0 = (tb * 8 + t) * 128
                    P.dma("sp", H[r0:r0 + 128, :], HT[t][:], reads=hk(HT[t]))
                P.emit()


def ple_pass(kb, C, H, p_l, wple, wgate, nvcol):
    import contextlib
    for tb in range(2):
        with contextlib.ExitStack() as st:
            HT = [kb.sb(st, "ht", [128, D], F32) for _ in range(8)]
            hpT = kb.sb(st, "hpT", [128, KD, 1024], BF16)
            pT = kb.sb(st, "pT", [128, 2, 1024], BF16)
            with contextlib.ExitStack() as st1:
                P = kb.prog()
                for t in range(8):
                    r0 = (tb * 8 + t) * 128
                    P.dma("sp", HT[t][:], H[r0:r0 + 128, :], writes=hk(HT[t]))
                trr = tr_ring(kb, st1)
                norm_T(kb, P, st1, C, HT, nvcol, hpT, trr)
                pf = Ring([kb.sb(st1, "pf", [128, 256], F32) for _ in range(2)])
                pb = Ring([kb.sb(st1, "pb", [128, 256], BF16) for _ in range(2)])
                for t in range(8):
                    r0 = (tb * 8 + t) * 128
                    f, b_ = pf.next(), pb.next()
                    P.dma("sp", f[:], p_l[r0:r0 + 128, :], writes=[f])
                    P.act(lambda e, f=f, b_=b_: e.activation(out=b_[:], in_=f[:], func=AF.Copy), reads=[f], writes=[b_])
                    pt = trr.next()
                    for c in range(2):
                        P.pe(lambda e, pt=pt, c=c, b_=b_: e.transpose(out=pt[:, c * 128:(c + 1) * 128],
                                                                    in_=b_[:, c * 128:(c + 1) * 128], identity=C["identb"][:]),
                             reads=[b_, C["identb"]], writes=[pt])
                    for c in range(2):
                        P.dve(lambda e, pt=pt, c=c, t=t: e.tensor_copy(out=pT[:, c, t * 128:(t + 1) * 128],
                                                                     in_=pt[:, c * 128:(c + 1) * 128]),
                              reads=[], writes=[pt, (pT.name, c, t)])
                P.emit()
            with contextlib.ExitStack() as st2:
                P = kb.prog()
                wg = [kb.sb(st2, "wg", [128, KD, 512], BF16) for _ in range(2)]
                wp = kb.sb(st2, "wp", [128, 2, D], BF16)
                gs = Ring([kb.sb(st2, "gs", [128, 512], F32) for _ in range(2)])
                tm = Ring([kb.sb(st2, "tm", [128, 512], F32) for _ in range(2)])
                psg = Ring([kb.ps(st2, "psg", [128, 512]) for _ in range(3)])
                psp = Ring([kb.ps(st2, "psp", [128, 512]) for _ in range(3)])
                P.dma("pool", wp[:], wple.rearrange("(c p) n -> p c n", p=128), writes=[wp])
                wgv = w_view(wgate)
                P.dma("pool", wg[0][:], wgv[:, :, 0:512], writes=[wg[0]])
                for db in range(4):
                    b = db % 2
                    if db + 1 < 4:
                        P.dma("pool", wg[1 - b][:], wgv[:, :, (db + 1) * 512:(db + 2) * 512], writes=[wg[1 - b]])
                    for t in range(8):
                        pg, pp = psg.next(), psp.next()
                        for k in range(KD):
                            P.pe(lambda e, pg=pg, k=k, t=t, b=b: e.matmul(pg[:], lhsT=hpT[:, k, t * 128:(t + 1) * 128], rhs=wg[b][:, k, :],
                                                                        start=(k == 0), stop=(k == KD - 1)),
                                 reads=[wg[b], (hpT.name, k, t // 4)], writes=[pg])
                        for c in range(2):
                            P.pe(lambda e, pp=pp, c=c, t=t, db=db: e.matmul(pp[:], lhsT=pT[:, c, t * 128:(t + 1) * 128],
                                                                          rhs=wp[:, c, db * 512:(db + 1) * 512], start=(c == 0), stop=(c == 1)),
                                 reads=[wp, (pT.name, c, t)], writes=[pp])
                        g_, m_ = gs.next(), tm.next()
                        P.act(lambda e, g_=g_, pg=pg: e.activation(out=g_[:], in_=pg[:], func=AF.Sigmoid), reads=[], writes=[pg, g_])
                        P.dve(lambda e, m_=m_, pp=pp, g_=g_: e.tensor_tensor(out=m_[:], in0=pp[:], in1=g_[:], op=ALU.mult),
                              reads=[g_], writes=[pp, m_])
                        hs = HT[t][:, db * 512:(db + 1) * 512]
                        P.dve(lambda e, hs=hs, m_=m_: e.tensor_tensor(out=hs, in0=hs, in1=m_[:], op=ALU.add),
                               reads=[m_, (HT[t].name, db)], writes=[(HT[t].name, db)])
                for t in range(8):
                    r0 = (tb * 8 + t) * 128
                    P.dma("sp", H[r0:r0 + 128, :], HT[t][:], reads=hk(HT[t]))
                P.emit()


def final_pass(kb, C, H, out, fnorm):
    import contextlib
    with contextlib.ExitStack() as st:
        P = kb.prog()
        FN = kb.sb(st, "FN", [128, D], F32)
        P.dma("sp", FN[:], fnorm, writes=[FN])
        hts = Ring([kb.sb(st, "ht", [128, D], F32) for _ in range(3)])
        ots = Ring([kb.sb(st, "ot", [128, D], F32) for _ in range(2)])
        junk = kb.sb(st, "junk", [128, D], BF16)
        ss = kb.sb(st, "ss", [128, NT], F32)
        sd = kb.sb(st, "sd", [128, NT], F32)
        rs = kb.sb(st, "rs", [128, NT], F32)
        P.dve(lambda e: e.memset(ss[:], 0.0), writes=[ss])
        for t in range(NT):
            h, o = hts.next(), ots.next()
            P.dma("sp", h[:], H[t * 128:(t + 1) * 128, :], writes=[h])
            P.act(lambda e, h=h, t=t: e.activation(out=junk[:], in_=h[:], func=AF.Square, accum_out=ss[:, t:t + 1]),
                  reads=[h, ss], writes=[junk, (ss.name, t)])
            P.act(lambda e, t=t: e.activation(out=sd[:, t:t + 1], in_=ss[:, t:t + 1], func=AF.Sqrt, bias=C["eps"][:, 0:1], scale=1.0 / D),
                  reads=[(ss.name, t)], writes=[(sd.name, t)])
            P.dve(lambda e, t=t: e.reciprocal(out=rs[:, t:t + 1], in_=sd[:, t:t + 1]), reads=[(sd.name, t)], writes=[(rs.name, t)])
            P.dve(lambda e, h=h, o=o, t=t: e.scalar_tensor_tensor(out=o[:], in0=h[:], scalar=rs[:, t:t + 1], in1=FN[:],
                                                                op0=ALU.mult, op1=ALU.mult),
                  reads=[h, (rs.name, t), FN], writes=[o])
            P.dma("sp", out[t * 128:(t + 1) * 128, :], o[:], reads=[o])
        P.emit()


def build(stages):
    import contextlib
    kb = KB()
    nc = kb.nc
    x_own = kb.din("x_own", [TO, D])
    out = nc.dram_tensor("out", [TO, D], F32, kind="ExternalOutput").ap()
    H = kb.dscr("H", [TO, D], F32)
    cf = kb.din("cf", [128, 5 * 128])
    nv = kb.din("nv", [128, NVC])
    used = {"x_own", "cf", "nv"}
    ins = {}

    def inp(name, shape):
        if name not in ins:
            ins[name] = kb.din(name, shape)
            used.add(name)
        return ins[name]

    with contextlib.ExitStack() as gst:
        C = {}
        CF = kb.sb(gst, "CF", [128, 5, 128], F32)
        C["CF"] = CF
        C["identb"] = kb.sb(gst, "identb", [128, 128], BF16)
        C["onesb"] = kb.sb(gst, "onesb", [128, 128], BF16)
        C["NV"] = kb.sb(gst, "NV", [128, NVC], F32)
        C["eps"] = kb.sb(gst, "eps", [128, 1], F32)
        P = kb.prog()
        P.dma("sp", CF[:], cf.rearrange("p (a b) -> p a b", a=5), writes=[CF])
        P.dma("sp", C["NV"][:], nv, writes=[C["NV"]])
        P.dve(lambda e: e.tensor_copy(out=C["identb"][:], in_=CF[:, 0, :]), reads=[CF], writes=[C["identb"]])
        P.dve(lambda e: e.tensor_copy(out=C["onesb"][:], in_=CF[:, 4, :]), reads=[CF], writes=[C["onesb"]])
        P.dve(lambda e: e.memset(C["eps"][:], EPS), writes=[C["eps"]])
        for i in range(4):
            P.dma("sp", H[i * 512:(i + 1) * 512, :], x_own[i * 512:(i + 1) * 512, :], writes=[("H", i)])
        P.emit()

        C["one"] = kb.sb(gst, "one", [128, 1], F32)
        P = kb.prog()
        P.dve(lambda e: e.memset(C["one"][:], 1.0), writes=[C["one"]])
        P.emit()
        G = None
        S_ = {}

        def gdn_setup():
            nonlocal G
            if G is not None:
                return
            G = {}
            G["CW"] = kb.sb(gst, "CW", [128, 2, 32, 4], F32)
            G["AB"] = kb.sb(gst, "AB", [128, 2, 2, 16], F32)
            G["BG"] = kb.sb(gst, "BG", [128, 32, 32], F32)
            G["NEA"] = kb.sb(gst, "NEA", [128, 16], F32)
            G["SEL16"] = kb.sb(gst, "SEL16", [16, 16, 128], F32)
            G["SELC"] = kb.sb(gst, "SELC", [128, 2], F32)
            P = kb.prog()
            P.dma("sp", G["CW"][:], inp("convw", [128, 256]).rearrange("p (l c j) -> p l c j", l=2, c=32), writes=[G["CW"]])
            P.dma("sp", G["AB"][:], inp("ab", [128, 64]).rearrange("p (l a h) -> p l a h", l=2, a=2), writes=[G["AB"]])
            P.dma("sp", G["SEL16"][:], inp("sel16", [16, 2048]).rearrange("p (h m) -> p h m", h=16), writes=[G["SEL16"]])
            P.dma("sp", G["SELC"][:], inp("selc", [128, 2]), writes=[G["SELC"]])
            G["lvlmask"] = inp("lvlmask", [128, 7680])
            P.emit()
            S_["XNT_own"] = kb.dscr("XNT_own", [2048, TO], BF16)
            S_["XNT_all"] = kb.dscr("XNT_all", [8, 512, TO], BF16)
            S_["QKVZ"] = kb.dscr("QKVZ", [48, 128, T], BF16)
            S_["OT_lo"] = kb.dscr("OT_lo", [2048, TO], BF16)
            S_["OT_hi"] = kb.dscr("OT_hi", [2048, TO], BF16)
            S_["G_lo"] = kb.dscr("G_lo", [8, 512, TO], BF16)
            S_["G_hi"] = kb.dscr("G_hi", [8, 512, TO], BF16)

        A = {}

        def att_setup():
            if A:
                return
            A["LAMV"] = kb.sb(gst, "LAMV", [128, 2, 4, 128], F32)
            A["SUBW"] = kb.sb(gst, "SUBW", [128, 2, 256], F32)
            A["SW"] = kb.sb(gst, "SW", [128, 256], F32)
            A["nlam"] = kb.sb(gst, "nlam", [128, 1], F32)
            A["l2t"] = kb.sb(gst, "l2t", [128, 128], F32)
            A["lsum"] = kb.sb(gst, "lsum", [128, 2], F32)
            A["BT"] = kb.sb(gst, "BT", [128, 8, 32], F32)
            A["MA"] = kb.sb(gst, "MA", [128, 128], BF16)
            A["MB"] = kb.sb(gst, "MB", [128, 128], BF16)
            P = kb.prog()
            P.dma("sp", A["LAMV"][:], inp("lamv", [128, 1024]).rearrange("p (l a d) -> p l a d", l=2, a=4), writes=[A["LAMV"]])
            P.dma("sp", A["SUBW"][:], inp("sublnw", [128, 512]).rearrange("p (l d) -> p l d", l=2), writes=[A["SUBW"]])
            P.dma("sp", A["BT"][:], inp("bt", [128, 256]).rearrange("p (h d) -> p h d", h=8), writes=[A["BT"]])
            mab = inp("mab", [128, 256])
            P.dma("pool", A["MA"][:], mab[:, 0:128], writes=[A["MA"]])
            P.dma("pool", A["MB"][:], mab[:, 128:256], writes=[A["MB"]])
            P.emit()
            S_["KT_own"] = kb.dscr("KT_own", [2048, TO], BF16)
            S_["KT_all"] = kb.dscr("KT_all", [8, 512, TO], BF16)
            S_["V_own"] = kb.dscr("V_own", [TO, 2048], BF16)
            S_["V_all"] = kb.dscr("V_all", [8, 512, 2048], BF16)
            S_["QT"] = kb.dscr("QT", [16, 128, TO], BF16)

        for stg in stages:
            kind, li = stg[:-1], int(stg[-1]) if stg[-1].isdigit() else None
            if kind == "gdn":
                gdn_setup()
                S_["out"] = out
                gi = {"w_in": [inp(f"w_in_{l}", [D, NCOLS_IN]) if l == li else None for l in range(2)],
                      "w_out": [inp(f"w_out_{l}", [4096, D]) if l == li else None for l in range(2)]}
                gdn_layer(kb, C, G, li, H, S_, gi)
            elif stg == "kv":
                att_setup()
                kv_pass(kb, C, H, inp("w_kv", [D, 4096]), 12 * 16, S_["KT_own"], S_["V_own"])
                allgather(kb, S_["KT_own"], S_["KT_all"])
                allgather(kb, S_["V_own"], S_["V_all"])
            elif kind == "att":
                att_setup()
                attn_layer(kb, C, A, li, H, S_, inp(f"w_q_{li - 2}", [D, D]), inp(f"w_o_{li - 2}", [D, D]), li * 16)
            elif kind == "mlp":
                w1 = inp(f"mlp_w1_{li}", [D, FF])
                w2 = inp(f"mlp_w2_{li}", [FF, D])
                mlp_pass(kb, C, H, w1, w2, (4 + li) * 16)
            elif kind == "ple":
                p_own = inp("p_own", [DEPTH, TO, 256])
                wple = inp(f"ple_w_proj_{li}", [256, D])
                wgate = inp(f"ple_w_gate_{li}", [D, D])
                ple_pass(kb, C, H, p_own[li], wple, wgate, (8 + li) * 16)
            elif stg == "final":
                final_pass(kb, C, H, out, inp("fnorm", [128, D]))
            elif stg == "outh":
                P = kb.prog()
                for i in range(4):
                    P.dma("sp", out[i * 512:(i + 1) * 512, :], H[i * 512:(i + 1) * 512, :])
                P.emit()
            else:
                raise ValueError(stg)
    return nc, sorted(used)


def _fm(v):
    return np.ascontiguousarray(np.asarray(v, np.float32).reshape(16, 128).T)


def host_consts():
    i = np.arange(128)
    ident = np.eye(128, dtype=np.float32)
    U = (i[:, None] <= i[None, :]).astype(np.float32)
    maskl = np.where(i[:, None] > i[None, :], 0.0, BIG).astype(np.float32)
    masklt = np.where(i[None, :] >= i[:, None], 0.0, BIG).astype(np.float32)
    ones = np.ones((128, 128), np.float32)
    return np.concatenate([ident, U, maskl, masklt, ones], axis=1)


def prep_inputs(inputs, used):
    g = {k: np.asarray(v) for k, v in inputs.items()}
    cf = host_consts()
    nvl = [_fm(g["norm_mix"][i]) for i in range(4)] + [_fm(g["norm_mlp"][i]) for i in range(4)] + \
          [_fm(g["norm_ple"][i]) for i in range(4)] + [_fm(g["kv_norm"])]
    nvl.append(np.ascontiguousarray(g["gdn_norm_w"].astype(np.float32).T))
    nv = np.ascontiguousarray(np.concatenate(nvl, axis=1))
    fnorm = np.ascontiguousarray(np.broadcast_to(g["final_norm"].astype(np.float32)[None, :], (128, D)))
    maps = []
    for c in range(8):
        b, s = c // 2, c % 2
        m = {}
        m["x_own"] = np.ascontiguousarray(g["x"][b, s * TO:(s + 1) * TO])
        m["p_own"] = np.ascontiguousarray(g["p"][:, b, s * TO:(s + 1) * TO, :])
        m["cf"] = cf
        m["nv"] = nv
        m["fnorm"] = fnorm
        pp = np.arange(128, dtype=np.float32)
        bt = np.zeros((128, 8, 32), np.float32)
        for hh in range(8):
            slope = 2.0 ** -(hh + 1)
            for dd in range(32):
                dg = dd - 16 + 16 * s
                bt[:, hh, dd] = slope * (pp - 127.0 - 128.0 * dg) if dg >= 0 else -BIG
        m["bt"] = bt.reshape(128, 256)
        tri = (pp[None, :] >= pp[:, None]).astype(np.float32)
        mab = np.zeros((128, 256), np.float32)
        mab[:, 0:128] = tri if s == 0 else 1.0
        mab[:, 128:256] = tri if s == 1 else 0.0
        m["mab"] = mab
        lamv = np.stack([np.stack([g["diff_lambda_q1"][jj], g["diff_lambda_k1"][jj], g["diff_lambda_q2"][jj], g["diff_lambda_k2"][jj]]) for jj in range(2)])
        m["lamv"] = np.ascontiguousarray(np.broadcast_to(lamv.astype(np.float32).reshape(1, 1024), (128, 1024)))
        m["sublnw"] = np.ascontiguousarray(np.broadcast_to(g["diff_subln_w"].astype(np.float32).reshape(1, 512), (128, 512)))
        m["w_kv"] = g["w_kv"]
        for jj in range(2):
            m[f"w_q_{jj}"] = g["diff_w_q"][jj]
            m[f"w_o_{jj}"] = g["diff_w_o"][jj]
        sel16 = np.zeros((16, 16, 128), np.float32)
        for hh in range(16):
            sel16[hh, hh, :] = 1.0
        m["sel16"] = sel16.reshape(16, 2048)
        ii = np.arange(128)
        lm = np.zeros((128, 15, 4, 128), np.float32)
        for l in range(7):
            bsz = 2 ** l
            mk = (((ii[:, None] // bsz) % 2 == 1) & ((ii[None, :] // bsz) == (ii[:, None] // bsz) - 1)).astype(np.float32)
            lm[:, l, :, :] = mk[:, None, :]
            lm[:, 7 + l, :, :] = mk.T[:, None, :]
        lm[:, 14, :, :] = np.eye(128, dtype=np.float32)[:, None, :]
        m["lvlmask"] = lm.reshape(128, 7680)
        selc = np.zeros((128, 2), np.float32)
        selc[:, s] = 1.0
        m["selc"] = selc
        cw = np.zeros((128, 2, 32, 4), np.float32)
        ab = np.zeros((128, 2, 2, 16), np.float32)
        for l in range(2):
            if f"w_in_{l}" in used:
                wi = g["gdn_w_in"][l]
                m[f"w_in_{l}"] = np.ascontiguousarray(np.concatenate([
                    wi[:, s * 1024:(s + 1) * 1024], wi[:, 2048 + s * 1024:2048 + (s + 1) * 1024],
                    wi[:, 4096 + s * 2048:4096 + (s + 1) * 2048], wi[:, 8192 + s * 2048:8192 + (s + 1) * 2048],
                    wi[:, 12288 + s * 16:12288 + (s + 1) * 16], wi[:, 12320 + s * 16:12320 + (s + 1) * 16]], axis=1))
            if f"w_out_{l}" in used:
                m[f"w_out_{l}"] = g["gdn_w_out"][l]
            cwl = g["gdn_conv_w"][l]
            own = np.concatenate([cwl[:, s * 1024:(s + 1) * 1024], cwl[:, 2048 + s * 1024:2048 + (s + 1) * 1024],
                                  cwl[:, 4096 + s * 2048:4096 + (s + 1) * 2048]], axis=1)
            cw[:, l] = own.reshape(4, 32, 128).transpose(2, 1, 0)
            ab[:, l, 0, :] = g["gdn_a_log"][l, s * 16:(s + 1) * 16][None, :]
            ab[:, l, 1, :] = g["gdn_dt_bias"][l, s * 16:(s + 1) * 16][None, :]
        m["convw"] = cw.reshape(128, 256)
        m["ab"] = ab.reshape(128, 64)
        for k in ("mlp_w1", "mlp_w2", "ple_w_proj", "ple_w_gate"):
            for li in range(DEPTH):
                if f"{k}_{li}" in used:
                    m[f"{k}_{li}"] = g[k][li]
        maps.append({k: v for k, v in m.items() if k in used})
    return maps


ALL_STAGES = ["gdn0", "mlp0", "ple0", "gdn1", "mlp1", "ple1", "kv", "att2", "mlp2", "ple2", "att3", "mlp3", "ple3", "final"]
_CACHE = {}


def run_stages(inputs, stages):
    key = tuple(stages)
    if key not in _CACHE:
        _CACHE[key] = build(stages)
    nc, used = _CACHE[key]
    maps = prep_inputs(inputs, set(used))
    res = run_bass_kernel_spmd(nc, maps, core_ids=list(range(8)))
    outs = [r["out"] for r in res.results]
    full = np.empty((4, T, D), np.float32)
    for c in range(8):
        full[c // 2, (c % 2) * TO:(c % 2 + 1) * TO] = outs[c]
    return full


def kernel(**inputs):
    return run_stages(inputs, ALL_STAGES)


def gdn_norm_gather(kb, C, H, nvcol, XNT_own, XNT_all):
    import contextlib
    xv = XNT_own.rearrange("(k p) t -> p k t", p=128)
    for tb in range(2):
        with contextlib.ExitStack() as st:
            HT = [kb.sb(st, "ht", [128, D], F32) for _ in range(8)]
            xT = kb.sb(st, "xT", [128, KD, 1024], BF16)
            P = kb.prog()
            for t in range(8):
                r0 = (tb * 8 + t) * 128
                P.dma("sp", HT[t][:], H[r0:r0 + 128, :], writes=hk(HT[t]))
            norm_T(kb, P, st, C, HT, nvcol, xT, tr_ring(kb, st))
            for kh in range(2):
                P.dma("sp", xv[:, kh * 8:(kh + 1) * 8, tb * 1024:(tb + 1) * 1024], xT[:, kh * 8:(kh + 1) * 8, :],
                      reads=[(xT.name, k, g) for k in range(kh * 8, kh * 8 + 8) for g in range(2)])
            P.emit()
    allgather(kb, XNT_own, XNT_all)


PAIRS = [[0, 1], [2, 3], [4, 5], [6, 7]]


def allgather(kb, src, dst, nch=8):
    R = src.shape[0]
    rc = R // nch
    P = kb.prog()
    for j in range(nch):
        P.add("pool", lambda e, j=j: e.collective_compute("AllGather", ALU.bypass, replica_groups=PAIRS,
                                                        ins=[src[j * rc:(j + 1) * rc, :].opt()], outs=[dst[j].opt()]),
              dma=True, inc=1)
    P.emit()


def gdn_inproj(kb, C, G, li, XNT_all, w_in, QKVZ):
    import contextlib
    with contextlib.ExitStack() as st:
        XT = kb.sb(st, "XT", [128, KD, T], BF16)
        wb = [kb.sb(st, "wb", [128, KD, 256], BF16) for _ in range(2)]
        obuf = [kb.sb(st, "ob", [128, T], BF16) for _ in range(2)]
        xbuf = [kb.sb(st, "xb", [128, 515], F32) for _ in range(2)]
        cacc = Ring([kb.sb(st, "ca", [128, 512], F32) for _ in range(2)])
        sl = Ring([kb.sb(st, "sl", [128, 512], F32) for _ in range(2)])
        sq = Ring([kb.sb(st, "sq", [128, 512], BF16) for _ in range(2)])
        sd = Ring([kb.sb(st, "sd", [128, 512], F32) for _ in range(2)])
        ri = Ring([kb.sb(st, "ri", [128, 512], F32) for _ in range(2)])
        wba = kb.sb(st, "wba", [128, KD, 32], BF16)
        xa = kb.sb(st, "xa", [128, 16], F32)
        ea = kb.sb(st, "ea", [128, 16], F32)
        sp_ = kb.sb(st, "sp", [128, 16], F32)
        psa = Ring([kb.ps(st, "psa", [128, 512]) for _ in range(3)])
        pss = Ring([kb.ps(st, "pss", [128, 512]) for _ in range(2)])
        psb = Ring([kb.ps(st, "psb", [128, 512]) for _ in range(2)])
        CW, AB, BG, NEA = G["CW"], G["AB"], G["BG"], G["NEA"]
        onesb, eps = C["onesb"], C["eps"]
        P = kb.prog()
        for r in range(2):
            for j in range(8):
                P.dma("sp", XT[:, 2 * j:2 * j + 2, r * TO:(r + 1) * TO],
                      XNT_all[j, r * 256:(r + 1) * 256, :].rearrange("(k p) t -> p k t", p=128),
                      writes=[(XT.name, r, j)])
        P.act(lambda e: e.activation(out=NEA[:], in_=AB[:, li, 0, :], func=AF.Exp), reads=[AB], writes=[NEA])
        P.dve(lambda e: e.tensor_scalar(out=NEA[:], in0=NEA[:], scalar1=-1.0, scalar2=None, op0=ALU.mult), reads=[NEA], writes=[NEA])
        wv = w_view(w_in)
        P.dma("pool", wba[:], wv[:, :, 6144:6176], writes=[wba])

        def loadw(wbk):
            P.dma("pool", wb[wbk % 2][:], wv[:, :, wbk * 256:(wbk + 1) * 256], writes=[wb[wbk % 2]])

        loadw(0)
        for wbk in range(24):
            if wbk + 1 < 24:
                loadw(wbk + 1)
            b = wbk % 2
            for cti in range(2):
                ct = wbk * 2 + cti
                typ = "q" if ct < 8 else "k" if ct < 16 else "v" if ct < 32 else "z"
                ob = obuf[ct % 2]
                for blk in range(8):
                    pa = psa.next()
                    osl = ob[:, blk * 512:(blk + 1) * 512]
                    okey = (ob.name, blk)
                    for k in range(KD):
                        P.pe(lambda e, pa=pa, k=k, cti=cti, blk=blk, b=b: e.matmul(
                            pa[:], lhsT=wb[b][:, k, cti * 128:(cti + 1) * 128], rhs=XT[:, k, blk * 512:(blk + 1) * 512],
                            start=(k == 0), stop=(k == KD - 1)),
                            reads=[wb[b], (XT.name, blk // 4, k // 2)], writes=[pa])
                    if typ == "z":
                        P.act(lambda e, pa=pa, osl=osl: e.activation(out=osl, in_=pa[:], func=AF.Silu), writes=[pa, okey])
                        continue
                    xb_, prev = xbuf[blk % 2], xbuf[(blk + 1) % 2]
                    if blk == 0:
                        P.dve(lambda e, xb_=xb_: e.memset(xb_[:, 0:3], 0.0), writes=[(xb_.name, "h")])
                    else:
                        P.dve(lambda e, xb_=xb_, prev=prev: e.tensor_copy(out=xb_[:, 0:3], in_=prev[:, 512:515]),
                              reads=[prev], writes=[(xb_.name, "h")])
                    P.act(lambda e, pa=pa, xb_=xb_: e.activation(out=xb_[:, 3:515], in_=pa[:], func=AF.Copy), writes=[pa, xb_])
                    ca = cacc.next()
                    P.dve(lambda e, ca=ca, xb_=xb_, ct=ct: e.tensor_scalar(out=ca[:], in0=xb_[:, 3:515], scalar1=CW[:, li, ct, 3:4],
                                                                        scalar2=None, op0=ALU.mult),
                          reads=[xb_, CW], writes=[ca])
                    for j in (2, 1, 0):
                        P.dve(lambda e, ca=ca, xb_=xb_, ct=ct, j=j: e.scalar_tensor_tensor(
                            out=ca[:], in0=xb_[:, j:j + 512], scalar=CW[:, li, ct, j:j + 1], in1=ca[:], op0=ALU.mult, op1=ALU.add),
                            reads=[xb_, (xb_.name, "h"), CW], writes=[ca])
                    if typ == "v":
                        P.act(lambda e, ca=ca, osl=osl: e.activation(out=osl, in_=ca[:], func=AF.Silu), reads=[ca], writes=[okey])
                        continue
                    s_, q_, d_, r_ = sl.next(), sq.next(), sd.next(), ri.next()
                    P.act(lambda e, ca=ca, s_=s_: e.activation(out=s_[:], in_=ca[:], func=AF.Silu), reads=[ca], writes=[s_])
                    P.act(lambda e, s_=s_, q_=q_: e.activation(out=q_[:], in_=s_[:], func=AF.Square), reads=[s_], writes=[q_])
                    ps_ = pss.next()
                    P.pe(lambda e, ps_=ps_, q_=q_: e.matmul(ps_[:], lhsT=onesb[:], rhs=q_[:], start=True, stop=True),
                         reads=[q_, onesb], writes=[ps_])
                    P.act(lambda e, ps_=ps_, d_=d_: e.activation(out=d_[:], in_=ps_[:], func=AF.Sqrt, bias=eps[:, 0:1], scale=1.0),
                          reads=[eps], writes=[ps_, d_])
                    P.dve(lambda e, d_=d_, r_=r_: e.reciprocal(out=r_[:], in_=d_[:]), reads=[d_], writes=[r_])
                    qs = (128.0 ** -0.5) if typ == "q" else 1.0
                    P.dve(lambda e, s_=s_, r_=r_, osl=osl, qs=qs: e.scalar_tensor_tensor(out=osl, in0=s_[:], scalar=qs, in1=r_[:],
                                                                                     op0=ALU.mult, op1=ALU.mult),
                          reads=[s_, r_], writes=[okey])
                P.dma("sp", QKVZ[ct], ob[:], reads=[(ob.name, blk) for blk in range(8)])
        for tt in range(32):
            pb = psb.next()
            for k in range(KD):
                P.pe(lambda e, pb=pb, k=k, tt=tt: e.matmul(pb[:, 0:32], lhsT=XT[:, k, tt * 128:(tt + 1) * 128], rhs=wba[:, k, :],
                                                        start=(k == 0), stop=(k == KD - 1)),
                     reads=[wba, (XT.name, tt // 16, k // 2)], writes=[pb])
            P.act(lambda e, pb=pb, tt=tt: e.activation(out=BG[:, tt, 0:16], in_=pb[:, 0:16], func=AF.Sigmoid), writes=[pb, (BG.name, tt, 0)])
            P.dve(lambda e, pb=pb: e.tensor_tensor(out=xa[:], in0=pb[:, 16:32], in1=AB[:, li, 1, :], op=ALU.add), reads=[AB], writes=[pb, xa])
            P.act(lambda e: e.activation(out=ea[:], in_=xa[:], func=AF.Exp), reads=[xa], writes=[ea])
            P.act(lambda e: e.activation(out=sp_[:], in_=ea[:], func=AF.Ln, bias=C["one"][:, 0:1], scale=1.0), reads=[ea, C["one"]], writes=[sp_])
            P.dve(lambda e, tt=tt: e.tensor_tensor(out=BG[:, tt, 16:32], in0=sp_[:], in1=NEA[:], op=ALU.mult),
                  reads=[sp_, NEA], writes=[(BG.name, tt, 1)])
        P.emit()


def gdn_scan(kb, C, G, li, QKVZ, OT_lo, OT_hi):
    import contextlib
    CF, identb, onesb, NV, eps = C["CF"], C["identb"], C["onesb"], C["NV"], C["eps"]
    identf, U, MASKL, MASKLT = CF[:, 0, :], CF[:, 1, :], CF[:, 2, :], CF[:, 3, :]
    onesf = CF[:, 4, :]
    BG, SEL16 = G["BG"], G["SEL16"]
    qkvz_v = QKVZ.rearrange("c p t -> p c t")
    with contextlib.ExitStack() as st:
        S = kb.sb(st, "S", [128, HL, 128], F32)
        Sbf = kb.sb(st, "Sbf", [128, HL, 128], BF16)
        inb = [kb.sb(st, "inb", [128, 48, 512], BF16) for _ in range(1)]
        otst = [kb.sb(st, "otst", [128, HL, 512], BF16) for _ in range(1)]
        tk = {n: kb.sb(st, n, [128, 16], F32) for n in ("gc", "egc", "bege", "edl", "negb", "glb")}
        gcT = kb.sb(st, "gcT", [16, 128], F32)

        def gt(name, dt, depth=1):
            return PRing(Ring([kb.sb(st, name, [128, 4, 128], dt) for _ in range(depth)]),
                         Ring([kb.sb(st, name, [128, 4, 128], dt) for _ in range(depth)]))

        Lm, LT, ER = gt("Lm", F32), gt("LT", F32), gt("ER", F32)
        tmpA, tmpB = gt("tmpA", F32), gt("tmpB", F32)
        Yb = [gt("Y0", BF16)]
        Pb = [gt("P0", BF16)]
        ymr, pmr, w1r, w2r, tcr, ttr = gt("YM", BF16), gt("PM", BF16), gt("W1", BF16), gt("W2", BF16), gt("Tc", BF16, 2), gt("TTc", BF16, 2)
        t1, rr, vnew, MT, qd, kdec = gt("t1", F32), gt("rr", BF16), gt("vnew", BF16), gt("MT", BF16), gt("qd", BF16), gt("kdec", BF16)
        osb, osq, sdn, rsn, onn = gt("osb", F32), gt("osq", BF16), gt("sdn", F32), gt("rsn", F32), gt("onn", F32)
        psf = PRing(Ring([kb.ps(st, "psf", [128, 4, 128]) for _ in range(3)]), Ring([kb.ps(st, "psf", [128, 4, 128]) for _ in range(3)]))
        psh = PRing(Ring([kb.ps(st, "psh", [128, 8, 128], BF16) for _ in range(1)]), Ring([kb.ps(st, "psh", [128, 8, 128], BF16) for _ in range(1)]))

        G["LM"] = kb.sb(st, "LM", [128, 15, 4, 128], BF16)
        P = kb.prog()
        P.dma("pool", G["LM"][:], G["lvlmask"].rearrange("p (l a m) -> p l a m", l=15, a=4), writes=[G["LM"]])
        P.dve(lambda e: e.memset(S[:], 0.0), writes=[S])
        P.dve(lambda e: e.memset(Sbf[:], 0.0), writes=[Sbf])
        P.emit()

        for sc in range(8):
            P = kb.prog()
            ib = inb[0]
            ot = otst[0]
            for q4 in range(4):
                P.dma("sp", ib[:, q4 * 12:(q4 + 1) * 12, :], qkvz_v[:, q4 * 12:(q4 + 1) * 12, sc * 512:(sc + 1) * 512],
                      writes=[(ib.name, q4)])
            ibk = [(ib.name, q4) for q4 in range(4)]
            def chunk(cl, sc=sc, P=P):
                n = sc * 4 + cl
                cs = slice(cl * 128, (cl + 1) * 128)
                qT = lambda hq: ib[:, hq, cs]
                kT = lambda hq: ib[:, 8 + hq, cs]
                vT = lambda h: ib[:, 16 + h, cs]
                zT = lambda h: ib[:, 32 + h, cs]
                beta, g_ = BG[:, n, 0:16], BG[:, n, 16:32]
                bgk = [(BG.name, n, 0), (BG.name, n, 1)]
                pg = psf.next()
                P.pe(lambda e, pg=pg, g_=g_: e.matmul(pg[:, 0, 0:16], lhsT=U, rhs=g_, start=True, stop=True), reads=[CF] + bgk, writes=[pg])
                P.pe(lambda e, pg=pg, g_=g_: e.matmul(pg[:, 1, 0:16], lhsT=onesf, rhs=g_, start=True, stop=True), reads=[CF] + bgk, writes=[pg])
                P.dve(lambda e, pg=pg: e.tensor_copy(out=tk["gc"][:], in_=pg[:, 0, 0:16]), writes=[pg, tk["gc"]])
                P.dve(lambda e, pg=pg: e.tensor_copy(out=tk["glb"][:], in_=pg[:, 1, 0:16]), writes=[pg, tk["glb"]])
                P.act(lambda e: e.activation(out=tk["egc"][:], in_=tk["gc"][:], func=AF.Exp), reads=[tk["gc"]], writes=[tk["egc"]])
                P.dve(lambda e, beta=beta: e.tensor_tensor(out=tk["bege"][:], in0=tk["egc"][:], in1=beta, op=ALU.mult),
                      reads=[tk["egc"]] + bgk, writes=[tk["bege"]])
                P.dve(lambda e: e.tensor_tensor(out=tk["edl"][:], in0=tk["glb"][:], in1=tk["gc"][:], op=ALU.subtract),
                      reads=[tk["glb"], tk["gc"]], writes=[tk["edl"]])
                P.act(lambda e: e.activation(out=tk["edl"][:], in_=tk["edl"][:], func=AF.Exp), reads=[tk["edl"]], writes=[tk["edl"]])
                P.dve(lambda e, beta=beta: e.tensor_scalar(out=tk["negb"][:], in0=beta, scalar1=-1.0, scalar2=None, op0=ALU.mult),
                      reads=bgk, writes=[tk["negb"]])
                pt_ = psf.next()
                P.pe(lambda e, pt_=pt_: e.transpose(out=pt_[0:16, 0, :], in_=tk["gc"][:], identity=identf), reads=[tk["gc"], CF], writes=[pt_])
                P.dve(lambda e, pt_=pt_: e.tensor_copy(out=gcT[:], in_=pt_[0:16, 0, :]), writes=[pt_, gcT])

                def group(g4):
                    PAR[0] = g4 % 2
                    hs = [g4 * 4 + i for i in range(4)]
                    hqs = [g4 * 2, g4 * 2 + 1]
                    Lm_, LT_, ER_, tA, tB = Lm.next(), LT.next(), ER.next(), tmpA.next(), tmpB.next()
                    pgr = psf.next()
                    for i, h in enumerate(hs):
                        P.pe(lambda e, pgr=pgr, i=i, h=h: e.matmul(pgr[:, i, :], lhsT=SEL16[:, h, :], rhs=gcT[:], start=True, stop=True),
                             reads=[SEL16, gcT], writes=[pgr])
                    for i, h in enumerate(hs):
                        gcol = tk["gc"][:, h:h + 1]
                        P.dve(lambda e, pgr=pgr, i=i, gcol=gcol, tA=tA: e.scalar_tensor_tensor(
                            out=tA[:, i, :], in0=pgr[:, i, :], scalar=gcol, in1=MASKL, op0=ALU.subtract, op1=ALU.add),
                            reads=[tk["gc"], CF], writes=[pgr, (tA.name, i)])
                        P.dve(lambda e, pgr=pgr, i=i, gcol=gcol, tB=tB: e.scalar_tensor_tensor(
                            out=tB[:, i, :], in0=pgr[:, i, :], scalar=gcol, in1=MASKLT, op0=ALU.subtract, op1=ALU.subtract),
                            reads=[tk["gc"], CF], writes=[pgr, (tB.name, i)])
                    P.act(lambda e, pgr=pgr, ER_=ER_: e.activation(out=ER_[:], in_=pgr[:], func=AF.Exp), writes=[pgr] + [(ER_.name, i) for i in range(4)])
                    P.act(lambda e, tA=tA, Lm_=Lm_: e.activation(out=Lm_[:], in_=tA[:], func=AF.Exp, scale=-1.0),
                          reads=[(tA.name, i) for i in range(4)], writes=[(Lm_.name, i) for i in range(4)])
                    P.act(lambda e, tB=tB, LT_=LT_: e.activation(out=LT_[:], in_=tB[:], func=AF.Exp),
                          reads=[(tB.name, i) for i in range(4)], writes=[(LT_.name, i) for i in range(4)])
                    pkk = psf.next()
                    for a, hq in enumerate(hqs):
                        P.pe(lambda e, pkk=pkk, a=a, hq=hq: e.matmul(pkk[:, a, :], lhsT=kT(hq), rhs=kT(hq), start=True, stop=True),
                             reads=ibk, writes=[pkk])
                        P.pe(lambda e, pkk=pkk, a=a, hq=hq: e.matmul(pkk[:, 2 + a, :], lhsT=kT(hq), rhs=qT(hq), start=True, stop=True),
                             reads=ibk, writes=[pkk])
                    Y, Pm = Yb[0].next(), Pb[0].next()
                    MT_ = MT.next()
                    for i, h in enumerate(hs):
                        P.dve(lambda e, pkk=pkk, i=i, h=h, Y=Y, Lm_=Lm_: e.scalar_tensor_tensor(
                            out=Y[:, i, :], in0=pkk[:, i // 2, :], scalar=tk["negb"][:, h:h + 1], in1=Lm_[:, i, :], op0=ALU.mult, op1=ALU.mult),
                            reads=[tk["negb"], (Lm_.name, i)], writes=[pkk, (Y.name, i)])
                        P.dve(lambda e, pkk=pkk, i=i, MT_=MT_, LT_=LT_: e.tensor_tensor(
                            out=MT_[:, i, :], in0=pkk[:, 2 + i // 2, :], in1=LT_[:, i, :], op=ALU.mult),
                            reads=[(LT_.name, i)], writes=[pkk, (MT_.name, i)])
                    ph = psh.next()
                    for i in range(4):
                        P.pe(lambda e, ph=ph, i=i, Y=Y: e.transpose(out=ph[:, i, :], in_=Y[:, i, :], identity=identb[:]),
                             reads=[(Y.name, i), identb], writes=[ph])
                    P.act(lambda e, ph=ph, Pm=Pm: e.activation(out=Pm[:], in_=ph[:, 0:4, :], func=AF.Copy),
                          writes=[ph] + [(Pm.name, i) for i in range(4)])
                    k4_ = lambda t_: [(t_.name, i) for i in range(4)]
                    LMt = G["LM"]
                    Tc = TTc = None
                    for l in range(7):
                        YM, PM = ymr.next(), pmr.next()
                        P.dve(lambda e, YM=YM, Y=Y, l=l: e.tensor_tensor(out=YM[:], in0=Y[:], in1=LMt[:, l, :, :], op=ALU.mult),
                              reads=k4_(Y) + [LMt], writes=k4_(YM))
                        P.dve(lambda e, PM=PM, Pm=Pm, l=l: e.tensor_tensor(out=PM[:], in0=Pm[:], in1=LMt[:, 7 + l, :, :], op=ALU.mult),
                              reads=k4_(Pm) + [LMt], writes=k4_(PM))
                        Tn, TTn = tcr.next(), ttr.next()
                        if l == 0:
                            P.dve(lambda e, TTn=TTn, PM=PM: e.tensor_tensor(out=TTn[:], in0=PM[:], in1=LMt[:, 14, :, :], op=ALU.add),
                                  reads=k4_(PM) + [LMt], writes=k4_(TTn))
                            P.dve(lambda e, Tn=Tn, YM=YM: e.tensor_tensor(out=Tn[:], in0=YM[:], in1=LMt[:, 14, :, :], op=ALU.add),
                                  reads=k4_(YM) + [LMt], writes=k4_(Tn))
                        else:
                            W1 = w1r.next()
                            pw = psf.next()
                            for i in range(4):
                                P.pe(lambda e, pw=pw, i=i, YM=YM, TTc=TTc: e.matmul(pw[:, i, :], lhsT=YM[:, i, :], rhs=TTc[:, i, :], start=True, stop=True),
                                     reads=[(YM.name, i), (TTc.name, i)], writes=[pw])
                            P.act(lambda e, pw=pw, W1=W1: e.activation(out=W1[:], in_=pw[:], func=AF.Copy), writes=[pw] + k4_(W1))
                            pt2 = psf.next()
                            for i in range(4):
                                P.pe(lambda e, pt2=pt2, i=i, Tc=Tc, W1=W1: e.matmul(pt2[:, i, :], lhsT=Tc[:, i, :], rhs=W1[:, i, :], start=True, stop=True),
                                     reads=[(Tc.name, i), (W1.name, i)], writes=[pt2])
                            P.dve(lambda e, pt2=pt2, TTn=TTn, TTc=TTc: e.tensor_tensor(out=TTn[:], in0=pt2[:], in1=TTc[:], op=ALU.add),
                                  reads=k4_(TTc), writes=[pt2] + k4_(TTn))
                            if l < 6:
                                W2 = w2r.next()
                                pw2 = psf.next()
                                for i in range(4):
                                    P.pe(lambda e, pw2=pw2, i=i, PM=PM, Tc=Tc: e.matmul(pw2[:, i, :], lhsT=PM[:, i, :], rhs=Tc[:, i, :], start=True, stop=True),
                                         reads=[(PM.name, i), (Tc.name, i)], writes=[pw2])
                                P.act(lambda e, pw2=pw2, W2=W2: e.activation(out=W2[:], in_=pw2[:], func=AF.Copy), writes=[pw2] + k4_(W2))
                                pt3 = psf.next()
                                for i in range(4):
                                    P.pe(lambda e, pt3=pt3, i=i, TTc=TTc, W2=W2: e.matmul(pt3[:, i, :], lhsT=TTc[:, i, :], rhs=W2[:, i, :], start=True, stop=True),
                                         reads=[(TTc.name, i), (W2.name, i)], writes=[pt3])
                                P.dve(lambda e, pt3=pt3, Tn=Tn, Tc=Tc: e.tensor_tensor(out=Tn[:], in0=pt3[:], in1=Tc[:], op=ALU.add),
                                      reads=k4_(Tc), writes=[pt3] + k4_(Tn))
                        Tc, TTc = Tn, TTn
                    R = TTc
                    TT = R
                    pks = psf.next()
                    for i, h in enumerate(hs):
                        P.pe(lambda e, pks=pks, i=i, h=h: e.matmul(pks[:, i, :], lhsT=kT(h // 2), rhs=Sbf[:, h, :], start=True, stop=True),
                             reads=ibk + [(Sbf.name, h)], writes=[pks])
                    pv = psh.next()
                    for i, h in enumerate(hs):
                        P.pe(lambda e, pv=pv, i=i, h=h: e.transpose(out=pv[:, i, :], in_=vT(h), identity=identb[:]), reads=ibk + [identb], writes=[pv])
                    for a, hq in enumerate(hqs):
                        P.pe(lambda e, pv=pv, a=a, hq=hq: e.transpose(out=pv[:, 4 + a, :], in_=kT(hq), identity=identb[:]), reads=ibk + [identb], writes=[pv])
                    t1_, rr_, vn_, qd_, kd_ = t1.next(), rr.next(), vnew.next(), qd.next(), kdec.next()
                    for i, h in enumerate(hs):
                        P.dve(lambda e, pks=pks, i=i, h=h, t1_=t1_: e.tensor_scalar(out=t1_[:, i, :], in0=pks[:, i, :], scalar1=tk["bege"][:, h:h + 1],
                                                                                 scalar2=None, op0=ALU.mult),
                              reads=[tk["bege"]], writes=[pks, (t1_.name, i)])
                        P.dve(lambda e, pv=pv, i=i, h=h, t1_=t1_, rr_=rr_: e.scalar_tensor_tensor(
                            out=rr_[:, i, :], in0=pv[:, i, :], scalar=BG[:, n, h:h + 1], in1=t1_[:, i, :], op0=ALU.mult, op1=ALU.subtract),
                            reads=bgk + [(t1_.name, i)], writes=[pv, (rr_.name, i)])
                        P.act(lambda e, pv=pv, i=i, h=h, kd_=kd_: e.activation(out=kd_[:, i, :], in_=pv[:, 4 + i // 2, :], func=AF.Copy,
                                                                             scale=tk["edl"][:, h:h + 1]),
                              reads=[tk["edl"]], writes=[pv, (kd_.name, i)])
                    pvn = psf.next()
                    for i in range(4):
                        P.pe(lambda e, pvn=pvn, i=i, TT=TT, rr_=rr_: e.matmul(pvn[:, i, :], lhsT=TT[:, i, :], rhs=rr_[:, i, :], start=True, stop=True),
                             reads=[(TT.name, i), (rr_.name, i)], writes=[pvn])
                    P.act(lambda e, pvn=pvn, vn_=vn_: e.activation(out=vn_[:], in_=pvn[:], func=AF.Copy),
                          writes=[pvn] + [(vn_.name, i) for i in range(4)])
                    for i, h in enumerate(hs):
                        P.dve(lambda e, i=i, h=h, qd_=qd_, ER_=ER_: e.tensor_tensor(out=qd_[:, i, :], in0=qT(h // 2), in1=ER_[:, i, :], op=ALU.mult),
                              reads=ibk + [(ER_.name, i)], writes=[(qd_.name, i)])
                    po = psf.next()
                    for i, h in enumerate(hs):
                        P.pe(lambda e, po=po, i=i, h=h, qd_=qd_: e.matmul(po[:, i, :], lhsT=Sbf[:, h, :], rhs=qd_[:, i, :], start=True, stop=False),
                             reads=[(Sbf.name, h), (qd_.name, i)], writes=[po])
                        P.pe(lambda e, po=po, i=i, vn_=vn_, MT_=MT_: e.matmul(po[:, i, :], lhsT=vn_[:, i, :], rhs=MT_[:, i, :], start=False, stop=True),
                             reads=[(vn_.name, i), (MT_.name, i)], writes=[po])
                    pds = psf.next()
                    for i in range(4):
                        P.pe(lambda e, pds=pds, i=i, kd_=kd_, vn_=vn_: e.matmul(pds[:, i, :], lhsT=kd_[:, i, :], rhs=vn_[:, i, :], start=True, stop=True),
                             reads=[(kd_.name, i), (vn_.name, i)], writes=[pds])
                    for i, h in enumerate(hs):
                        P.dve(lambda e, pds=pds, i=i, h=h, ER_=ER_: e.scalar_tensor_tensor(
                            out=S[:, h, :], in0=S[:, h, :], scalar=ER_[:, i, 127:128], in1=pds[:, i, :], op0=ALU.mult, op1=ALU.add),
                            reads=[(ER_.name, i), (S.name, h)], writes=[pds, (S.name, h)])
                        P.act(lambda e, h=h: e.activation(out=Sbf[:, h, :], in_=S[:, h, :], func=AF.Copy), reads=[(S.name, h)], writes=[(Sbf.name, h)])
                    ob_, oq_, sd_, rs_, on_ = osb.next(), osq.next(), sdn.next(), rsn.next(), onn.next()
                    k4 = lambda t_: [(t_.name, i) for i in range(4)]
                    P.act(lambda e, po=po, ob_=ob_: e.activation(out=ob_[:], in_=po[:], func=AF.Copy), writes=[po] + k4(ob_))
                    P.act(lambda e, ob_=ob_, oq_=oq_: e.activation(out=oq_[:], in_=ob_[:], func=AF.Square), reads=k4(ob_), writes=k4(oq_))
                    pss = psf.next()
                    for i in range(4):
                        P.pe(lambda e, pss=pss, i=i, oq_=oq_: e.matmul(pss[:, i, :], lhsT=onesb[:], rhs=oq_[:, i, :], start=True, stop=True),
                             reads=[onesb, (oq_.name, i)], writes=[pss])
                    P.act(lambda e, pss=pss, sd_=sd_: e.activation(out=sd_[:], in_=pss[:], func=AF.Sqrt, bias=eps[:, 0:1], scale=1.0 / 128),
                          reads=[eps], writes=[pss] + k4(sd_))
                    P.dve(lambda e, sd_=sd_, rs_=rs_: e.reciprocal(out=rs_[:], in_=sd_[:]), reads=k4(sd_), writes=k4(rs_))
                    P.dve(lambda e, ob_=ob_, rs_=rs_, on_=on_: e.tensor_tensor(out=on_[:], in0=ob_[:], in1=rs_[:], op=ALU.mult),
                          reads=k4(ob_) + k4(rs_), writes=k4(on_))
                    for i, h in enumerate(hs):
                        P.dve(lambda e, i=i, h=h, on_=on_: e.scalar_tensor_tensor(
                            out=ot[:, h, cs], in0=on_[:, i, :], scalar=NV[:, 208 + li:209 + li], in1=zT(h), op0=ALU.mult, op1=ALU.mult),
                            reads=[(on_.name, i), NV] + ibk, writes=[(ot.name, h, cl)])
                for ga, gb_ in ((0, 1), (2, 3)):
                    n0 = len(P.ops)
                    group(ga)
                    opsA = P.ops[n0:]
                    del P.ops[n0:]
                    group(gb_)
                    opsB = P.ops[n0:]
                    del P.ops[n0:]
                    PAR[0] = 0
                    for k_ in range(max(len(opsA), len(opsB))):
                        if k_ < len(opsA):
                            P.ops.append(opsA[k_])
                        if k_ < len(opsB):
                            P.ops.append(opsB[k_])

            for cl in range(4):
                chunk(cl)
            dst = OT_lo if sc < 4 else OT_hi
            dv = dst.rearrange("(h e) t -> e h t", e=128)
            P.dma("sp", dv[:, :, (sc % 4) * 512:(sc % 4 + 1) * 512], ot[:],
                  reads=[(ot.name, h, cl) for h in range(HL) for cl in range(4)])
            P.emit()


def gdn_outproj(kb, C, G, H, G_lo, G_hi, w_out):
    import contextlib
    SELC = G["SELC"]
    wv = w_out.rearrange("(c p) n -> p c n", p=128)
    for tb in range(4):
        with contextlib.ExitStack() as st:
            HT = [kb.sb(st, "ht", [128, D], F32) for _ in range(4)]
            oa = kb.sb(st, "oa", [128, 32, 512], BF16)
            ob = kb.sb(st, "ob", [128, 32, 512], BF16)
            wo = [kb.sb(st, "wo", [128, 32, 512], BF16) for _ in range(2)]
            psy = Ring([kb.ps(st, "psy", [128, 512]) for _ in range(4)])
            P = kb.prog()
            for t in range(4):
                r0 = (tb * 4 + t) * 128
                P.dma("sp", HT[t][:], H[r0:r0 + 128, :], writes=hk(HT[t]))
            for r in range(2):
                for j in range(8):
                    c0 = r * 16 + 2 * j
                    P.dma("sp", oa[:, c0:c0 + 2, :], G_lo[j, r * 256:(r + 1) * 256, :].rearrange("(k p) t -> p k t", p=128)[:, :, tb * 512:(tb + 1) * 512],
                          writes=[(oa.name, r, j)])
                    P.dma("sp", ob[:, c0:c0 + 2, :], G_hi[j, r * 256:(r + 1) * 256, :].rearrange("(k p) t -> p k t", p=128)[:, :, tb * 512:(tb + 1) * 512],
                          writes=[(ob.name, r, j)])
            for c2 in range(2):
                sl_ = slice(c2 * 16, (c2 + 1) * 16)
                P.dve(lambda e, sl_=sl_: e.tensor_scalar(out=oa[:, sl_, :], in0=oa[:, sl_, :], scalar1=SELC[:, 0:1], scalar2=None, op0=ALU.mult),
                      reads=[SELC] + [(oa.name, c2, j) for j in range(8)], writes=[(oa.name, c2)])
                P.dve(lambda e, sl_=sl_: e.scalar_tensor_tensor(out=oa[:, sl_, :], in0=ob[:, sl_, :], scalar=SELC[:, 1:2], in1=oa[:, sl_, :],
                                                              op0=ALU.mult, op1=ALU.add),
                      reads=[SELC] + [(ob.name, c2, j) for j in range(8)], writes=[(oa.name, c2)])
            P.dma("pool", wo[0][:], wv[:, :, 0:512], writes=[wo[0]])
            for db in range(4):
                b = db % 2
                if db + 1 < 4:
                    P.dma("pool", wo[1 - b][:], wv[:, :, (db + 1) * 512:(db + 2) * 512], writes=[wo[1 - b]])
                for t in range(4):
                    py = psy.next()
                    for c in range(32):
                        P.pe(lambda e, py=py, c=c, t=t, b=b: e.matmul(py[:], lhsT=oa[:, c, t * 128:(t + 1) * 128], rhs=wo[b][:, c, :],
                                                                    start=(c == 0), stop=(c == 31)),
                             reads=[wo[b], (oa.name, c // 16)], writes=[py])
                    hs_ = HT[t][:, db * 512:(db + 1) * 512]
                    P.dve(lambda e, hs_=hs_, py=py: e.tensor_tensor(out=hs_, in0=hs_, in1=py[:], op=ALU.add),
                          reads=[(HT[t].name, db)], writes=[py, (HT[t].name, db)])
            for t in range(4):
                r0 = (tb * 4 + t) * 128
                P.dma("sp", H[r0:r0 + 128, :], HT[t][:], reads=hk(HT[t]))
            P.emit()


def gdn_layer(kb, C, G, li, H, S_, ins):
    gdn_norm_gather(kb, C, H, li * 16, S_["XNT_own"], S_["XNT_all"])
    if DEBUG_CUT == 11:
        return
    gdn_inproj(kb, C, G, li, S_["XNT_all"], ins["w_in"][li], S_["QKVZ"])
    if DEBUG_CUT == 12:
        import contextlib
        ov = S_["out"].rearrange("(a b) c -> a (b c)", b=2)
        with contextlib.ExitStack() as st:
            tb_ = kb.sb(st, "dbb", [128, T], BF16)
            tf_ = kb.sb(st, "dbf", [128, T], F32)
            P = kb.prog()
            for i, ct in enumerate((0, 8, 16, 32, 7, 15, 31, 47)):
                P.dma("sp", tb_[:], S_["QKVZ"][ct], writes=[tb_])
                P.dve(lambda e: e.tensor_copy(out=tf_[:], in_=tb_[:]), reads=[tb_], writes=[tf_])
                P.dma("sp", ov[i * 128:(i + 1) * 128, :], tf_[:], reads=[tf_])
            P.dma("sp", ov[1024 - 128:1024, 0:1024], G["BG"][:].rearrange("p a b -> p (a b)"), reads=[])
            P.emit()
        return
    gdn_scan(kb, C, G, li, S_["QKVZ"], S_["OT_lo"], S_["OT_hi"])
    if DEBUG_CUT == 13:
        import contextlib
        with contextlib.ExitStack() as st:
            tb_ = kb.sb(st, "dbb", [128, TO], BF16)
            tf_ = kb.sb(st, "dbf", [128, TO], F32)
            P = kb.prog()
            for i in range(16):
                P.dma("sp", tb_[:], S_["OT_lo"][i * 128:(i + 1) * 128, :], writes=[tb_])
                P.dve(lambda e: e.tensor_copy(out=tf_[:], in_=tb_[:]), reads=[tb_], writes=[tf_])
                P.dma("sp", S_["out"][i * 128:(i + 1) * 128, :], tf_[:], reads=[tf_])
            P.emit()
        return
    allgather(kb, S_["OT_lo"], S_["G_lo"])
    allgather(kb, S_["OT_hi"], S_["G_hi"])
    gdn_outproj(kb, C, G, H, S_["G_lo"], S_["G_hi"], ins["w_out"][li])


def kv_pass(kb, C, H, w_kv, nvcol, KT_own, V_own):
    import contextlib
    wv = w_view(w_kv)
    for tb in range(2):
        with contextlib.ExitStack() as st:
            HT = [kb.sb(st, "ht", [128, D], F32) for _ in range(8)]
            xT = kb.sb(st, "xT", [128, KD, 1024], BF16)
            with contextlib.ExitStack() as st1:
                P = kb.prog()
                for t in range(8):
                    r0 = (tb * 8 + t) * 128
                    P.dma("sp", HT[t][:], H[r0:r0 + 128, :], writes=hk(HT[t]))
                norm_T(kb, P, st1, C, HT, nvcol, xT, tr_ring(kb, st1))
                P.emit()
            with contextlib.ExitStack() as st2:
                P = kb.prog()
                wb = [kb.sb(st2, "wkv", [128, KD, 512], BF16) for _ in range(2)]
                kst = Ring([kb.sb(st2, "kst", [128, 1024], BF16) for _ in range(2)])
                vst = Ring([kb.sb(st2, "vst", [128, 512], BF16) for _ in range(3)])
                psa = Ring([kb.ps(st2, "psa", [128, 512]) for _ in range(4)])
                P.dma("pool", wb[0][:], wv[:, :, 0:512], writes=[wb[0]])
                for blk in range(8):
                    b = blk % 2
                    if blk + 1 < 8:
                        P.dma("pool", wb[1 - b][:], wv[:, :, (blk + 1) * 512:(blk + 2) * 512], writes=[wb[1 - b]])
                    if blk < 4:
                        for ci in range(4):
                            hc = blk * 4 + ci
                            ks = kst.next()
                            for tg in range(2):
                                pa = psa.next()
                                for k in range(KD):
                                    P.pe(lambda e, pa=pa, k=k, ci=ci, tg=tg, b=b: e.matmul(pa[:], lhsT=wb[b][:, k, ci * 128:(ci + 1) * 128],
                                                                                         rhs=xT[:, k, tg * 512:(tg + 1) * 512],
                                                                                         start=(k == 0), stop=(k == KD - 1)),
                                         reads=[wb[b], (xT.name, k, tg)], writes=[pa])
                                P.act(lambda e, pa=pa, ks=ks, tg=tg: e.activation(out=ks[:, tg * 512:(tg + 1) * 512], in_=pa[:], func=AF.Copy),
                                      writes=[pa, (ks.name, tg)])
                            P.dma("sp", KT_own[hc * 128:(hc + 1) * 128, tb * 1024:(tb + 1) * 1024], ks[:], reads=[(ks.name, 0), (ks.name, 1)])
                    else:
                        db = blk - 4
                        for t in range(8):
                            pa = psa.next()
                            vs = vst.next()
                            for k in range(KD):
                                P.pe(lambda e, pa=pa, k=k, t=t, b=b: e.matmul(pa[:], lhsT=xT[:, k, t * 128:(t + 1) * 128], rhs=wb[b][:, k, :],
                                                                            start=(k == 0), stop=(k == KD - 1)),
                                     reads=[wb[b], (xT.name, k, t // 4)], writes=[pa])
                            P.act(lambda e, pa=pa, vs=vs: e.activation(out=vs[:], in_=pa[:], func=AF.Copy), writes=[pa, vs])
                            r0 = (tb * 8 + t) * 128
                            P.dma("sp", V_own[r0:r0 + 128, db * 512:(db + 1) * 512], vs[:], reads=[vs])
                P.emit()


def attn_layer(kb, C, A, li, H, S_, w_q, w_o, nvcol):
    import contextlib, math
    j = li - N_A
    lam_init = 0.8 - 0.6 * math.exp(-0.3 * li)
    QT, KT_all, V_all = S_["QT"], S_["KT_all"], S_["V_all"]
    identb, eps = C["identb"], C["eps"]
    P = kb.prog()
    LAMV, nlam, SW, SUBW = A["LAMV"], A["nlam"], A["SW"], A["SUBW"]
    l2t, lsum = A["l2t"], A["lsum"]
    P.dve(lambda e: e.memset(lsum[:], 0.0), writes=[lsum])
    for a in range(2):
        P.dve(lambda e, a=a: e.tensor_tensor(out=l2t[:], in0=LAMV[:, j, 2 * a, :], in1=LAMV[:, j, 2 * a + 1, :], op=ALU.mult),
              reads=[LAMV], writes=[l2t])
        P.act(lambda e, a=a: e.activation(out=l2t[:], in_=l2t[:], func=AF.Copy, accum_out=lsum[:, a:a + 1]), reads=[l2t, lsum], writes=[l2t, (lsum.name, a)])
    P.act(lambda e: e.activation(out=lsum[:], in_=lsum[:], func=AF.Exp), reads=[lsum, (lsum.name, 0), (lsum.name, 1)], writes=[lsum])
    P.dve(lambda e: e.tensor_tensor(out=nlam[:], in0=lsum[:, 1:2], in1=lsum[:, 0:1], op=ALU.subtract), reads=[lsum], writes=[nlam])
    P.dve(lambda e: e.tensor_scalar(out=nlam[:], in0=nlam[:], scalar1=-lam_init, scalar2=None, op0=ALU.add), reads=[nlam], writes=[nlam])
    P.dve(lambda e: e.tensor_scalar(out=SW[:], in0=SUBW[:, j, :], scalar1=1.0 - lam_init, scalar2=None, op0=ALU.mult), reads=[SUBW], writes=[SW])
    P.emit()
    wqv = w_view(w_q)
    for tb in range(2):
        with contextlib.ExitStack() as st:
            HT = [kb.sb(st, "ht", [128, D], F32) for _ in range(8)]
            xT = kb.sb(st, "xT", [128, KD, 1024], BF16)
            with contextlib.ExitStack() as st1:
                P = kb.prog()
                for t in range(8):
                    r0 = (tb * 8 + t) * 128
                    P.dma("sp", HT[t][:], H[r0:r0 + 128, :], writes=hk(HT[t]))
                norm_T(kb, P, st1, C, HT, nvcol, xT, tr_ring(kb, st1))
                P.emit()
            with contextlib.ExitStack() as st2:
                P = kb.prog()
                wb = [kb.sb(st2, "wq", [128, KD, 512], BF16) for _ in range(2)]
                qst = Ring([kb.sb(st2, "qst", [128, 1024], BF16) for _ in range(2)])
                psa = Ring([kb.ps(st2, "psa", [128, 512]) for _ in range(4)])
                P.dma("pool", wb[0][:], wqv[:, :, 0:512], writes=[wb[0]])
                for blk in range(4):
                    b = blk % 2
                    if blk + 1 < 4:
                        P.dma("pool", wb[1 - b][:], wqv[:, :, (blk + 1) * 512:(blk + 2) * 512], writes=[wb[1 - b]])
                    for ci in range(4):
                        hm = blk * 4 + ci
                        qs = qst.next()
                        for tg in range(2):
                            pa = psa.next()
                            for k in range(KD):
                                P.pe(lambda e, pa=pa, k=k, ci=ci, tg=tg, b=b: e.matmul(pa[:], lhsT=wb[b][:, k, ci * 128:(ci + 1) * 128],
                                                                                     rhs=xT[:, k, tg * 512:(tg + 1) * 512],
                                                                                     start=(k == 0), stop=(k == KD - 1)),
                                     reads=[wb[b], (xT.name, k, tg)], writes=[pa])
                            P.act(lambda e, pa=pa, qs=qs, tg=tg: e.activation(out=qs[:, tg * 512:(tg + 1) * 512], in_=pa[:], func=AF.Copy,
                                                                            scale=128.0 ** -0.5),
                                  writes=[pa, (qs.name, tg)])
                        P.dma("sp", QT[hm][:, tb * 1024:(tb + 1) * 1024], qs[:], reads=[(qs.name, 0), (qs.name, 1)])
                P.emit()
    with contextlib.ExitStack() as sto:
        OATT = kb.sb(sto, "OATT", [128, NT, D], BF16)
        with contextlib.ExitStack() as st:
            KTh = [kb.sb(st, "KTh", [128, 2, T], BF16) for _ in range(2)]
            Vh = [kb.sb(st, "Vh", [128, 32, 257], BF16) for _ in range(2)]
            QTh = [kb.sb(st, "QTh", [128, 2, TO], BF16) for _ in range(2)]
            pT = Ring([kb.sb(st, "pT", [128, 128], BF16) for _ in range(6)])
            o1 = Ring([kb.sb(st, "o1", [128, 256], F32) for _ in range(2)])
            oo = Ring([kb.sb(st, "oo", [128, 256], F32) for _ in range(2)])
            jk = kb.sb(st, "jk", [128, 256], BF16)
            sm = Ring([kb.sb(st, "sm", [128, 8], F32) for _ in range(4)])
            sring = Ring([kb.ps(st, "pss", [128, 512]) for _ in range(3)])
            av = [[kb.ps(st, "av", [128, 512]) for _ in range(2)] for _ in range(2)]
            BT, MA, MB = A["BT"], A["MA"], A["MB"]
            P = kb.prog()
            for b in range(2):
                P.dve(lambda e, b=b: e.memset(Vh[b][:, :, 256:257], 1.0), writes=[(Vh[b].name, "one")])
            P.emit()
            for h in range(8):
                b = h % 2
                P = kb.prog()
                for m in range(2):
                    for r in range(2):
                        P.dma("sp", KTh[b][:, m, r * TO:(r + 1) * TO], KT_all[h, r * 256 + m * 128:r * 256 + (m + 1) * 128, :], writes=[(KTh[b].name, m, r)])
                    P.dma("sp", QTh[b][:, m, :], QT[2 * h + m], writes=[(QTh[b].name, m)])
                for r in range(2):
                    for jj in range(8):
                        kt0 = r * 16 + 2 * jj
                        P.dma("sp", Vh[b][:, kt0:kt0 + 2, 0:256],
                              V_all[jj, r * 256:(r + 1) * 256, h * 256:(h + 1) * 256].rearrange("(a p) e -> p a e", p=128),
                              writes=[(Vh[b].name, kt0 // 2)])

                def qblock(qb, h=h, b=b, P=P):
                    i0 = 2 * qb
                    nkt = 18 + 2 * qb
                    for kt in range(nkt):
                        for m in range(2):
                            ps = sring.next()
                            P.pe(lambda e, ps=ps, m=m, kt=kt: e.matmul(ps[:, 0:256], lhsT=KTh[b][:, m, kt * 128:(kt + 1) * 128],
                                                                     rhs=QTh[b][:, m, i0 * 128:(i0 + 2) * 128], start=True, stop=True),
                                 reads=[(KTh[b].name, m, kt // 16), (QTh[b].name, m)], writes=[ps])
                            for jj in range(2):
                                i = i0 + jj
                                if kt > 16 + i:
                                    continue
                                p_ = pT.next()
                                dd = i - kt + 16
                                P.act(lambda e, ps=ps, p_=p_, jj=jj, dd=dd: e.activation(out=p_[:], in_=ps[:, jj * 128:(jj + 1) * 128], func=AF.Exp,
                                                                                       bias=BT[:, h, dd:dd + 1], scale=1.0),
                                      reads=[BT], writes=[ps, p_])
                                if kt == i:
                                    P.dve(lambda e, p_=p_: e.tensor_tensor(out=p_[:], in0=p_[:], in1=MA[:], op=ALU.mult), reads=[MA], writes=[p_])
                                if kt == 16 + i:
                                    P.dve(lambda e, p_=p_: e.tensor_tensor(out=p_[:], in0=p_[:], in1=MB[:], op=ALU.mult), reads=[MB], writes=[p_])
                                acc = av[jj][m]
                                P.pe(lambda e, acc=acc, p_=p_, kt=kt, i=i: e.matmul(acc[:, 0:257], lhsT=p_[:], rhs=Vh[b][:, kt, :],
                                                                                  start=(kt == 0), stop=(kt == 16 + i)),
                                     reads=[p_, (Vh[b].name, kt // 2), (Vh[b].name, "one")], writes=[acc])
                    for jj in range(2):
                        i = i0 + jj
                        s_ = sm.next()
                        a0, a1 = av[jj][0], av[jj][1]
                        o1_, oo_ = o1.next(), oo.next()
                        P.dve(lambda e, s_=s_, a0=a0: e.reciprocal(out=s_[:, 0:1], in_=a0[:, 256:257]), writes=[a0, (s_.name, 0)])
                        P.dve(lambda e, s_=s_, a1=a1: e.reciprocal(out=s_[:, 1:2], in_=a1[:, 256:257]), writes=[a1, (s_.name, 1)])
                        P.dve(lambda e, s_=s_: e.tensor_tensor(out=s_[:, 2:3], in0=s_[:, 1:2], in1=nlam[:], op=ALU.mult),
                              reads=[(s_.name, 1), nlam], writes=[(s_.name, 2)])
                        P.dve(lambda e, s_=s_, a0=a0, o1_=o1_: e.tensor_scalar(out=o1_[:], in0=a0[:, 0:256], scalar1=s_[:, 0:1], scalar2=None, op0=ALU.mult),
                              reads=[(s_.name, 0)], writes=[a0, o1_])
                        P.dve(lambda e, s_=s_, a1=a1, o1_=o1_, oo_=oo_: e.scalar_tensor_tensor(out=oo_[:], in0=a1[:, 0:256], scalar=s_[:, 2:3], in1=o1_[:],
                                                                                             op0=ALU.mult, op1=ALU.add),
                              reads=[(s_.name, 2), o1_], writes=[a1, oo_])
                        P.dve(lambda e, s_=s_: e.memset(s_[:, 3:4], 0.0), writes=[(s_.name, 3)])
                        P.act(lambda e, s_=s_, oo_=oo_: e.activation(out=jk[:], in_=oo_[:], func=AF.Square, accum_out=s_[:, 3:4]),
                              reads=[oo_, (s_.name, 3)], writes=[jk, (s_.name, 4)])
                        P.act(lambda e, s_=s_: e.activation(out=s_[:, 5:6], in_=s_[:, 3:4], func=AF.Sqrt, bias=eps[:, 0:1], scale=1.0 / 256),
                              reads=[(s_.name, 4), eps], writes=[(s_.name, 5)])
                        P.dve(lambda e, s_=s_: e.reciprocal(out=s_[:, 6:7], in_=s_[:, 5:6]), reads=[(s_.name, 5)], writes=[(s_.name, 6)])
                        P.dve(lambda e, s_=s_, oo_=oo_, i=i: e.scalar_tensor_tensor(out=OATT[:, i, h * 256:(h + 1) * 256], in0=oo_[:], scalar=s_[:, 6:7],
                                                                                  in1=SW[:], op0=ALU.mult, op1=ALU.mult),
                              reads=[oo_, (s_.name, 6), SW], writes=[(OATT.name, i, h)])

                for qb in range(8):
                    qblock(qb)
                P.emit()
        wov = w_view(w_o)
        for tb in range(4):
            with contextlib.ExitStack() as st:
                HT = [kb.sb(st, "ht", [128, D], F32) for _ in range(4)]
                oT = kb.sb(st, "oT", [128, KD, 512], BF16)
                wo = [kb.sb(st, "wo", [128, KD, 512], BF16) for _ in range(2)]
                trr = Ring([kb.ps(st, "ptr3", [128, 8, 128], BF16) for _ in range(2)])
                psy = Ring([kb.ps(st, "psy", [128, 512]) for _ in range(4)])
                P = kb.prog()
                for t in range(4):
                    r0 = (tb * 4 + t) * 128
                    P.dma("sp", HT[t][:], H[r0:r0 + 128, :], writes=hk(HT[t]))
                for t in range(4):
                    i = tb * 4 + t
                    for c2 in range(2):
                        pt = trr.next()
                        for cc in range(8):
                            c = c2 * 8 + cc
                            P.pe(lambda e, pt=pt, cc=cc, c=c, i=i: e.transpose(out=pt[:, cc, :], in_=OATT[:, i, c * 128:(c + 1) * 128],
                                                                             identity=identb[:]),
                                 reads=[(OATT.name, i, c // 2), identb], writes=[pt])
                        P.act(lambda e, pt=pt, c2=c2, t=t: e.activation(out=oT[:, c2 * 8:(c2 + 1) * 8, t * 128:(t + 1) * 128],
                                                                      in_=pt[:], func=AF.Copy),
                              writes=[pt, (oT.name, c2, t)])
                P.dma("pool", wo[0][:], wov[:, :, 0:512], writes=[wo[0]])
                for db in range(4):
                    b = db % 2
                    if db + 1 < 4:
                        P.dma("pool", wo[1 - b][:], wov[:, :, (db + 1) * 512:(db + 2) * 512], writes=[wo[1 - b]])
                    for t in range(4):
                        py = psy.next()
                        for c in range(KD):
                            P.pe(lambda e, py=py, c=c, t=t, b=b: e.matmul(py[:], lhsT=oT[:, c, t * 128:(t + 1) * 128], rhs=wo[b][:, c, :],
                                                                        start=(c == 0), stop=(c == KD - 1)),
                                 reads=[wo[b], (oT.name, c // 8, t)], writes=[py])
                        hs_ = HT[t][:, db * 512:(db + 1) * 512]
                        P.dve(lambda e, hs_=hs_, py=py: e.tensor_tensor(out=hs_, in0=hs_, in1=py[:], op=ALU.add),
                              reads=[(HT[t].name, db)], writes=[py, (HT[t].name, db)])
                for t in range(4):
                    r0 = (tb * 4 + t) * 128
                    P.dma("sp", H[r0:r0 + 128, :], HT[t][:], reads=hk(HT[t]))
                P.emit()
```

```python
import numpy as np
import concourse.bass as bass
import concourse.mybir as mybir
from concourse.bass_utils import run_bass_kernel_spmd

F32 = mybir.dt.float32
BF16 = mybir.dt.bfloat16
AF = mybir.ActivationFunctionType
ALU = mybir.AluOpType
AX = mybir.AxisListType

ENGS = ("pe", "act", "dve", "pool", "sp")
DMA_RING = {"sp": 12, "pool": 12, "act": 6, "cc": 4}


class Op:
    __slots__ = ("eng", "fn", "reads", "writes", "dma", "waits", "signal", "sem", "count", "idx", "inc")

    def __init__(self, eng, fn, reads, writes, dma, inc=16):
        self.eng, self.fn, self.reads, self.writes, self.dma = eng, fn, reads, writes, dma
        self.inc = inc if dma else 1
        self.waits = []
        self.signal = False
        self.sem = None
        self.count = 0


def _key(r):
    return r if isinstance(r, (str, tuple, int)) else r.name


class Prog:
    def __init__(self, kb, same_engine_sync=True):
        self.kb = kb
        self.nc = kb.nc
        self.ops = []
        self.same_engine_sync = same_engine_sync

    def add(self, eng, fn, reads=(), writes=(), dma=False, inc=16):
        op = Op(eng, fn, tuple(_key(r) for r in reads), tuple(_key(w) for w in writes), dma, inc)
        op.idx = len(self.ops)
        self.ops.append(op)
        return op

    def pe(self, fn, reads=(), writes=()):
        return self.add("pe", fn, reads, writes)

    def act(self, fn, reads=(), writes=()):
        return self.add("act", fn, reads, writes)

    def dve(self, fn, reads=(), writes=()):
        return self.add("dve", fn, reads, writes)

    def pool(self, fn, reads=(), writes=()):
        return self.add("pool", fn, reads, writes)

    def dma(self, q, out, in_, reads=(), writes=(), **kw):
        return self.add(q, lambda e: e.dma_start(out=out, in_=in_, **kw), reads, writes, dma=True)

    def _schedule(self):
        for k_, op_ in enumerate(self.ops):
            op_.idx = k_
        last_write = {}
        readers = {}
        deps_of = []
        for op in self.ops:
            deps = set()
            for r in op.reads:
                lw = last_write.get(r)
                if lw is not None:
                    deps.add(lw)
            for w in op.writes:
                lw = last_write.get(w)
                if lw is not None:
                    deps.add(lw)
                for rd in readers.get(w, ()):
                    deps.add(rd)
            for r in op.reads:
                readers.setdefault(r, []).append(op.idx)
            for w in op.writes:
                last_write[w] = op.idx
                readers[w] = []
            deps.discard(op.idx)
            deps_of.append(deps)

        for op in self.ops:
            for j in deps_of[op.idx]:
                d = self.ops[j]
                if d.dma:
                    continue
                if d.eng == op.eng and not op.dma:
                    if op.eng == "pe" or not self.same_engine_sync:
                        continue
                d.signal = True

        eng_count = {e: 0 for e in ENGS}
        dma_n = {q: 0 for q in DMA_RING}
        persist = getattr(self.kb, "persist", None)
        if persist is None:
            persist = self.kb.persist = {}
        dma_slot_count = dict(persist)
        dma_prev = {}
        for op in self.ops:
            if op.dma:
                q = op.eng if op.inc != 1 else "cc"
                slot = dma_n[q] % DMA_RING[q]
                dma_n[q] += 1
                key = ("dma", q, slot)
                prev = dma_prev.get(key)
                if prev is not None:
                    deps_of[op.idx].add(prev)
                dma_slot_count[key] = dma_slot_count.get(key, 0) + op.inc
                op.sem, op.count, op.signal = key, dma_slot_count[key], True
                dma_prev[key] = op.idx
            elif op.signal:
                eng_count[op.eng] += 1
                op.sem, op.count = ("eng", op.eng), eng_count[op.eng]
        assert max(list(eng_count.values()) + list(dma_slot_count.values()) + [0]) < 60000, eng_count

        known = {e: {} for e in ENGS}
        snap = [None] * len(self.ops)
        for op in self.ops:
            kn = known[op.eng]
            need = {}
            for j in deps_of[op.idx]:
                d = self.ops[j]
                if not d.signal:
                    continue
                if d.eng == op.eng and not d.dma and not op.dma:
                    if op.eng == "pe" or not self.same_engine_sync:
                        continue
                if kn.get(d.sem, 0) >= d.count:
                    continue
                if need.get(d.sem, (0, None))[0] < d.count:
                    need[d.sem] = (d.count, j)
            for sem, (cnt, j) in need.items():
                op.waits.append((sem, cnt))
                if kn.get(sem, 0) < cnt:
                    kn[sem] = cnt
                sj = snap[j]
                if sj:
                    for s2, c2 in sj.items():
                        if kn.get(s2, 0) < c2:
                            kn[s2] = c2
            snap[op.idx] = dict(kn) if op.signal else None
        self.final_dma = {k: v for k, v in dma_slot_count.items() if v != persist.get(k, 0) or k[1] not in ("pool", "cc")}
        for k, v in dma_slot_count.items():
            if k[1] in ("pool", "cc"):
                persist[k] = v
        return sorted({op.sem for op in self.ops if op.signal}, key=str)

    def emit(self):
        nc = self.nc
        if not self.ops:
            return
        used = self._schedule()
        sem = self.kb.sem
        for key in used:
            if key not in sem:
                sem[key] = nc.alloc_semaphore("s_" + "_".join(str(k) for k in key))
        with nc.Block() as cb:
            def clr(e):
                for key in used:
                    if not (key[0] == "dma" and key[1] in ("pool", "cc")):
                        e.sem_clear(sem[key])
            cb.gpsimd(clr)
        by_eng = {e: [] for e in ENGS}
        for op in self.ops:
            by_eng[op.eng].append(op)
        final_dma = self.final_dma
        with nc.Block() as block:
            def section(ename):
                def body(e):
                    for op in by_eng[ename]:
                        for s, c in op.waits:
                            e.wait_ge(sem[s], c)
                        ins = op.fn(e)
                        if op.signal:
                            ins.then_inc(sem[op.sem], op.inc)
                    if ename == "sp":
                        for key, cnt in final_dma.items():
                            e.wait_ge(sem[key], cnt)
                return body

            block.tensor(section("pe"))
            block.scalar(section("act"))
            block.vector(section("dve"))
            block.gpsimd(section("pool"))
            block.sync(section("sp"))
        self.ops = []


D = 2048
T = 4096
TO = 2048
NT = TO // 128
KD = D // 128
FF = 8192
DEPTH = 4
N_A = 2
EPS = 1e-6
HL = 16
HQ = 8
NCOLS_IN = 6176
NVC = 13 * 16 + 2
BIG = 30000.0
import os
DEBUG_CUT = int(os.environ.get('MK_DEBUG_CUT', '0'))
SUBCUT = int(os.environ.get('MK_SUBCUT', '9'))


class KB:
    def __init__(self):
        self.nc = bass.Bass("TRN2", target_bir_lowering=False)
        self.sem = {}
        self.uid = 0
        self.rr = {}

    def din(self, name, shape, dt=F32):
        return self.nc.dram_tensor(name, list(shape), dt, kind="ExternalInput").ap()

    def dscr(self, name, shape, dt):
        return self.nc.dram_tensor(name, list(shape), dt, kind="Internal").ap()

    def sb(self, st, name, shape, dt):
        self.uid += 1
        return st.enter_context(self.nc.sbuf_tensor(f"{name}_{self.uid}", list(shape), dt))

    def ps(self, st, name, shape, dt=F32):
        self.uid += 1
        return st.enter_context(self.nc.psum_tensor(f"{name}_{self.uid}", list(shape), dt))

    def prog(self):
        return Prog(self)


PAR = [0]


class PRing:
    def __init__(self, a, b):
        self.r = (a, b)

    def next(self):
        return self.r[PAR[0]].next()


class Ring:
    def __init__(self, bufs):
        self.bufs = bufs
        self.i = 0

    def next(self):
        b = self.bufs[self.i % len(self.bufs)]
        self.i += 1
        return b


def w_view(w2d):
    return w2d.rearrange("(k p) n -> p k n", p=128)


def hk(ht):
    return [(ht.name, d) for d in range(4)]


def norm_T(kb, P, st, C, hts, nvcol, xT, ps_tr):
    n = len(hts)
    ss = kb.sb(st, "ss", [128, n], F32)
    sd = kb.sb(st, "sd", [128, n], F32)
    rstd = kb.sb(st, "rstd", [128, n], F32)
    junk = kb.sb(st, "junk", [128, D], BF16)
    xs = [kb.sb(st, "xs", [128, D], BF16) for _ in range(4)]
    NV, identb = C["NV"], C["identb"]
    P.dve(lambda e: e.memset(ss[:], 0.0), writes=[ss])
    for t, h in enumerate(hts):
        P.act(lambda e, h=h, t=t: e.activation(out=junk[:], in_=h[:], func=AF.Square, accum_out=ss[:, t:t + 1]),
              reads=hk(h) + [ss], writes=[junk, (ss.name, t)])
    P.act(lambda e: e.activation(out=sd[:], in_=ss[:], func=AF.Sqrt, bias=C["eps"][:, 0:1], scale=1.0 / D),
          reads=[ss] + [(ss.name, t) for t in range(n)], writes=[sd])
    P.dve(lambda e: e.reciprocal(out=rstd[:], in_=sd[:]), reads=[sd], writes=[rstd])
    cnt = 0
    for g in range(n // 4):
        for j in range(4):
            t = g * 4 + j
            P.act(lambda e, t=t, j=j: e.activation(out=xs[j][:], in_=hts[t][:], func=AF.Copy, scale=rstd[:, t:t + 1]),
                  reads=hk(hts[t]) + [rstd], writes=[xs[j]])
        for k2 in range(KD // 2):
            pt = ps_tr.next()
            for kk in range(2):
                k = k2 * 2 + kk
                for j in range(4):
                    c0 = kk * 512 + j * 128
                    P.pe(lambda e, pt=pt, j=j, k=k, c0=c0: e.transpose(out=pt[:, c0:c0 + 128], in_=xs[j][:, k * 128:(k + 1) * 128],
                                                                     identity=identb[:]),
                         reads=[xs[j], identb], writes=[pt])
            for kk in range(2):
                k = k2 * 2 + kk
                dst = xT[:, k, g * 512:(g + 1) * 512]
                src = pt[:, kk * 512:(kk + 1) * 512]
                if cnt % 2 == 0:
                    P.dve(lambda e, dst=dst, src=src, k=k: e.tensor_scalar(out=dst, in0=src, scalar1=NV[:, nvcol + k:nvcol + k + 1],
                                                                         scalar2=None, op0=ALU.mult),
                          reads=[NV], writes=[pt, (xT.name, k, g)])
                else:
                    P.act(lambda e, dst=dst, src=src, k=k: e.activation(out=dst, in_=src, func=AF.Copy,
                                                                      scale=NV[:, nvcol + k:nvcol + k + 1]),
                          reads=[NV], writes=[pt, (xT.name, k, g)])
            cnt += 1


def tr_ring(kb, st, nbank=2):
    return Ring([kb.ps(st, "ptr", [128, 1024], BF16) for _ in range(nbank)])


def xT_keys(xT, g):
    return [(xT.name, k, g) for k in range(KD)]


def mlp_pass(kb, C, H, w1, w2, nvcol):
    import contextlib
    for tb in range(int(os.environ.get("MK_NTB", "2"))):
        with contextlib.ExitStack() as st:
            HT = [kb.sb(st, "ht", [128, D], F32) for _ in range(8)]
            hnT = kb.sb(st, "hnT", [128, KD, 1024], BF16)
            with contextlib.ExitStack() as st1:
                P = kb.prog()
                for t in range(8):
                    r0 = (tb * 8 + t) * 128
                    P.dma("sp", HT[t][:], H[r0:r0 + 128, :], writes=hk(HT[t]))
                norm_T(kb, P, st1, C, HT, nvcol, hnT, tr_ring(kb, st1))
                P.emit()
            if DEBUG_CUT == 1 or (tb == 1 and os.environ.get("MK_SKIPB2")):
                continue
            with contextlib.ExitStack() as st2:
                P = kb.prog()
                gT = [kb.sb(st2, "gT", [128, 4, 1024], BF16) for _ in range(2)]
                w1b = [kb.sb(st2, "w1b", [128, KD, 512], BF16) for _ in range(2)]
                w2b = [kb.sb(st2, "w2b", [128, 4, D], BF16) for _ in range(2)]
                rt = Ring([kb.sb(st2, "rt", [128, 512], F32) for _ in range(2)])
                psu = Ring([kb.ps(st2, "psu", [128, 512]) for _ in range(3)])
                psy = Ring([kb.ps(st2, "psy", [128, 512]) for _ in range(3)])
                w1v = w_view(w1)

                def loadw(fb):
                    b = fb % 2
                    P.dma("pool", w1b[b][:], w1v[:, :, fb * 512:(fb + 1) * 512], writes=[w1b[b]])
                    if DEBUG_CUT != 2:
                        P.dma("pool", w2b[b][:], w2[fb * 512:(fb + 1) * 512, :].rearrange("(c p) n -> p c n", p=128), writes=[w2b[b]])

                def stage1(fb):
                    b = fb % 2
                    for fc in range(4):
                        for tg in range(2):
                            pu = psu.next()
                            for k in range(KD if SUBCUT >= 0 else 0):
                                P.pe(lambda e, pu=pu, k=k, fc=fc, tg=tg: e.matmul(pu[:], lhsT=w1b[b][:, k, fc * 128:(fc + 1) * 128],
                                                                                 rhs=hnT[:, k, tg * 512:(tg + 1) * 512],
                                                                                 start=(k == 0), stop=(k == KD - 1)),
                                     reads=[w1b[b], (hnT.name, k, tg)], writes=[pu])
                            r = rt.next()
                            if SUBCUT >= 1:
                                P.act(lambda e, pu=pu, r=r: e.activation(out=r[:], in_=pu[:], func=AF.Relu), reads=[], writes=[pu, r])
                            if SUBCUT >= 2:
                                P.act(lambda e, r=r, fc=fc, tg=tg: e.activation(out=gT[b][:, fc, tg * 512:(tg + 1) * 512], in_=r[:], func=AF.Square),
                                      reads=[r], writes=[(gT[b].name, fc, tg)])

                def stage2(fb):
                    b = fb % 2
                    if DEBUG_CUT == 2:
                        return
                    for t in range(8):
                        for db in range(4):
                            py = psy.next()
                            for fc in range(4):
                                P.pe(lambda e, py=py, fc=fc, t=t, db=db: e.matmul(py[:], lhsT=gT[b][:, fc, t * 128:(t + 1) * 128],
                                                                                 rhs=w2b[b][:, fc, db * 512:(db + 1) * 512],
                                                                                 start=(fc == 0), stop=(fc == 3)),
                                     reads=[w2b[b], (gT[b].name, fc, t // 4)], writes=[py])
                            hs = HT[t][:, db * 512:(db + 1) * 512]
                            P.dve(lambda e, hs=hs, py=py: e.tensor_tensor(out=hs, in0=hs, in1=py[:], op=ALU.add),
                                  reads=[(HT[t].name, db)], writes=[py, (HT[t].name, db)])

                NFB = int(os.environ.get("MK_NFB", "16"))
                loadw(0)
                stage1(0)
                for fb in range(NFB):
                    if fb + 1 < NFB:
                        loadw(fb + 1)
                        stage1(fb + 1)
                    stage2(fb)
                for t in range(8):
                    r0 = (tb * 8 + t) * 128
                    P.dma("sp", H[r0:r0 + 128, :], HT[t][:], reads=hk(HT[t]))
                P.emit()


def ple_pass(kb, C, H, p_l, wple, wgate, nvcol):
    import contextlib
    for tb in range(2):
        with contextlib.ExitStack() as st:
            HT = [kb.sb(st, "ht", [128, D], F32) for _ in range(8)]
            hpT = kb.sb(st, "hpT", [128, KD, 1024], BF16)
            pT = kb.sb(st, "pT", [128, 2, 1024], BF16)
            with contextlib.ExitStack() as st1:
                P = kb.prog()
                for t in range(8):
                    r0 = (tb * 8 + t) * 128
                    P.dma("sp", HT[t][:], H[r0:r0 + 128, :], writes=hk(HT[t]))
                trr = tr_ring(kb, st1)
                norm_T(kb, P, st1, C, HT, nvcol, hpT, trr)
                pf = Ring([kb.sb(st1, "pf", [128, 256], F32) for _ in range(2)])
                pb = Ring([kb.sb(st1, "pb", [128, 256], BF16) for _ in range(2)])
                for t in range(8):
                    r0 = (tb * 8 + t) * 128
                    f, b_ = pf.next(), pb.next()
                    P.dma("sp", f[:], p_l[r0:r0 + 128, :], writes=[f])
                    P.act(lambda e, f=f, b_=b_: e.activation(out=b_[:], in_=f[:], func=AF.Copy), reads=[f], writes=[b_])
                    pt = trr.next()
                    for c in range(2):
                        P.pe(lambda e, pt=pt, c=c, b_=b_: e.transpose(out=pt[:, c * 128:(c + 1) * 128],
                                                                    in_=b_[:, c * 128:(c + 1) * 128], identity=C["identb"][:]),
                             reads=[b_, C["identb"]], writes=[pt])
                    for c in range(2):
                        P.dve(lambda e, pt=pt, c=c, t=t: e.tensor_copy(out=pT[:, c, t * 128:(t + 1) * 128],
                                                                     in_=pt[:, c * 128:(c + 1) * 128]),
                              reads=[], writes=[pt, (pT.name, c, t)])
                P.emit()
            with contextlib.ExitStack() as st2:
                P = kb.prog()
                wg = [kb.sb(st2, "wg", [128, KD, 512], BF16) for _ in range(2)]
                wp = kb.sb(st2, "wp", [128, 2, D], BF16)
                gs = Ring([kb.sb(st2, "gs", [128, 512], F32) for _ in range(2)])
                tm = Ring([kb.sb(st2, "tm", [128, 512], F32) for _ in range(2)])
                psg = Ring([kb.ps(st2, "psg", [128, 512]) for _ in range(3)])
                psp = Ring([kb.ps(st2, "psp", [128, 512]) for _ in range(3)])
                P.dma("pool", wp[:], wple.rearrange("(c p) n -> p c n", p=128), writes=[wp])
                wgv = w_view(wgate)
                P.dma("pool", wg[0][:], wgv[:, :, 0:512], writes=[wg[0]])
                for db in range(4):
                    b = db % 2
                    if db + 1 < 4:
                        P.dma("pool", wg[1 - b][:], wgv[:, :, (db + 1) * 512:(db + 2) * 512], writes=[wg[1 - b]])
                    for t in range(8):
                        pg, pp = psg.next(), psp.next()
                        for k in range(KD):
                            P.pe(lambda e, pg=pg, k=k, t=t, b=b: e.matmul(pg[:], lhsT=hpT[:, k, t * 128:(t + 1) * 128], rhs=wg[b][:, k, :],
                                                                        start=(k == 0), stop=(k == KD - 1)),
                                 reads=[wg[b], (hpT.name, k, t // 4)], writes=[pg])
                        for c in range(2):
                            P.pe(lambda e, pp=pp, c=c, t=t, db=db: e.matmul(pp[:], lhsT=pT[:, c, t * 128:(t + 1) * 128],
                                                                          rhs=wp[:, c, db * 512:(db + 1) * 512], start=(c == 0), stop=(c == 1)),
                                 reads=[wp, (pT.name, c, t)], writes=[pp])
                        g_, m_ = gs.next(), tm.next()
                        P.act(lambda e, g_=g_, pg=pg: e.activation(out=g_[:], in_=pg[:], func=AF.Sigmoid), reads=[], writes=[pg, g_])
                        P.dve(lambda e, m_=m_, pp=pp, g_=g_: e.tensor_tensor(out=m_[:], in0=pp[:], in1=g_[:], op=ALU.mult),
                              reads=[g_], writes=[pp, m_])
                        hs = HT[t][:, db * 512:(db + 1) * 512]
                        P.dve(lambda e, hs=hs, m_=m_: e.tensor_tensor(out=hs, in0=hs, in1=m_[:], op=ALU.add),
                               reads=[m_, (HT[t].name, db)], writes=[(HT[t].name, db)])
                for t in range(8):
                    r0 = (tb * 8 + t) * 128
                    P.dma("sp", H[r0:r0 + 128, :], HT[t][:], reads=hk(HT[t]))
                P.emit()


def final_pass(kb, C, H, out, fnorm):
    import contextlib
    with contextlib.ExitStack() as st:
        P = kb.prog()
        FN = kb.sb(st, "FN", [128, D], F32)
        P.dma("sp", FN[:], fnorm, writes=[FN])
        hts = Ring([kb.sb(st, "ht", [128, D], F32) for _ in range(3)])
        ots = Ring([kb.sb(st, "ot", [128, D], F32) for _ in range(2)])
        junk = kb.sb(st, "junk", [128, D], BF16)
        ss = kb.sb(st, "ss", [128, NT], F32)
        sd = kb.sb(st, "sd", [128, NT], F32)
        rs = kb.sb(st, "rs", [128, NT], F32)
        P.dve(lambda e: e.memset(ss[:], 0.0), writes=[ss])
        for t in range(NT):
            h, o = hts.next(), ots.next()
            P.dma("sp", h[:], H[t * 128:(t + 1) * 128, :], writes=[h])
            P.act(lambda e, h=h, t=t: e.activation(out=junk[:], in_=h[:], func=AF.Square, accum_out=ss[:, t:t + 1]),
                  reads=[h, ss], writes=[junk, (ss.name, t)])
            P.act(lambda e, t=t: e.activation(out=sd[:, t:t + 1], in_=ss[:, t:t + 1], func=AF.Sqrt, bias=C["eps"][:, 0:1], scale=1.0 / D),
                  reads=[(ss.name, t)], writes=[(sd.name, t)])
            P.dve(lambda e, t=t: e.reciprocal(out=rs[:, t:t + 1], in_=sd[:, t:t + 1]), reads=[(sd.name, t)], writes=[(rs.name, t)])
            P.dve(lambda e, h=h, o=o, t=t: e.scalar_tensor_tensor(out=o[:], in0=h[:], scalar=rs[:, t:t + 1], in1=FN[:],
                                                                op0=ALU.mult, op1=ALU.mult),
                  reads=[h, (rs.name, t), FN], writes=[o])
            P.dma("sp", out[t * 128:(t + 1) * 128, :], o[:], reads=[o])
        P.emit()


def build(stages):
    import contextlib
    kb = KB()
    nc = kb.nc
    x_own = kb.din("x_own", [TO, D])
    out = nc.dram_tensor("out", [TO, D], F32, kind="ExternalOutput").ap()
    H = kb.dscr("H", [TO, D], F32)
    cf = kb.din("cf", [128, 5 * 128])
    nv = kb.din("nv", [128, NVC])
    used = {"x_own", "cf", "nv"}
    ins = {}

    def inp(name, shape):
        if name not in ins:
            ins[name] = kb.din(name, shape)
            used.add(name)
        return ins[name]

    with contextlib.ExitStack() as gst:
        C = {}
        CF = kb.sb(gst, "CF", [128, 5, 128], F32)
        C["CF"] = CF
        C["identb"] = kb.sb(gst, "identb", [128, 128], BF16)
        C["onesb"] = kb.sb(gst, "onesb", [128, 128], BF16)
        C["NV"] = kb.sb(gst, "NV", [128, NVC], F32)
        C["eps"] = kb.sb(gst, "eps", [128, 1], F32)
        P = kb.prog()
        P.dma("sp", CF[:], cf.rearrange("p (a b) -> p a b", a=5), writes=[CF])
        P.dma("sp", C["NV"][:], nv, writes=[C["NV"]])
        P.dve(lambda e: e.tensor_copy(out=C["identb"][:], in_=CF[:, 0, :]), reads=[CF], writes=[C["identb"]])
        P.dve(lambda e: e.tensor_copy(out=C["onesb"][:], in_=CF[:, 4, :]), reads=[CF], writes=[C["onesb"]])
        P.dve(lambda e: e.memset(C["eps"][:], EPS), writes=[C["eps"]])
        for i in range(4):
            P.dma("sp", H[i * 512:(i + 1) * 512, :], x_own[i * 512:(i + 1) * 512, :], writes=[("H", i)])
        P.emit()

        C["one"] = kb.sb(gst, "one", [128, 1], F32)
        P = kb.prog()
        P.dve(lambda e: e.memset(C["one"][:], 1.0), writes=[C["one"]])
        P.emit()
        G = None
        S_ = {}

        def gdn_setup():
            nonlocal G
            if G is not None:
                return
            G = {}
            G["CW"] = kb.sb(gst, "CW", [128, 2, 32, 4], F32)
            G["AB"] = kb.sb(gst, "AB", [128, 2, 2, 16], F32)
            G["BG"] = kb.sb(gst, "BG", [128, 32, 32], F32)
            G["NEA"] = kb.sb(gst, "NEA", [128, 16], F32)
            G["SEL16"] = kb.sb(gst, "SEL16", [16, 16, 128], F32)
            G["SELC"] = kb.sb(gst, "SELC", [128, 2], F32)
            P = kb.prog()
            P.dma("sp", G["CW"][:], inp("convw", [128, 256]).rearrange("p (l c j) -> p l c j", l=2, c=32), writes=[G["CW"]])
            P.dma("sp", G["AB"][:], inp("ab", [128, 64]).rearrange("p (l a h) -> p l a h", l=2, a=2), writes=[G["AB"]])
            P.dma("sp", G["SEL16"][:], inp("sel16", [16, 2048]).rearrange("p (h m) -> p h m", h=16), writes=[G["SEL16"]])
            P.dma("sp", G["SELC"][:], inp("selc", [128, 2]), writes=[G["SELC"]])
            G["lvlmask"] = inp("lvlmask", [128, 7680])
            P.emit()
            S_["XNT_own"] = kb.dscr("XNT_own", [2048, TO], BF16)
            S_["XNT_all"] = kb.dscr("XNT_all", [8, 512, TO], BF16)
            S_["QKVZ"] = kb.dscr("QKVZ", [48, 128, T], BF16)
            S_["OT_lo"] = kb.dscr("OT_lo", [2048, TO], BF16)
            S_["OT_hi"] = kb.dscr("OT_hi", [2048, TO], BF16)
            S_["G_lo"] = kb.dscr("G_lo", [8, 512, TO], BF16)
            S_["G_hi"] = kb.dscr("G_hi", [8, 512, TO], BF16)

        A = {}

        def att_setup():
            if A:
                return
            A["LAMV"] = kb.sb(gst, "LAMV", [128, 2, 4, 128], F32)
            A["SUBW"] = kb.sb(gst, "SUBW", [128, 2, 256], F32)
            A["SW"] = kb.sb(gst, "SW", [128, 256], F32)
            A["nlam"] = kb.sb(gst, "nlam", [128, 1], F32)
            A["l2t"] = kb.sb(gst, "l2t", [128, 128], F32)
            A["lsum"] = kb.sb(gst, "lsum", [128, 2], F32)
            A["BT"] = kb.sb(gst, "BT", [128, 8, 32], F32)
            A["MA"] = kb.sb(gst, "MA", [128, 128], BF16)
            A["MB"] = kb.sb(gst, "MB", [128, 128], BF16)
            P = kb.prog()
            P.dma("sp", A["LAMV"][:], inp("lamv", [128, 1024]).rearrange("p (l a d) -> p l a d", l=2, a=4), writes=[A["LAMV"]])
            P.dma("sp", A["SUBW"][:], inp("sublnw", [128, 512]).rearrange("p (l d) -> p l d", l=2), writes=[A["SUBW"]])
            P.dma("sp", A["BT"][:], inp("bt", [128, 256]).rearrange("p (h d) -> p h d", h=8), writes=[A["BT"]])
            mab = inp("mab", [128, 256])
            P.dma("pool", A["MA"][:], mab[:, 0:128], writes=[A["MA"]])
            P.dma("pool", A["MB"][:], mab[:, 128:256], writes=[A["MB"]])
            P.emit()
            S_["KT_own"] = kb.dscr("KT_own", [2048, TO], BF16)
            S_["KT_all"] = kb.dscr("KT_all", [8, 512, TO], BF16)
            S_["V_own"] = kb.dscr("V_own", [TO, 2048], BF16)
            S_["V_all"] = kb.dscr("V_all", [8, 512, 2048], BF16)
            S_["QT"] = kb.dscr("QT", [16, 128, TO], BF16)

        for stg in stages:
            kind, li = stg[:-1], int(stg[-1]) if stg[-1].isdigit() else None
            if kind == "gdn":
                gdn_setup()
                S_["out"] = out
                gi = {"w_in": [inp(f"w_in_{l}", [D, NCOLS_IN]) if l == li else None for l in range(2)],
                      "w_out": [inp(f"w_out_{l}", [4096, D]) if l == li else None for l in range(2)]}
                gdn_layer(kb, C, G, li, H, S_, gi)
            elif stg == "kv":
                att_setup()
                kv_pass(kb, C, H, inp("w_kv", [D, 4096]), 12 * 16, S_["KT_own"], S_["V_own"])
                allgather(kb, S_["KT_own"], S_["KT_all"])
                allgather(kb, S_["V_own"], S_["V_all"])
            elif kind == "att":
                att_setup()
                attn_layer(kb, C, A, li, H, S_, inp(f"w_q_{li - 2}", [D, D]), inp(f"w_o_{li - 2}", [D, D]), li * 16)
            elif kind == "mlp":
                w1 = inp(f"mlp_w1_{li}", [D, FF])
                w2 = inp(f"mlp_w2_{li}", [FF, D])
                mlp_pass(kb, C, H, w1, w2, (4 + li) * 16)
            elif kind == "ple":
                p_own = inp("p_own", [DEPTH, TO, 256])
                wple = inp(f"ple_w_proj_{li}", [256, D])
                wgate = inp(f"ple_w_gate_{li}", [D, D])
                ple_pass(kb, C, H, p_own[li], wple, wgate, (8 + li) * 16)
            elif stg == "final":
                final_pass(kb, C, H, out, inp("fnorm", [128, D]))
            elif stg == "outh":
                P = kb.prog()
                for i in range(4):
                    P.dma("sp", out[i * 512:(i + 1) * 512, :], H[i * 512:(i + 1) * 512, :])
                P.emit()
            else:
                raise ValueError(stg)
    return nc, sorted(used)


def _fm(v):
    return np.ascontiguousarray(np.asarray(v, np.float32).reshape(16, 128).T)


def host_consts():
    i = np.arange(128)
    ident = np.eye(128, dtype=np.float32)
    U = (i[:, None] <= i[None, :]).astype(np.float32)
    maskl = np.where(i[:, None] > i[None, :], 0.0, BIG).astype(np.float32)
    masklt = np.where(i[None, :] >= i[:, None], 0.0, BIG).astype(np.float32)
    ones = np.ones((128, 128), np.float32)
    return np.concatenate([ident, U, maskl, masklt, ones], axis=1)


def prep_inputs(inputs, used):
    g = {k: np.asarray(v) for k, v in inputs.items()}
    cf = host_consts()
    nvl = [_fm(g["norm_mix"][i]) for i in range(4)] + [_fm(g["norm_mlp"][i]) for i in range(4)] + \
          [_fm(g["norm_ple"][i]) for i in range(4)] + [_fm(g["kv_norm"])]
    nvl.append(np.ascontiguousarray(g["gdn_norm_w"].astype(np.float32).T))
    nv = np.ascontiguousarray(np.concatenate(nvl, axis=1))
    fnorm = np.ascontiguousarray(np.broadcast_to(g["final_norm"].astype(np.float32)[None, :], (128, D)))
    maps = []
    for c in range(8):
        b, s = c // 2, c % 2
        m = {}
        m["x_own"] = np.ascontiguousarray(g["x"][b, s * TO:(s + 1) * TO])
        m["p_own"] = np.ascontiguousarray(g["p"][:, b, s * TO:(s + 1) * TO, :])
        m["cf"] = cf
        m["nv"] = nv
        m["fnorm"] = fnorm
        pp = np.arange(128, dtype=np.float32)
        bt = np.zeros((128, 8, 32), np.float32)
        for hh in range(8):
            slope = 2.0 ** -(hh + 1)
            for dd in range(32):
                dg = dd - 16 + 16 * s
                bt[:, hh, dd] = slope * (pp - 127.0 - 128.0 * dg) if dg >= 0 else -BIG
        m["bt"] = bt.reshape(128, 256)
        tri = (pp[None, :] >= pp[:, None]).astype(np.float32)
        mab = np.zeros((128, 256), np.float32)
        mab[:, 0:128] = tri if s == 0 else 1.0
        mab[:, 128:256] = tri if s == 1 else 0.0
        m["mab"] = mab
        lamv = np.stack([np.stack([g["diff_lambda_q1"][jj], g["diff_lambda_k1"][jj], g["diff_lambda_q2"][jj], g["diff_lambda_k2"][jj]]) for jj in range(2)])
        m["lamv"] = np.ascontiguousarray(np.broadcast_to(lamv.astype(np.float32).reshape(1, 1024), (128, 1024)))
        m["sublnw"] = np.ascontiguousarray(np.broadcast_to(g["diff_subln_w"].astype(np.float32).reshape(1, 512), (128, 512)))
        m["w_kv"] = g["w_kv"]
        for jj in range(2):
            m[f"w_q_{jj}"] = g["diff_w_q"][jj]
            m[f"w_o_{jj}"] = g["diff_w_o"][jj]
        sel16 = np.zeros((16, 16, 128), np.float32)
        for hh in range(16):
            sel16[hh, hh, :] = 1.0
        m["sel16"] = sel16.reshape(16, 2048)
        ii = np.arange(128)
        lm = np.zeros((128, 15, 4, 128), np.float32)
        for l in range(7):
            bsz = 2 ** l
            mk = (((ii[:, None] // bsz) % 2 == 1) & ((ii[None, :] // bsz) == (ii[:, None] // bsz) - 1)).astype(np.float32)
            lm[:, l, :, :] = mk[:, None, :]
            lm[:, 7 + l, :, :] = mk.T[:, None, :]
        lm[:, 14, :, :] = np.eye(128, dtype=np.float32)[:, None, :]
        m["lvlmask"] = lm.reshape(128, 7680)
        selc = np.zeros((128, 2), np.float32)
        selc[:, s] = 1.0
        m["selc"] = selc
        cw = np.zeros((128, 2, 32, 4), np.float32)
        ab = np.zeros((128, 2, 2, 16), np.float32)
        for l in range(2):
            if f"w_in_{l}" in used:
                wi = g["gdn_w_in"][l]
                m[f"w_in_{l}"] = np.ascontiguousarray(np.concatenate([
                    wi[:, s * 1024:(s + 1) * 1024], wi[:, 2048 + s * 1024:2048 + (s + 1) * 1024],
                    wi[:, 4096 + s * 2048:4096 + (s + 1) * 2048], wi[:, 8192 + s * 2048:8192 + (s + 1) * 2048],
                    wi[:, 12288 + s * 16:12288 + (s + 1) * 16], wi[:, 12320 + s * 16:12320 + (s + 1) * 16]], axis=1))
            if f"w_out_{l}" in used:
                m[f"w_out_{l}"] = g["gdn_w_out"][l]
            cwl = g["gdn_conv_w"][l]
            own = np.concatenate([cwl[:, s * 1024:(s + 1) * 1024], cwl[:, 2048 + s * 1024:2048 + (s + 1) * 1024],
                                  cwl[:, 4096 + s * 2048:4096 + (s + 1) * 2048]], axis=1)
            cw[:, l] = own.reshape(4, 32, 128).transpose(2, 1, 0)
            ab[:, l, 0, :] = g["gdn_a_log"][l, s * 16:(s + 1) * 16][None, :]
            ab[:, l, 1, :] = g["gdn_dt_bias"][l, s * 16:(s + 1) * 16][None, :]
        m["convw"] = cw.reshape(128, 256)
        m["ab"] = ab.reshape(128, 64)
        for k in ("mlp_w1", "mlp_w2", "ple_w_proj", "ple_w_gate"):
            for li in range(DEPTH):
                if f"{k}_{li}" in used:
                    m[f"{k}_{li}"] = g[k][li]
        maps.append({k: v for k, v in m.items() if k in used})
    return maps


ALL_STAGES = ["gdn0", "mlp0", "ple0", "gdn1", "mlp1", "ple1", "kv", "att2", "mlp2", "ple2", "att3", "mlp3", "ple3", "final"]
_CACHE = {}


def run_stages(inputs, stages):
    key = tuple(stages)
    if key not in _CACHE:
        _CACHE[key] = build(stages)
    nc, used = _CACHE[key]
    maps = prep_inputs(inputs, set(used))
    res = run_bass_kernel_spmd(nc, maps, core_ids=list(range(8)))
    outs = [r["out"] for r in res.results]
    full = np.empty((4, T, D), np.float32)
    for c in range(8):
        full[c // 2, (c % 2) * TO:(c % 2 + 1) * TO] = outs[c]
    return full


def kernel(**inputs):
    return run_stages(inputs, ALL_STAGES)


def gdn_norm_gather(kb, C, H, nvcol, XNT_own, XNT_all):
    import contextlib
    xv = XNT_own.rearrange("(k p) t -> p k t", p=128)
    for tb in range(2):
        with contextlib.ExitStack() as st:
            HT = [kb.sb(st, "ht", [128, D], F32) for _ in range(8)]
            xT = kb.sb(st, "xT", [128, KD, 1024], BF16)
            P = kb.prog()
            for t in range(8):
                r0 = (tb * 8 + t) * 128
                P.dma("sp", HT[t][:], H[r0:r0 + 128, :], writes=hk(HT[t]))
            norm_T(kb, P, st, C, HT, nvcol, xT, tr_ring(kb, st))
            for kh in range(2):
                P.dma("sp", xv[:, kh * 8:(kh + 1) * 8, tb * 1024:(tb + 1) * 1024], xT[:, kh * 8:(kh + 1) * 8, :],
                      reads=[(xT.name, k, g) for k in range(kh * 8, kh * 8 + 8) for g in range(2)])
            P.emit()
    allgather(kb, XNT_own, XNT_all)


PAIRS = [[0, 1], [2, 3], [4, 5], [6, 7]]


def allgather(kb, src, dst, nch=8):
    R = src.shape[0]
    rc = R // nch
    P = kb.prog()
    for j in range(nch):
        P.add("pool", lambda e, j=j: e.collective_compute("AllGather", ALU.bypass, replica_groups=PAIRS,
                                                        ins=[src[j * rc:(j + 1) * rc, :].opt()], outs=[dst[j].opt()]),
              dma=True, inc=1)
    P.emit()


def gdn_inproj(kb, C, G, li, XNT_all, w_in, QKVZ):
    import contextlib
    with contextlib.ExitStack() as st:
        XT = kb.sb(st, "XT", [128, KD, T], BF16)
        wb = [kb.sb(st, "wb", [128, KD, 256], BF16) for _ in range(2)]
        obuf = [kb.sb(st, "ob", [128, T], BF16) for _ in range(2)]
        xbuf = [kb.sb(st, "xb", [128, 515], F32) for _ in range(2)]
        cacc = Ring([kb.sb(st, "ca", [128, 512], F32) for _ in range(2)])
        sl = Ring([kb.sb(st, "sl", [128, 512], F32) for _ in range(2)])
        sq = Ring([kb.sb(st, "sq", [128, 512], BF16) for _ in range(2)])
        sd = Ring([kb.sb(st, "sd", [128, 512], F32) for _ in range(2)])
        ri = Ring([kb.sb(st, "ri", [128, 512], F32) for _ in range(2)])
        wba = kb.sb(st, "wba", [128, KD, 32], BF16)
        xa = kb.sb(st, "xa", [128, 16], F32)
        ea = kb.sb(st, "ea", [128, 16], F32)
        sp_ = kb.sb(st, "sp", [128, 16], F32)
        psa = Ring([kb.ps(st, "psa", [128, 512]) for _ in range(3)])
        pss = Ring([kb.ps(st, "pss", [128, 512]) for _ in range(2)])
        psb = Ring([kb.ps(st, "psb", [128, 512]) for _ in range(2)])
        CW, AB, BG, NEA = G["CW"], G["AB"], G["BG"], G["NEA"]
        onesb, eps = C["onesb"], C["eps"]
        P = kb.prog()
        for r in range(2):
            for j in range(8):
                P.dma("sp", XT[:, 2 * j:2 * j + 2, r * TO:(r + 1) * TO],
                      XNT_all[j, r * 256:(r + 1) * 256, :].rearrange("(k p) t -> p k t", p=128),
                      writes=[(XT.name, r, j)])
        P.act(lambda e: e.activation(out=NEA[:], in_=AB[:, li, 0, :], func=AF.Exp), reads=[AB], writes=[NEA])
        P.dve(lambda e: e.tensor_scalar(out=NEA[:], in0=NEA[:], scalar1=-1.0, scalar2=None, op0=ALU.mult), reads=[NEA], writes=[NEA])
        wv = w_view(w_in)
        P.dma("pool", wba[:], wv[:, :, 6144:6176], writes=[wba])

        def loadw(wbk):
            P.dma("pool", wb[wbk % 2][:], wv[:, :, wbk * 256:(wbk + 1) * 256], writes=[wb[wbk % 2]])

        loadw(0)
        for wbk in range(24):
            if wbk + 1 < 24:
                loadw(wbk + 1)
            b = wbk % 2
            for cti in range(2):
                ct = wbk * 2 + cti
                typ = "q" if ct < 8 else "k" if ct < 16 else "v" if ct < 32 else "z"
                ob = obuf[ct % 2]
                for blk in range(8):
                    pa = psa.next()
                    osl = ob[:, blk * 512:(blk + 1) * 512]
                    okey = (ob.name, blk)
                    for k in range(KD):
                        P.pe(lambda e, pa=pa, k=k, cti=cti, blk=blk, b=b: e.matmul(
                            pa[:], lhsT=wb[b][:, k, cti * 128:(cti + 1) * 128], rhs=XT[:, k, blk * 512:(blk + 1) * 512],
                            start=(k == 0), stop=(k == KD - 1)),
                            reads=[wb[b], (XT.name, blk // 4, k // 2)], writes=[pa])
                    if typ == "z":
                        P.act(lambda e, pa=pa, osl=osl: e.activation(out=osl, in_=pa[:], func=AF.Silu), writes=[pa, okey])
                        continue
                    xb_, prev = xbuf[blk % 2], xbuf[(blk + 1) % 2]
                    if blk == 0:
                        P.dve(lambda e, xb_=xb_: e.memset(xb_[:, 0:3], 0.0), writes=[(xb_.name, "h")])
                    else:
                        P.dve(lambda e, xb_=xb_, prev=prev: e.tensor_copy(out=xb_[:, 0:3], in_=prev[:, 512:515]),
                              reads=[prev], writes=[(xb_.name, "h")])
                    P.act(lambda e, pa=pa, xb_=xb_: e.activation(out=xb_[:, 3:515], in_=pa[:], func=AF.Copy), writes=[pa, xb_])
                    ca = cacc.next()
                    P.dve(lambda e, ca=ca, xb_=xb_, ct=ct: e.tensor_scalar(out=ca[:], in0=xb_[:, 3:515], scalar1=CW[:, li, ct, 3:4],
                                                                        scalar2=None, op0=ALU.mult),
                          reads=[xb_, CW], writes=[ca])
                    for j in (2, 1, 0):
                        P.dve(lambda e, ca=ca, xb_=xb_, ct=ct, j=j: e.scalar_tensor_tensor(
                            out=ca[:], in0=xb_[:, j:j + 512], scalar=CW[:, li, ct, j:j + 1], in1=ca[:], op0=ALU.mult, op1=ALU.add),
                            reads=[xb_, (xb_.name, "h"), CW], writes=[ca])
                    if typ == "v":
                        P.act(lambda e, ca=ca, osl=osl: e.activation(out=osl, in_=ca[:], func=AF.Silu), reads=[ca], writes=[okey])
                        continue
                    s_, q_, d_, r_ = sl.next(), sq.next(), sd.next(), ri.next()
                    P.act(lambda e, ca=ca, s_=s_: e.activation(out=s_[:], in_=ca[:], func=AF.Silu), reads=[ca], writes=[s_])
                    P.act(lambda e, s_=s_, q_=q_: e.activation(out=q_[:], in_=s_[:], func=AF.Square), reads=[s_], writes=[q_])
                    ps_ = pss.next()
                    P.pe(lambda e, ps_=ps_, q_=q_: e.matmul(ps_[:], lhsT=onesb[:], rhs=q_[:], start=True, stop=True),
                         reads=[q_, onesb], writes=[ps_])
                    P.act(lambda e, ps_=ps_, d_=d_: e.activation(out=d_[:], in_=ps_[:], func=AF.Sqrt, bias=eps[:, 0:1], scale=1.0),
                          reads=[eps], writes=[ps_, d_])
                    P.dve(lambda e, d_=d_, r_=r_: e.reciprocal(out=r_[:], in_=d_[:]), reads=[d_], writes=[r_])
                    qs = (128.0 ** -0.5) if typ == "q" else 1.0
                    P.dve(lambda e, s_=s_, r_=r_, osl=osl, qs=qs: e.scalar_tensor_tensor(out=osl, in0=s_[:], scalar=qs, in1=r_[:],
                                                                                     op0=ALU.mult, op1=ALU.mult),
                          reads=[s_, r_], writes=[okey])
                P.dma("sp", QKVZ[ct], ob[:], reads=[(ob.name, blk) for blk in range(8)])
        for tt in range(32):
            pb = psb.next()
            for k in range(KD):
                P.pe(lambda e, pb=pb, k=k, tt=tt: e.matmul(pb[:, 0:32], lhsT=XT[:, k, tt * 128:(tt + 1) * 128], rhs=wba[:, k, :],
                                                        start=(k == 0), stop=(k == KD - 1)),
                     reads=[wba, (XT.name, tt // 16, k // 2)], writes=[pb])
            P.act(lambda e, pb=pb, tt=tt: e.activation(out=BG[:, tt, 0:16], in_=pb[:, 0:16], func=AF.Sigmoid), writes=[pb, (BG.name, tt, 0)])
            P.dve(lambda e, pb=pb: e.tensor_tensor(out=xa[:], in0=pb[:, 16:32], in1=AB[:, li, 1, :], op=ALU.add), reads=[AB], writes=[pb, xa])
            P.act(lambda e: e.activation(out=ea[:], in_=xa[:], func=AF.Exp), reads=[xa], writes=[ea])
            P.act(lambda e: e.activation(out=sp_[:], in_=ea[:], func=AF.Ln, bias=C["one"][:, 0:1], scale=1.0), reads=[ea, C["one"]], writes=[sp_])
            P.dve(lambda e, tt=tt: e.tensor_tensor(out=BG[:, tt, 16:32], in0=sp_[:], in1=NEA[:], op=ALU.mult),
                  reads=[sp_, NEA], writes=[(BG.name, tt, 1)])
        P.emit()


def gdn_scan(kb, C, G, li, QKVZ, OT_lo, OT_hi):
    import contextlib
    CF, identb, onesb, NV, eps = C["CF"], C["identb"], C["onesb"], C["NV"], C["eps"]
    identf, U, MASKL, MASKLT = CF[:, 0, :], CF[:, 1, :], CF[:, 2, :], CF[:, 3, :]
    onesf = CF[:, 4, :]
    BG, SEL16 = G["BG"], G["SEL16"]
    qkvz_v = QKVZ.rearrange("c p t -> p c t")
    with contextlib.ExitStack() as st:
        S = kb.sb(st, "S", [128, HL, 128], F32)
        Sbf = kb.sb(st, "Sbf", [128, HL, 128], BF16)
        inb = [kb.sb(st, "inb", [128, 48, 512], BF16) for _ in range(1)]
        otst = [kb.sb(st, "otst", [128, HL, 512], BF16) for _ in range(1)]
        tk = {n: kb.sb(st, n, [128, 16], F32) for n in ("gc", "egc", "bege", "edl", "negb", "glb")}
        gcT = kb.sb(st, "gcT", [16, 128], F32)

        def gt(name, dt, depth=1):
            return PRing(Ring([kb.sb(st, name, [128, 4, 128], dt) for _ in range(depth)]),
                         Ring([kb.sb(st, name, [128, 4, 128], dt) for _ in range(depth)]))

        Lm, LT, ER = gt("Lm", F32), gt("LT", F32), gt("ER", F32)
        tmpA, tmpB = gt("tmpA", F32), gt("tmpB", F32)
        Yb = [gt("Y0", BF16)]
        Pb = [gt("P0", BF16)]
        ymr, pmr, w1r, w2r, tcr, ttr = gt("YM", BF16), gt("PM", BF16), gt("W1", BF16), gt("W2", BF16), gt("Tc", BF16, 2), gt("TTc", BF16, 2)
        t1, rr, vnew, MT, qd, kdec = gt("t1", F32), gt("rr", BF16), gt("vnew", BF16), gt("MT", BF16), gt("qd", BF16), gt("kdec", BF16)
        osb, osq, sdn, rsn, onn = gt("osb", F32), gt("osq", BF16), gt("sdn", F32), gt("rsn", F32), gt("onn", F32)
        psf = PRing(Ring([kb.ps(st, "psf", [128, 4, 128]) for _ in range(3)]), Ring([kb.ps(st, "psf", [128, 4, 128]) for _ in range(3)]))
        psh = PRing(Ring([kb.ps(st, "psh", [128, 8, 128], BF16) for _ in range(1)]), Ring([kb.ps(st, "psh", [128, 8, 128], BF16) for _ in range(1)]))

        G["LM"] = kb.sb(st, "LM", [128, 15, 4, 128], BF16)
        P = kb.prog()
        P.dma("pool", G["LM"][:], G["lvlmask"].rearrange("p (l a m) -> p l a m", l=15, a=4), writes=[G["LM"]])
        P.dve(lambda e: e.memset(S[:], 0.0), writes=[S])
        P.dve(lambda e: e.memset(Sbf[:], 0.0), writes=[Sbf])
        P.emit()

        for sc in range(8):
            P = kb.prog()
            ib = inb[0]
            ot = otst[0]
            for q4 in range(4):
                P.dma("sp", ib[:, q4 * 12:(q4 + 1) * 12, :], qkvz_v[:, q4 * 12:(q4 + 1) * 12, sc * 512:(sc + 1) * 512],
                      writes=[(ib.name, q4)])
            ibk = [(ib.name, q4) for q4 in range(4)]
            def chunk(cl, sc=sc, P=P):
                n = sc * 4 + cl
                cs = slice(cl * 128, (cl + 1) * 128)
                qT = lambda hq: ib[:, hq, cs]
                kT = lambda hq: ib[:, 8 + hq, cs]
                vT = lambda h: ib[:, 16 + h, cs]
                zT = lambda h: ib[:, 32 + h, cs]
                beta, g_ = BG[:, n, 0:16], BG[:, n, 16:32]
                bgk = [(BG.name, n, 0), (BG.name, n, 1)]
                pg = psf.next()
                P.pe(lambda e, pg=pg, g_=g_: e.matmul(pg[:, 0, 0:16], lhsT=U, rhs=g_, start=True, stop=True), reads=[CF] + bgk, writes=[pg])
                P.pe(lambda e, pg=pg, g_=g_: e.matmul(pg[:, 1, 0:16], lhsT=onesf, rhs=g_, start=True, stop=True), reads=[CF] + bgk, writes=[pg])
                P.dve(lambda e, pg=pg: e.tensor_copy(out=tk["gc"][:], in_=pg[:, 0, 0:16]), writes=[pg, tk["gc"]])
                P.dve(lambda e, pg=pg: e.tensor_copy(out=tk["glb"][:], in_=pg[:, 1, 0:16]), writes=[pg, tk["glb"]])
                P.act(lambda e: e.activation(out=tk["egc"][:], in_=tk["gc"][:], func=AF.Exp), reads=[tk["gc"]], writes=[tk["egc"]])
                P.dve(lambda e, beta=beta: e.tensor_tensor(out=tk["bege"][:], in0=tk["egc"][:], in1=beta, op=ALU.mult),
                      reads=[tk["egc"]] + bgk, writes=[tk["bege"]])
                P.dve(lambda e: e.tensor_tensor(out=tk["edl"][:], in0=tk["glb"][:], in1=tk["gc"][:], op=ALU.subtract),
                      reads=[tk["glb"], tk["gc"]], writes=[tk["edl"]])
                P.act(lambda e: e.activation(out=tk["edl"][:], in_=tk["edl"][:], func=AF.Exp), reads=[tk["edl"]], writes=[tk["edl"]])
                P.dve(lambda e, beta=beta: e.tensor_scalar(out=tk["negb"][:], in0=beta, scalar1=-1.0, scalar2=None, op0=ALU.mult),
                      reads=bgk, writes=[tk["negb"]])
                pt_ = psf.next()
                P.pe(lambda e, pt_=pt_: e.transpose(out=pt_[0:16, 0, :], in_=tk["gc"][:], identity=identf), reads=[tk["gc"], CF], writes=[pt_])
                P.dve(lambda e, pt_=pt_: e.tensor_copy(out=gcT[:], in_=pt_[0:16, 0, :]), writes=[pt_, gcT])

                def group(g4):
                    PAR[0] = g4 % 2
                    hs = [g4 * 4 + i for i in range(4)]
                    hqs = [g4 * 2, g4 * 2 + 1]
                    Lm_, LT_, ER_, tA, tB = Lm.next(), LT.next(), ER.next(), tmpA.next(), tmpB.next()
                    pgr = psf.next()
                    for i, h in enumerate(hs):
                        P.pe(lambda e, pgr=pgr, i=i, h=h: e.matmul(pgr[:, i, :], lhsT=SEL16[:, h, :], rhs=gcT[:], start=True, stop=True),
                             reads=[SEL16, gcT], writes=[pgr])
                    for i, h in enumerate(hs):
                        gcol = tk["gc"][:, h:h + 1]
                        P.dve(lambda e, pgr=pgr, i=i, gcol=gcol, tA=tA: e.scalar_tensor_tensor(
                            out=tA[:, i, :], in0=pgr[:, i, :], scalar=gcol, in1=MASKL, op0=ALU.subtract, op1=ALU.add),
                            reads=[tk["gc"], CF], writes=[pgr, (tA.name, i)])
                        P.dve(lambda e, pgr=pgr, i=i, gcol=gcol, tB=tB: e.scalar_tensor_tensor(
                            out=tB[:, i, :], in0=pgr[:, i, :], scalar=gcol, in1=MASKLT, op0=ALU.subtract, op1=ALU.subtract),
                            reads=[tk["gc"], CF], writes=[pgr, (tB.name, i)])
                    P.act(lambda e, pgr=pgr, ER_=ER_: e.activation(out=ER_[:], in_=pgr[:], func=AF.Exp), writes=[pgr] + [(ER_.name, i) for i in range(4)])
                    P.act(lambda e, tA=tA, Lm_=Lm_: e.activation(out=Lm_[:], in_=tA[:], func=AF.Exp, scale=-1.0),
                          reads=[(tA.name, i) for i in range(4)], writes=[(Lm_.name, i) for i in range(4)])
                    P.act(lambda e, tB=tB, LT_=LT_: e.activation(out=LT_[:], in_=tB[:], func=AF.Exp),
                          reads=[(tB.name, i) for i in range(4)], writes=[(LT_.name, i) for i in range(4)])
                    pkk = psf.next()
                    for a, hq in enumerate(hqs):
                        P.pe(lambda e, pkk=pkk, a=a, hq=hq: e.matmul(pkk[:, a, :], lhsT=kT(hq), rhs=kT(hq), start=True, stop=True),
                             reads=ibk, writes=[pkk])
                        P.pe(lambda e, pkk=pkk, a=a, hq=hq: e.matmul(pkk[:, 2 + a, :], lhsT=kT(hq), rhs=qT(hq), start=True, stop=True),
                             reads=ibk, writes=[pkk])
                    Y, Pm = Yb[0].next(), Pb[0].next()
                    MT_ = MT.next()
                    for i, h in enumerate(hs):
                        P.dve(lambda e, pkk=pkk, i=i, h=h, Y=Y, Lm_=Lm_: e.scalar_tensor_tensor(
                            out=Y[:, i, :], in0=pkk[:, i // 2, :], scalar=tk["negb"][:, h:h + 1], in1=Lm_[:, i, :], op0=ALU.mult, op1=ALU.mult),
                            reads=[tk["negb"], (Lm_.name, i)], writes=[pkk, (Y.name, i)])
                        P.dve(lambda e, pkk=pkk, i=i, MT_=MT_, LT_=LT_: e.tensor_tensor(
                            out=MT_[:, i, :], in0=pkk[:, 2 + i // 2, :], in1=LT_[:, i, :], op=ALU.mult),
                            reads=[(LT_.name, i)], writes=[pkk, (MT_.name, i)])
                    ph = psh.next()
                    for i in range(4):
                        P.pe(lambda e, ph=ph, i=i, Y=Y: e.transpose(out=ph[:, i, :], in_=Y[:, i, :], identity=identb[:]),
                             reads=[(Y.name, i), identb], writes=[ph])
                    P.act(lambda e, ph=ph, Pm=Pm: e.activation(out=Pm[:], in_=ph[:, 0:4, :], func=AF.Copy),
                          writes=[ph] + [(Pm.name, i) for i in range(4)])
                    k4_ = lambda t_: [(t_.name, i) for i in range(4)]
                    LMt = G["LM"]
                    Tc = TTc = None
                    for l in range(7):
                        YM, PM = ymr.next(), pmr.next()
                        P.dve(lambda e, YM=YM, Y=Y, l=l: e.tensor_tensor(out=YM[:], in0=Y[:], in1=LMt[:, l, :, :], op=ALU.mult),
                              reads=k4_(Y) + [LMt], writes=k4_(YM))
                        P.dve(lambda e, PM=PM, Pm=Pm, l=l: e.tensor_tensor(out=PM[:], in0=Pm[:], in1=LMt[:, 7 + l, :, :], op=ALU.mult),
                              reads=k4_(Pm) + [LMt], writes=k4_(PM))
                        Tn, TTn = tcr.next(), ttr.next()
                        if l == 0:
                            P.dve(lambda e, TTn=TTn, PM=PM: e.tensor_tensor(out=TTn[:], in0=PM[:], in1=LMt[:, 14, :, :], op=ALU.add),
                                  reads=k4_(PM) + [LMt], writes=k4_(TTn))
                            P.dve(lambda e, Tn=Tn, YM=YM: e.tensor_tensor(out=Tn[:], in0=YM[:], in1=LMt[:, 14, :, :], op=ALU.add),
                                  reads=k4_(YM) + [LMt], writes=k4_(Tn))
                        else:
                            W1 = w1r.next()
                            pw = psf.next()
                            for i in range(4):
                                P.pe(lambda e, pw=pw, i=i, YM=YM, TTc=TTc: e.matmul(pw[:, i, :], lhsT=YM[:, i, :], rhs=TTc[:, i, :], start=True, stop=True),
                                     reads=[(YM.name, i), (TTc.name, i)], writes=[pw])
                            if l < 6:
                                W2 = w2r.next()
                                pw2 = psf.next()
                                for i in range(4):
                                    P.pe(lambda e, pw2=pw2, i=i, PM=PM, Tc=Tc: e.matmul(pw2[:, i, :], lhsT=PM[:, i, :], rhs=Tc[:, i, :], start=True, stop=True),
                                         reads=[(PM.name, i), (Tc.name, i)], writes=[pw2])
                            P.act(lambda e, pw=pw, W1=W1: e.activation(out=W1[:], in_=pw[:], func=AF.Copy), writes=[pw] + k4_(W1))
                            if l < 6:
                                P.act(lambda e, pw2=pw2, W2=W2: e.activation(out=W2[:], in_=pw2[:], func=AF.Copy), writes=[pw2] + k4_(W2))
                            pt2 = psf.next()
                            for i in range(4):
                                P.pe(lambda e, pt2=pt2, i=i, Tc=Tc, W1=W1: e.matmul(pt2[:, i, :], lhsT=Tc[:, i, :], rhs=W1[:, i, :], start=True, stop=True),
                                     reads=[(Tc.name, i), (W1.name, i)], writes=[pt2])
                            if l < 6:
                                pt3 = psf.next()
                                for i in range(4):
                                    P.pe(lambda e, pt3=pt3, i=i, TTc=TTc, W2=W2: e.matmul(pt3[:, i, :], lhsT=TTc[:, i, :], rhs=W2[:, i, :], start=True, stop=True),
                                         reads=[(TTc.name, i), (W2.name, i)], writes=[pt3])
                            P.dve(lambda e, pt2=pt2, TTn=TTn, TTc=TTc: e.tensor_tensor(out=TTn[:], in0=pt2[:], in1=TTc[:], op=ALU.add),
                                  reads=k4_(TTc), writes=[pt2] + k4_(TTn))
                            if l < 6:
                                P.dve(lambda e, pt3=pt3, Tn=Tn, Tc=Tc: e.tensor_tensor(out=Tn[:], in0=pt3[:], in1=Tc[:], op=ALU.add),
                                      reads=k4_(Tc), writes=[pt3] + k4_(Tn))
                        Tc, TTc = Tn, TTn
                    R = TTc
                    TT = R
                    pks = psf.next()
                    for i, h in enumerate(hs):
                        P.pe(lambda e, pks=pks, i=i, h=h: e.matmul(pks[:, i, :], lhsT=kT(h // 2), rhs=Sbf[:, h, :], start=True, stop=True),
                             reads=ibk + [(Sbf.name, h)], writes=[pks])
                    pv = psh.next()
                    for i, h in enumerate(hs):
                        P.pe(lambda e, pv=pv, i=i, h=h: e.transpose(out=pv[:, i, :], in_=vT(h), identity=identb[:]), reads=ibk + [identb], writes=[pv])
                    for a, hq in enumerate(hqs):
                        P.pe(lambda e, pv=pv, a=a, hq=hq: e.transpose(out=pv[:, 4 + a, :], in_=kT(hq), identity=identb[:]), reads=ibk + [identb], writes=[pv])
                    t1_, rr_, vn_, qd_, kd_ = t1.next(), rr.next(), vnew.next(), qd.next(), kdec.next()
                    for i, h in enumerate(hs):
                        P.dve(lambda e, pks=pks, i=i, h=h, t1_=t1_: e.tensor_scalar(out=t1_[:, i, :], in0=pks[:, i, :], scalar1=tk["bege"][:, h:h + 1],
                                                                                 scalar2=None, op0=ALU.mult),
                              reads=[tk["bege"]], writes=[pks, (t1_.name, i)])
                        P.dve(lambda e, pv=pv, i=i, h=h, t1_=t1_, rr_=rr_: e.scalar_tensor_tensor(
                            out=rr_[:, i, :], in0=pv[:, i, :], scalar=BG[:, n, h:h + 1], in1=t1_[:, i, :], op0=ALU.mult, op1=ALU.subtract),
                            reads=bgk + [(t1_.name, i)], writes=[pv, (rr_.name, i)])
                        P.act(lambda e, pv=pv, i=i, h=h, kd_=kd_: e.activation(out=kd_[:, i, :], in_=pv[:, 4 + i // 2, :], func=AF.Copy,
                                                                             scale=tk["edl"][:, h:h + 1]),
                              reads=[tk["edl"]], writes=[pv, (kd_.name, i)])
                    pvn = psf.next()
                    for i in range(4):
                        P.pe(lambda e, pvn=pvn, i=i, TT=TT, rr_=rr_: e.matmul(pvn[:, i, :], lhsT=TT[:, i, :], rhs=rr_[:, i, :], start=True, stop=True),
                             reads=[(TT.name, i), (rr_.name, i)], writes=[pvn])
                    P.act(lambda e, pvn=pvn, vn_=vn_: e.activation(out=vn_[:], in_=pvn[:], func=AF.Copy),
                          writes=[pvn] + [(vn_.name, i) for i in range(4)])
                    for i, h in enumerate(hs):
                        P.dve(lambda e, i=i, h=h, qd_=qd_, ER_=ER_: e.tensor_tensor(out=qd_[:, i, :], in0=qT(h // 2), in1=ER_[:, i, :], op=ALU.mult),
                              reads=ibk + [(ER_.name, i)], writes=[(qd_.name, i)])
                    po = psf.next()
                    for i, h in enumerate(hs):
                        P.pe(lambda e, po=po, i=i, h=h, qd_=qd_: e.matmul(po[:, i, :], lhsT=Sbf[:, h, :], rhs=qd_[:, i, :], start=True, stop=False),
                             reads=[(Sbf.name, h), (qd_.name, i)], writes=[po])
                        P.pe(lambda e, po=po, i=i, vn_=vn_, MT_=MT_: e.matmul(po[:, i, :], lhsT=vn_[:, i, :], rhs=MT_[:, i, :], start=False, stop=True),
                             reads=[(vn_.name, i), (MT_.name, i)], writes=[po])
                    pds = psf.next()
                    for i in range(4):
                        P.pe(lambda e, pds=pds, i=i, kd_=kd_, vn_=vn_: e.matmul(pds[:, i, :], lhsT=kd_[:, i, :], rhs=vn_[:, i, :], start=True, stop=True),
                             reads=[(kd_.name, i), (vn_.name, i)], writes=[pds])
                    for i, h in enumerate(hs):
                        P.dve(lambda e, pds=pds, i=i, h=h, ER_=ER_: e.scalar_tensor_tensor(
                            out=S[:, h, :], in0=S[:, h, :], scalar=ER_[:, i, 127:128], in1=pds[:, i, :], op0=ALU.mult, op1=ALU.add),
                            reads=[(ER_.name, i), (S.name, h)], writes=[pds, (S.name, h)])
                        P.act(lambda e, h=h: e.activation(out=Sbf[:, h, :], in_=S[:, h, :], func=AF.Copy), reads=[(S.name, h)], writes=[(Sbf.name, h)])
                    ob_, oq_, sd_, rs_, on_ = osb.next(), osq.next(), sdn.next(), rsn.next(), onn.next()
                    k4 = lambda t_: [(t_.name, i) for i in range(4)]
                    P.act(lambda e, po=po, ob_=ob_: e.activation(out=ob_[:], in_=po[:], func=AF.Copy), writes=[po] + k4(ob_))
                    P.act(lambda e, ob_=ob_, oq_=oq_: e.activation(out=oq_[:], in_=ob_[:], func=AF.Square), reads=k4(ob_), writes=k4(oq_))
                    pss = psf.next()
                    for i in range(4):
                        P.pe(lambda e, pss=pss, i=i, oq_=oq_: e.matmul(pss[:, i, :], lhsT=onesb[:], rhs=oq_[:, i, :], start=True, stop=True),
                             reads=[onesb, (oq_.name, i)], writes=[pss])
                    P.act(lambda e, pss=pss, sd_=sd_: e.activation(out=sd_[:], in_=pss[:], func=AF.Sqrt, bias=eps[:, 0:1], scale=1.0 / 128),
                          reads=[eps], writes=[pss] + k4(sd_))
                    P.dve(lambda e, sd_=sd_, rs_=rs_: e.reciprocal(out=rs_[:], in_=sd_[:]), reads=k4(sd_), writes=k4(rs_))
                    P.dve(lambda e, ob_=ob_, rs_=rs_, on_=on_: e.tensor_tensor(out=on_[:], in0=ob_[:], in1=rs_[:], op=ALU.mult),
                          reads=k4(ob_) + k4(rs_), writes=k4(on_))
                    for i, h in enumerate(hs):
                        P.dve(lambda e, i=i, h=h, on_=on_: e.scalar_tensor_tensor(
                            out=ot[:, h, cs], in0=on_[:, i, :], scalar=NV[:, 208 + li:209 + li], in1=zT(h), op0=ALU.mult, op1=ALU.mult),
                            reads=[(on_.name, i), NV] + ibk, writes=[(ot.name, h, cl)])
                for ga, gb_ in ((0, 1), (2, 3)):
                    n0 = len(P.ops)
                    group(ga)
                    opsA = P.ops[n0:]
                    del P.ops[n0:]
                    group(gb_)
                    opsB = P.ops[n0:]
                    del P.ops[n0:]
                    PAR[0] = 0
                    for k_ in range(max(len(opsA), len(opsB))):
                        if k_ < len(opsA):
                            P.ops.append(opsA[k_])
                        if k_ < len(opsB):
                            P.ops.append(opsB[k_])

            for cl in range(4):
                chunk(cl)
            dst = OT_lo if sc < 4 else OT_hi
            dv = dst.rearrange("(h e) t -> e h t", e=128)
            P.dma("sp", dv[:, :, (sc % 4) * 512:(sc % 4 + 1) * 512], ot[:],
                  reads=[(ot.name, h, cl) for h in range(HL) for cl in range(4)])
            P.emit()


def gdn_outproj(kb, C, G, H, G_lo, G_hi, w_out):
    import contextlib
    SELC = G["SELC"]
    wv = w_out.rearrange("(c p) n -> p c n", p=128)
    for tb in range(4):
        with contextlib.ExitStack() as st:
            HT = [kb.sb(st, "ht", [128, D], F32) for _ in range(4)]
            oa = kb.sb(st, "oa", [128, 32, 512], BF16)
            ob = kb.sb(st, "ob", [128, 32, 512], BF16)
            wo = [kb.sb(st, "wo", [128, 32, 512], BF16) for _ in range(2)]
            psy = Ring([kb.ps(st, "psy", [128, 512]) for _ in range(4)])
            P = kb.prog()
            for t in range(4):
                r0 = (tb * 4 + t) * 128
                P.dma("sp", HT[t][:], H[r0:r0 + 128, :], writes=hk(HT[t]))
            for r in range(2):
                for j in range(8):
                    c0 = r * 16 + 2 * j
                    P.dma("sp", oa[:, c0:c0 + 2, :], G_lo[j, r * 256:(r + 1) * 256, :].rearrange("(k p) t -> p k t", p=128)[:, :, tb * 512:(tb + 1) * 512],
                          writes=[(oa.name, r, j)])
                    P.dma("sp", ob[:, c0:c0 + 2, :], G_hi[j, r * 256:(r + 1) * 256, :].rearrange("(k p) t -> p k t", p=128)[:, :, tb * 512:(tb + 1) * 512],
                          writes=[(ob.name, r, j)])
            for c2 in range(2):
                sl_ = slice(c2 * 16, (c2 + 1) * 16)
                P.dve(lambda e, sl_=sl_: e.tensor_scalar(out=oa[:, sl_, :], in0=oa[:, sl_, :], scalar1=SELC[:, 0:1], scalar2=None, op0=ALU.mult),
                      reads=[SELC] + [(oa.name, c2, j) for j in range(8)], writes=[(oa.name, c2)])
                P.dve(lambda e, sl_=sl_: e.scalar_tensor_tensor(out=oa[:, sl_, :], in0=ob[:, sl_, :], scalar=SELC[:, 1:2], in1=oa[:, sl_, :],
                                                              op0=ALU.mult, op1=ALU.add),
                      reads=[SELC] + [(ob.name, c2, j) for j in range(8)], writes=[(oa.name, c2)])
            P.dma("pool", wo[0][:], wv[:, :, 0:512], writes=[wo[0]])
            for db in range(4):
                b = db % 2
                if db + 1 < 4:
                    P.dma("pool", wo[1 - b][:], wv[:, :, (db + 1) * 512:(db + 2) * 512], writes=[wo[1 - b]])
                for t in range(4):
                    py = psy.next()
                    for c in range(32):
                        P.pe(lambda e, py=py, c=c, t=t, b=b: e.matmul(py[:], lhsT=oa[:, c, t * 128:(t + 1) * 128], rhs=wo[b][:, c, :],
                                                                    start=(c == 0), stop=(c == 31)),
                             reads=[wo[b], (oa.name, c // 16)], writes=[py])
                    hs_ = HT[t][:, db * 512:(db + 1) * 512]
                    P.dve(lambda e, hs_=hs_, py=py: e.tensor_tensor(out=hs_, in0=hs_, in1=py[:], op=ALU.add),
                          reads=[(HT[t].name, db)], writes=[py, (HT[t].name, db)])
            for t in range(4):
                r0 = (tb * 4 + t) * 128
                P.dma("sp", H[r0:r0 + 128, :], HT[t][:], reads=hk(HT[t]))
            P.emit()


def gdn_layer(kb, C, G, li, H, S_, ins):
    gdn_norm_gather(kb, C, H, li * 16, S_["XNT_own"], S_["XNT_all"])
    if DEBUG_CUT == 11:
        return
    gdn_inproj(kb, C, G, li, S_["XNT_all"], ins["w_in"][li], S_["QKVZ"])
    if DEBUG_CUT == 12:
        import contextlib
        ov = S_["out"].rearrange("(a b) c -> a (b c)", b=2)
        with contextlib.ExitStack() as st:
            tb_ = kb.sb(st, "dbb", [128, T], BF16)
            tf_ = kb.sb(st, "dbf", [128, T], F32)
            P = kb.prog()
            for i, ct in enumerate((0, 8, 16, 32, 7, 15, 31, 47)):
                P.dma("sp", tb_[:], S_["QKVZ"][ct], writes=[tb_])
                P.dve(lambda e: e.tensor_copy(out=tf_[:], in_=tb_[:]), reads=[tb_], writes=[tf_])
                P.dma("sp", ov[i * 128:(i + 1) * 128, :], tf_[:], reads=[tf_])
            P.dma("sp", ov[1024 - 128:1024, 0:1024], G["BG"][:].rearrange("p a b -> p (a b)"), reads=[])
            P.emit()
        return
    gdn_scan(kb, C, G, li, S_["QKVZ"], S_["OT_lo"], S_["OT_hi"])
    if DEBUG_CUT == 13:
        import contextlib
        with contextlib.ExitStack() as st:
            tb_ = kb.sb(st, "dbb", [128, TO], BF16)
            tf_ = kb.sb(st, "dbf", [128, TO], F32)
            P = kb.prog()
            for i in range(16):
                P.dma("sp", tb_[:], S_["OT_lo"][i * 128:(i + 1) * 128, :], writes=[tb_])
                P.dve(lambda e: e.tensor_copy(out=tf_[:], in_=tb_[:]), reads=[tb_], writes=[tf_])
                P.dma("sp", S_["out"][i * 128:(i + 1) * 128, :], tf_[:], reads=[tf_])
            P.emit()
        return
    allgather(kb, S_["OT_lo"], S_["G_lo"])
    allgather(kb, S_["OT_hi"], S_["G_hi"])
    gdn_outproj(kb, C, G, H, S_["G_lo"], S_["G_hi"], ins["w_out"][li])


def kv_pass(kb, C, H, w_kv, nvcol, KT_own, V_own):
    import contextlib
    wv = w_view(w_kv)
    for tb in range(2):
        with contextlib.ExitStack() as st:
            HT = [kb.sb(st, "ht", [128, D], F32) for _ in range(8)]
            xT = kb.sb(st, "xT", [128, KD, 1024], BF16)
            with contextlib.ExitStack() as st1:
                P = kb.prog()
                for t in range(8):
                    r0 = (tb * 8 + t) * 128
                    P.dma("sp", HT[t][:], H[r0:r0 + 128, :], writes=hk(HT[t]))
                norm_T(kb, P, st1, C, HT, nvcol, xT, tr_ring(kb, st1))
                P.emit()
            with contextlib.ExitStack() as st2:
                P = kb.prog()
                wb = [kb.sb(st2, "wkv", [128, KD, 512], BF16) for _ in range(2)]
                kst = Ring([kb.sb(st2, "kst", [128, 1024], BF16) for _ in range(2)])
                vst = Ring([kb.sb(st2, "vst", [128, 512], BF16) for _ in range(3)])
                psa = Ring([kb.ps(st2, "psa", [128, 512]) for _ in range(4)])
                P.dma("pool", wb[0][:], wv[:, :, 0:512], writes=[wb[0]])
                for blk in range(8):
                    b = blk % 2
                    if blk + 1 < 8:
                        P.dma("pool", wb[1 - b][:], wv[:, :, (blk + 1) * 512:(blk + 2) * 512], writes=[wb[1 - b]])
                    if blk < 4:
                        for ci in range(4):
                            hc = blk * 4 + ci
                            ks = kst.next()
                            for tg in range(2):
                                pa = psa.next()
                                for k in range(KD):
                                    P.pe(lambda e, pa=pa, k=k, ci=ci, tg=tg, b=b: e.matmul(pa[:], lhsT=wb[b][:, k, ci * 128:(ci + 1) * 128],
                                                                                         rhs=xT[:, k, tg * 512:(tg + 1) * 512],
                                                                                         start=(k == 0), stop=(k == KD - 1)),
                                         reads=[wb[b], (xT.name, k, tg)], writes=[pa])
                                P.act(lambda e, pa=pa, ks=ks, tg=tg: e.activation(out=ks[:, tg * 512:(tg + 1) * 512], in_=pa[:], func=AF.Copy),
                                      writes=[pa, (ks.name, tg)])
                            P.dma("sp", KT_own[hc * 128:(hc + 1) * 128, tb * 1024:(tb + 1) * 1024], ks[:], reads=[(ks.name, 0), (ks.name, 1)])
                    else:
                        db = blk - 4
                        for t in range(8):
                            pa = psa.next()
                            vs = vst.next()
                            for k in range(KD):
                                P.pe(lambda e, pa=pa, k=k, t=t, b=b: e.matmul(pa[:], lhsT=xT[:, k, t * 128:(t + 1) * 128], rhs=wb[b][:, k, :],
                                                                            start=(k == 0), stop=(k == KD - 1)),
                                     reads=[wb[b], (xT.name, k, t // 4)], writes=[pa])
                            P.act(lambda e, pa=pa, vs=vs: e.activation(out=vs[:], in_=pa[:], func=AF.Copy), writes=[pa, vs])
                            r0 = (tb * 8 + t) * 128
                            P.dma("sp", V_own[r0:r0 + 128, db * 512:(db + 1) * 512], vs[:], reads=[vs])
                P.emit()


def attn_layer(kb, C, A, li, H, S_, w_q, w_o, nvcol):
    import contextlib, math
    j = li - N_A
    lam_init = 0.8 - 0.6 * math.exp(-0.3 * li)
    QT, KT_all, V_all = S_["QT"], S_["KT_all"], S_["V_all"]
    identb, eps = C["identb"], C["eps"]
    P = kb.prog()
    LAMV, nlam, SW, SUBW = A["LAMV"], A["nlam"], A["SW"], A["SUBW"]
    l2t, lsum = A["l2t"], A["lsum"]
    P.dve(lambda e: e.memset(lsum[:], 0.0), writes=[lsum])
    for a in range(2):
        P.dve(lambda e, a=a: e.tensor_tensor(out=l2t[:], in0=LAMV[:, j, 2 * a, :], in1=LAMV[:, j, 2 * a + 1, :], op=ALU.mult),
              reads=[LAMV], writes=[l2t])
        P.act(lambda e, a=a: e.activation(out=l2t[:], in_=l2t[:], func=AF.Copy, accum_out=lsum[:, a:a + 1]), reads=[l2t, lsum], writes=[l2t, (lsum.name, a)])
    P.act(lambda e: e.activation(out=lsum[:], in_=lsum[:], func=AF.Exp), reads=[lsum, (lsum.name, 0), (lsum.name, 1)], writes=[lsum])
    P.dve(lambda e: e.tensor_tensor(out=nlam[:], in0=lsum[:, 1:2], in1=lsum[:, 0:1], op=ALU.subtract), reads=[lsum], writes=[nlam])
    P.dve(lambda e: e.tensor_scalar(out=nlam[:], in0=nlam[:], scalar1=-lam_init, scalar2=None, op0=ALU.add), reads=[nlam], writes=[nlam])
    P.dve(lambda e: e.tensor_scalar(out=SW[:], in0=SUBW[:, j, :], scalar1=1.0 - lam_init, scalar2=None, op0=ALU.mult), reads=[SUBW], writes=[SW])
    P.emit()
    wqv = w_view(w_q)
    for tb in range(2):
        with contextlib.ExitStack() as st:
            HT = [kb.sb(st, "ht", [128, D], F32) for _ in range(8)]
            xT = kb.sb(st, "xT", [128, KD, 1024], BF16)
            with contextlib.ExitStack() as st1:
                P = kb.prog()
                for t in range(8):
                    r0 = (tb * 8 + t) * 128
                    P.dma("sp", HT[t][:], H[r0:r0 + 128, :], writes=hk(HT[t]))
                norm_T(kb, P, st1, C, HT, nvcol, xT, tr_ring(kb, st1))
                P.emit()
            with contextlib.ExitStack() as st2:
                P = kb.prog()
                wb = [kb.sb(st2, "wq", [128, KD, 512], BF16) for _ in range(2)]
                qst = Ring([kb.sb(st2, "qst", [128, 1024], BF16) for _ in range(2)])
                psa = Ring([kb.ps(st2, "psa", [128, 512]) for _ in range(4)])
                P.dma("pool", wb[0][:], wqv[:, :, 0:512], writes=[wb[0]])
                for blk in range(4):
                    b = blk % 2
                    if blk + 1 < 4:
                        P.dma("pool", wb[1 - b][:], wqv[:, :, (blk + 1) * 512:(blk + 2) * 512], writes=[wb[1 - b]])
                    for ci in range(4):
                        hm = blk * 4 + ci
                        qs = qst.next()
                        for tg in range(2):
                            pa = psa.next()
                            for k in range(KD):
                                P.pe(lambda e, pa=pa, k=k, ci=ci, tg=tg, b=b: e.matmul(pa[:], lhsT=wb[b][:, k, ci * 128:(ci + 1) * 128],
                                                                                     rhs=xT[:, k, tg * 512:(tg + 1) * 512],
                                                                                     start=(k == 0), stop=(k == KD - 1)),
                                     reads=[wb[b], (xT.name, k, tg)], writes=[pa])
                            P.act(lambda e, pa=pa, qs=qs, tg=tg: e.activation(out=qs[:, tg * 512:(tg + 1) * 512], in_=pa[:], func=AF.Copy,
                                                                            scale=128.0 ** -0.5),
                                  writes=[pa, (qs.name, tg)])
                        P.dma("sp", QT[hm][:, tb * 1024:(tb + 1) * 1024], qs[:], reads=[(qs.name, 0), (qs.name, 1)])
                P.emit()
    with contextlib.ExitStack() as sto:
        OATT = kb.sb(sto, "OATT", [128, NT, D], BF16)
        with contextlib.ExitStack() as st:
            KTh = [kb.sb(st, "KTh", [128, 2, T], BF16) for _ in range(2)]
            Vh = [kb.sb(st, "Vh", [128, 32, 257], BF16) for _ in range(2)]
            QTh = [kb.sb(st, "QTh", [128, 2, TO], BF16) for _ in range(2)]
            pT = Ring([kb.sb(st, "pT", [128, 128], BF16) for _ in range(6)])
            o1 = Ring([kb.sb(st, "o1", [128, 256], F32) for _ in range(2)])
            oo = Ring([kb.sb(st, "oo", [128, 256], F32) for _ in range(2)])
            jk = kb.sb(st, "jk", [128, 256], BF16)
            sm = Ring([kb.sb(st, "sm", [128, 8], F32) for _ in range(4)])
            sring = Ring([kb.ps(st, "pss", [128, 512]) for _ in range(3)])
            av = [[kb.ps(st, "av", [128, 512]) for _ in range(2)] for _ in range(2)]
            BT, MA, MB = A["BT"], A["MA"], A["MB"]
            P = kb.prog()
            for b in range(2):
                P.dve(lambda e, b=b: e.memset(Vh[b][:, :, 256:257], 1.0), writes=[(Vh[b].name, "one")])
            P.emit()
            for h in range(8):
                b = h % 2
                P = kb.prog()
                for m in range(2):
                    for r in range(2):
                        P.dma("sp", KTh[b][:, m, r * TO:(r + 1) * TO], KT_all[h, r * 256 + m * 128:r * 256 + (m + 1) * 128, :], writes=[(KTh[b].name, m, r)])
                    P.dma("sp", QTh[b][:, m, :], QT[2 * h + m], writes=[(QTh[b].name, m)])
                for r in range(2):
                    for jj in range(8):
                        kt0 = r * 16 + 2 * jj
                        P.dma("sp", Vh[b][:, kt0:kt0 + 2, 0:256],
                              V_all[jj, r * 256:(r + 1) * 256, h * 256:(h + 1) * 256].rearrange("(a p) e -> p a e", p=128),
                              writes=[(Vh[b].name, kt0 // 2)])

                def qblock(qb, h=h, b=b, P=P):
                    i0 = 2 * qb
                    nkt = 18 + 2 * qb
                    for kt in range(nkt):
                        for m in range(2):
                            ps = sring.next()
                            P.pe(lambda e, ps=ps, m=m, kt=kt: e.matmul(ps[:, 0:256], lhsT=KTh[b][:, m, kt * 128:(kt + 1) * 128],
                                                                     rhs=QTh[b][:, m, i0 * 128:(i0 + 2) * 128], start=True, stop=True),
                                 reads=[(KTh[b].name, m, kt // 16), (QTh[b].name, m)], writes=[ps])
                            for jj in range(2):
                                i = i0 + jj
                                if kt > 16 + i:
                                    continue
                                p_ = pT.next()
                                dd = i - kt + 16
                                P.act(lambda e, ps=ps, p_=p_, jj=jj, dd=dd: e.activation(out=p_[:], in_=ps[:, jj * 128:(jj + 1) * 128], func=AF.Exp,
                                                                                       bias=BT[:, h, dd:dd + 1], scale=1.0),
                                      reads=[BT], writes=[ps, p_])
                                if kt == i:
                                    P.dve(lambda e, p_=p_: e.tensor_tensor(out=p_[:], in0=p_[:], in1=MA[:], op=ALU.mult), reads=[MA], writes=[p_])
                                if kt == 16 + i:
                                    P.dve(lambda e, p_=p_: e.tensor_tensor(out=p_[:], in0=p_[:], in1=MB[:], op=ALU.mult), reads=[MB], writes=[p_])
                                acc = av[jj][m]
                                P.pe(lambda e, acc=acc, p_=p_, kt=kt, i=i: e.matmul(acc[:, 0:257], lhsT=p_[:], rhs=Vh[b][:, kt, :],
                                                                                  start=(kt == 0), stop=(kt == 16 + i)),
                                     reads=[p_, (Vh[b].name, kt // 2), (Vh[b].name, "one")], writes=[acc])
                    for jj in range(2):
                        i = i0 + jj
                        s_ = sm.next()
                        a0, a1 = av[jj][0], av[jj][1]
                        o1_, oo_ = o1.next(), oo.next()
                        P.dve(lambda e, s_=s_, a0=a0: e.reciprocal(out=s_[:, 0:1], in_=a0[:, 256:257]), writes=[a0, (s_.name, 0)])
                        P.dve(lambda e, s_=s_, a1=a1: e.reciprocal(out=s_[:, 1:2], in_=a1[:, 256:257]), writes=[a1, (s_.name, 1)])
                        P.dve(lambda e, s_=s_: e.tensor_tensor(out=s_[:, 2:3], in0=s_[:, 1:2], in1=nlam[:], op=ALU.mult),
                              reads=[(s_.name, 1), nlam], writes=[(s_.name, 2)])
                        P.dve(lambda e, s_=s_, a0=a0, o1_=o1_: e.tensor_scalar(out=o1_[:], in0=a0[:, 0:256], scalar1=s_[:, 0:1], scalar2=None, op0=ALU.mult),
                              reads=[(s_.name, 0)], writes=[a0, o1_])
                        P.dve(lambda e, s_=s_, a1=a1, o1_=o1_, oo_=oo_: e.scalar_tensor_tensor(out=oo_[:], in0=a1[:, 0:256], scalar=s_[:, 2:3], in1=o1_[:],
                                                                                             op0=ALU.mult, op1=ALU.add),
                              reads=[(s_.name, 2), o1_], writes=[a1, oo_])
                        P.dve(lambda e, s_=s_: e.memset(s_[:, 3:4], 0.0), writes=[(s_.name, 3)])
                        P.act(lambda e, s_=s_, oo_=oo_: e.activation(out=jk[:], in_=oo_[:], func=AF.Square, accum_out=s_[:, 3:4]),
                              reads=[oo_, (s_.name, 3)], writes=[jk, (s_.name, 4)])
                        P.act(lambda e, s_=s_: e.activation(out=s_[:, 5:6], in_=s_[:, 3:4], func=AF.Sqrt, bias=eps[:, 0:1], scale=1.0 / 256),
                              reads=[(s_.name, 4), eps], writes=[(s_.name, 5)])
                        P.dve(lambda e, s_=s_: e.reciprocal(out=s_[:, 6:7], in_=s_[:, 5:6]), reads=[(s_.name, 5)], writes=[(s_.name, 6)])
                        P.dve(lambda e, s_=s_, oo_=oo_, i=i: e.scalar_tensor_tensor(out=OATT[:, i, h * 256:(h + 1) * 256], in0=oo_[:], scalar=s_[:, 6:7],
                                                                                  in1=SW[:], op0=ALU.mult, op1=ALU.mult),
                              reads=[oo_, (s_.name, 6), SW], writes=[(OATT.name, i, h)])

                for qb in range(8):
                    qblock(qb)
                P.emit()
        wov = w_view(w_o)
        for tb in range(4):
            with contextlib.ExitStack() as st:
                HT = [kb.sb(st, "ht", [128, D], F32) for _ in range(4)]
                oT = kb.sb(st, "oT", [128, KD, 512], BF16)
                wo = [kb.sb(st, "wo", [128, KD, 512], BF16) for _ in range(2)]
                trr = Ring([kb.ps(st, "ptr3", [128, 8, 128], BF16) for _ in range(2)])
                psy = Ring([kb.ps(st, "psy", [128, 512]) for _ in range(4)])
                P = kb.prog()
                for t in range(4):
                    r0 = (tb * 4 + t) * 128
                    P.dma("sp", HT[t][:], H[r0:r0 + 128, :], writes=hk(HT[t]))
                for t in range(4):
                    i = tb * 4 + t
                    for c2 in range(2):
                        pt = trr.next()
                        for cc in range(8):
                            c = c2 * 8 + cc
                            P.pe(lambda e, pt=pt, cc=cc, c=c, i=i: e.transpose(out=pt[:, cc, :], in_=OATT[:, i, c * 128:(c + 1) * 128],
                                                                             identity=identb[:]),
                                 reads=[(OATT.name, i, c // 2), identb], writes=[pt])
                        P.act(lambda e, pt=pt, c2=c2, t=t: e.activation(out=oT[:, c2 * 8:(c2 + 1) * 8, t * 128:(t + 1) * 128],
                                                                      in_=pt[:], func=AF.Copy),
                              writes=[pt, (oT.name, c2, t)])
                P.dma("pool", wo[0][:], wov[:, :, 0:512], writes=[wo[0]])
                for db in range(4):
                    b = db % 2
                    if db + 1 < 4:
                        P.dma("pool", wo[1 - b][:], wov[:, :, (db + 1) * 512:(db + 2) * 512], writes=[wo[1 - b]])
                    for t in range(4):
                        py = psy.next()
                        for c in range(KD):
                            P.pe(lambda e, py=py, c=c, t=t, b=b: e.matmul(py[:], lhsT=oT[:, c, t * 128:(t + 1) * 128], rhs=wo[b][:, c, :],
                                                                        start=(c == 0), stop=(c == KD - 1)),
                                 reads=[wo[b], (oT.name, c // 8, t)], writes=[py])
                        hs_ = HT[t][:, db * 512:(db + 1) * 512]
                        P.dve(lambda e, hs_=hs_, py=py: e.tensor_tensor(out=hs_, in0=hs_, in1=py[:], op=ALU.add),
                              reads=[(HT[t].name, db)], writes=[py, (HT[t].name, db)])
                for t in range(4):
                    r0 = (tb * 4 + t) * 128
                    P.dma("sp", H[r0:r0 + 128, :], HT[t][:], reads=hk(HT[t]))
                P.emit()
```

```python
import numpy as np
import concourse.bass as bass
import concourse.mybir as mybir
from concourse.bass_utils import run_bass_kernel_spmd

F32 = mybir.dt.float32
BF16 = mybir.dt.bfloat16
AF = mybir.ActivationFunctionType
ALU = mybir.AluOpType
AX = mybir.AxisListType

ENGS = ("pe", "act", "dve", "pool", "sp")
DMA_RING = {"sp": 12, "pool": 12, "act": 6, "cc": 4}


class Op:
    __slots__ = ("eng", "fn", "reads", "writes", "dma", "waits", "signal", "sem", "count", "idx", "inc")

    def __init__(self, eng, fn, reads, writes, dma, inc=16):
        self.eng, self.fn, self.reads, self.writes, self.dma = eng, fn, reads, writes, dma
        self.inc = inc if dma else 1
        self.waits = []
        self.signal = False
        self.sem = None
        self.count = 0


def _key(r):
    return r if isinstance(r, (str, tuple, int)) else r.name


class Prog:
    def __init__(self, kb, same_engine_sync=True):
        self.kb = kb
        self.nc = kb.nc
        self.ops = []
        self.same_engine_sync = same_engine_sync

    def add(self, eng, fn, reads=(), writes=(), dma=False, inc=16):
        op = Op(eng, fn, tuple(_key(r) for r in reads), tuple(_key(w) for w in writes), dma, inc)
        op.idx = len(self.ops)
        self.ops.append(op)
        return op

    def pe(self, fn, reads=(), writes=()):
        return self.add("pe", fn, reads, writes)

    def act(self, fn, reads=(), writes=()):
        return self.add("act", fn, reads, writes)

    def dve(self, fn, reads=(), writes=()):
        return self.add("dve", fn, reads, writes)

    def pool(self, fn, reads=(), writes=()):
        return self.add("pool", fn, reads, writes)

    def dma(self, q, out, in_, reads=(), writes=(), **kw):
        return self.add(q, lambda e: e.dma_start(out=out, in_=in_, **kw), reads, writes, dma=True)

    def _schedule(self):
        for k_, op_ in enumerate(self.ops):
            op_.idx = k_
        last_write = {}
        readers = {}
        deps_of = []
        for op in self.ops:
            deps = set()
            for r in op.reads:
                lw = last_write.get(r)
                if lw is not None:
                    deps.add(lw)
            for w in op.writes:
                lw = last_write.get(w)
                if lw is not None:
                    deps.add(lw)
                for rd in readers.get(w, ()):
                    deps.add(rd)
            for r in op.reads:
                readers.setdefault(r, []).append(op.idx)
            for w in op.writes:
                last_write[w] = op.idx
                readers[w] = []
            deps.discard(op.idx)
            deps_of.append(deps)

        for op in self.ops:
            for j in deps_of[op.idx]:
                d = self.ops[j]
                if d.dma:
                    continue
                if d.eng == op.eng and not op.dma:
                    if op.eng == "pe" or not self.same_engine_sync:
                        continue
                d.signal = True

        eng_count = {e: 0 for e in ENGS}
        dma_n = {q: 0 for q in DMA_RING}
        persist = getattr(self.kb, "persist", None)
        if persist is None:
            persist = self.kb.persist = {}
        dma_slot_count = dict(persist)
        dma_prev = {}
        for op in self.ops:
            if op.dma:
                q = op.eng if op.inc != 1 else "cc"
                slot = dma_n[q] % DMA_RING[q]
                dma_n[q] += 1
                key = ("dma", q, slot)
                prev = dma_prev.get(key)
                if prev is not None:
                    deps_of[op.idx].add(prev)
                dma_slot_count[key] = dma_slot_count.get(key, 0) + op.inc
                op.sem, op.count, op.signal = key, dma_slot_count[key], True
                dma_prev[key] = op.idx
            elif op.signal:
                eng_count[op.eng] += 1
                op.sem, op.count = ("eng", op.eng), eng_count[op.eng]
        assert max(list(eng_count.values()) + list(dma_slot_count.values()) + [0]) < 60000, eng_count

        known = {e: {} for e in ENGS}
        snap = [None] * len(self.ops)
        for op in self.ops:
            kn = known[op.eng]
            need = {}
            for j in deps_of[op.idx]:
                d = self.ops[j]
                if not d.signal:
                    continue
                if d.eng == op.eng and not d.dma and not op.dma:
                    if op.eng == "pe" or not self.same_engine_sync:
                        continue
                if kn.get(d.sem, 0) >= d.count:
                    continue
                if need.get(d.sem, (0, None))[0] < d.count:
                    need[d.sem] = (d.count, j)
            for sem, (cnt, j) in need.items():
                op.waits.append((sem, cnt))
                if kn.get(sem, 0) < cnt:
                    kn[sem] = cnt
                sj = snap[j]
                if sj:
                    for s2, c2 in sj.items():
                        if kn.get(s2, 0) < c2:
                            kn[s2] = c2
            snap[op.idx] = dict(kn) if op.signal else None
        self.final_dma = {k: v for k, v in dma_slot_count.items() if v != persist.get(k, 0) or k[1] not in ("pool", "cc")}
        for k, v in dma_slot_count.items():
            if k[1] in ("pool", "cc"):
                persist[k] = v
        return sorted({op.sem for op in self.ops if op.signal}, key=str)

    def emit(self):
        nc = self.nc
        if not self.ops:
            return
        used = self._schedule()
        sem = self.kb.sem
        for key in used:
            if key not in sem:
                sem[key] = nc.alloc_semaphore("s_" + "_".join(str(k) for k in key))
        with nc.Block() as cb:
            def clr(e):
                for key in used:
                    if not (key[0] == "dma" and key[1] in ("pool", "cc")):
                        e.sem_clear(sem[key])
            cb.gpsimd(clr)
        by_eng = {e: [] for e in ENGS}
        for op in self.ops:
            by_eng[op.eng].append(op)
        final_dma = self.final_dma
        with nc.Block() as block:
            def section(ename):
                def body(e):
                    for op in by_eng[ename]:
                        for s, c in op.waits:
                            e.wait_ge(sem[s], c)
                        ins = op.fn(e)
                        if op.signal:
                            ins.then_inc(sem[op.sem], op.inc)
                    if ename == "sp":
                        for key, cnt in final_dma.items():
                            e.wait_ge(sem[key], cnt)
                return body

            block.tensor(section("pe"))
            block.scalar(section("act"))
            block.vector(section("dve"))
            block.gpsimd(section("pool"))
            block.sync(section("sp"))
        self.ops = []


D = 2048
T = 4096
TO = 2048
NT = TO // 128
KD = D // 128
FF = 8192
DEPTH = 4
N_A = 2
EPS = 1e-6
HL = 16
HQ = 8
NCOLS_IN = 6176
NVC = 13 * 16 + 2
BIG = 30000.0
import os
DEBUG_CUT = int(os.environ.get('MK_DEBUG_CUT', '0'))
SUBCUT = int(os.environ.get('MK_SUBCUT', '9'))


class KB:
    def __init__(self):
        self.nc = bass.Bass("TRN2", target_bir_lowering=False)
        self.sem = {}
        self.uid = 0
        self.rr = {}

    def din(self, name, shape, dt=F32):
        return self.nc.dram_tensor(name, list(shape), dt, kind="ExternalInput").ap()

    def dscr(self, name, shape, dt):
        return self.nc.dram_tensor(name, list(shape), dt, kind="Internal").ap()

    def sb(self, st, name, shape, dt):
        self.uid += 1
        return st.enter_context(self.nc.sbuf_tensor(f"{name}_{self.uid}", list(shape), dt))

    def ps(self, st, name, shape, dt=F32):
        self.uid += 1
        return st.enter_context(self.nc.psum_tensor(f"{name}_{self.uid}", list(shape), dt))

    def prog(self):
        return Prog(self)


PAR = [0]


class PRing:
    def __init__(self, a, b):
        self.r = (a, b)

    def next(self):
        return self.r[PAR[0]].next()


class Ring:
    def __init__(self, bufs):
        self.bufs = bufs
        self.i = 0

    def next(self):
        b = self.bufs[self.i % len(self.bufs)]
        self.i += 1
        return b


def w_view(w2d):
    return w2d.rearrange("(k p) n -> p k n", p=128)


def hk(ht):
    return [(ht.name, d) for d in range(4)]


def norm_T(kb, P, st, C, hts, nvcol, xT, ps_tr):
    n = len(hts)
    ss = kb.sb(st, "ss", [128, n], F32)
    sd = kb.sb(st, "sd", [128, n], F32)
    rstd = kb.sb(st, "rstd", [128, n], F32)
    junk = kb.sb(st, "junk", [128, D], BF16)
    xs = [kb.sb(st, "xs", [128, D], BF16) for _ in range(4)]
    NV, identb = C["NV"], C["identb"]
    P.dve(lambda e: e.memset(ss[:], 0.0), writes=[ss])
    for t, h in enumerate(hts):
        P.act(lambda e, h=h, t=t: e.activation(out=junk[:], in_=h[:], func=AF.Square, accum_out=ss[:, t:t + 1]),
              reads=hk(h) + [ss], writes=[junk, (ss.name, t)])
    P.act(lambda e: e.activation(out=sd[:], in_=ss[:], func=AF.Sqrt, bias=C["eps"][:, 0:1], scale=1.0 / D),
          reads=[ss] + [(ss.name, t) for t in range(n)], writes=[sd])
    P.dve(lambda e: e.reciprocal(out=rstd[:], in_=sd[:]), reads=[sd], writes=[rstd])
    cnt = 0
    for g in range(n // 4):
        for j in range(4):
            t = g * 4 + j
            P.act(lambda e, t=t, j=j: e.activation(out=xs[j][:], in_=hts[t][:], func=AF.Copy, scale=rstd[:, t:t + 1]),
                  reads=hk(hts[t]) + [rstd], writes=[xs[j]])
        for k2 in range(KD // 2):
            pt = ps_tr.next()
            for kk in range(2):
                k = k2 * 2 + kk
                for j in range(4):
                    c0 = kk * 512 + j * 128
                    P.pe(lambda e, pt=pt, j=j, k=k, c0=c0: e.transpose(out=pt[:, c0:c0 + 128], in_=xs[j][:, k * 128:(k + 1) * 128],
                                                                     identity=identb[:]),
                         reads=[xs[j], identb], writes=[pt])
            for kk in range(2):
                k = k2 * 2 + kk
                dst = xT[:, k, g * 512:(g + 1) * 512]
                src = pt[:, kk * 512:(kk + 1) * 512]
                if cnt % 2 == 0:
                    P.dve(lambda e, dst=dst, src=src, k=k: e.tensor_scalar(out=dst, in0=src, scalar1=NV[:, nvcol + k:nvcol + k + 1],
                                                                         scalar2=None, op0=ALU.mult),
                          reads=[NV], writes=[pt, (xT.name, k, g)])
                else:
                    P.act(lambda e, dst=dst, src=src, k=k: e.activation(out=dst, in_=src, func=AF.Copy,
                                                                      scale=NV[:, nvcol + k:nvcol + k + 1]),
                          reads=[NV], writes=[pt, (xT.name, k, g)])
            cnt += 1


def tr_ring(kb, st, nbank=2):
    return Ring([kb.ps(st, "ptr", [128, 1024], BF16) for _ in range(nbank)])


def xT_keys(xT, g):
    return [(xT.name, k, g) for k in range(KD)]


def mlp_pass(kb, C, H, w1, w2, nvcol):
    import contextlib
    for tb in range(int(os.environ.get("MK_NTB", "2"))):
        with contextlib.ExitStack() as st:
            HT = [kb.sb(st, "ht", [128, D], F32) for _ in range(8)]
            hnT = kb.sb(st, "hnT", [128, KD, 1024], BF16)
            with contextlib.ExitStack() as st1:
                P = kb.prog()
                for t in range(8):
                    r0 = (tb * 8 + t) * 128
                    P.dma("sp", HT[t][:], H[r0:r0 + 128, :], writes=hk(HT[t]))
                norm_T(kb, P, st1, C, HT, nvcol, hnT, tr_ring(kb, st1))
                P.emit()
            if DEBUG_CUT == 1 or (tb == 1 and os.environ.get("MK_SKIPB2")):
                continue
            with contextlib.ExitStack() as st2:
                P = kb.prog()
                gT = [kb.sb(st2, "gT", [128, 4, 1024], BF16) for _ in range(2)]
                w1b = [kb.sb(st2, "w1b", [128, KD, 512], BF16) for _ in range(2)]
                w2b = [kb.sb(st2, "w2b", [128, 4, D], BF16) for _ in range(2)]
                rt = Ring([kb.sb(st2, "rt", [128, 512], F32) for _ in range(2)])
                psu = Ring([kb.ps(st2, "psu", [128, 512]) for _ in range(3)])
                psy = Ring([kb.ps(st2, "psy", [128, 512]) for _ in range(3)])
                w1v = w_view(w1)

                def loadw(fb):
                    b = fb % 2
                    P.dma("pool", w1b[b][:], w1v[:, :, fb * 512:(fb + 1) * 512], writes=[w1b[b]])
                    if DEBUG_CUT != 2:
                        P.dma("pool", w2b[b][:], w2[fb * 512:(fb + 1) * 512, :].rearrange("(c p) n -> p c n", p=128), writes=[w2b[b]])

                def stage1(fb):
                    b = fb % 2
                    for fc in range(4):
                        for tg in range(2):
                            pu = psu.next()
                            for k in range(KD if SUBCUT >= 0 else 0):
                                P.pe(lambda e, pu=pu, k=k, fc=fc, tg=tg: e.matmul(pu[:], lhsT=w1b[b][:, k, fc * 128:(fc + 1) * 128],
                                                                                 rhs=hnT[:, k, tg * 512:(tg + 1) * 512],
                                                                                 start=(k == 0), stop=(k == KD - 1)),
                                     reads=[w1b[b], (hnT.name, k, tg)], writes=[pu])
                            r = rt.next()
                            if SUBCUT >= 1:
                                P.act(lambda e, pu=pu, r=r: e.activation(out=r[:], in_=pu[:], func=AF.Relu), reads=[], writes=[pu, r])
                            if SUBCUT >= 2:
                                P.act(lambda e, r=r, fc=fc, tg=tg: e.activation(out=gT[b][:, fc, tg * 512:(tg + 1) * 512], in_=r[:], func=AF.Square),
                                      reads=[r], writes=[(gT[b].name, fc, tg)])

                def stage2(fb):
                    b = fb % 2
                    if DEBUG_CUT == 2:
                        return
                    for t in range(8):
                        for db in range(4):
                            py = psy.next()
                            for fc in range(4):
                                P.pe(lambda e, py=py, fc=fc, t=t, db=db: e.matmul(py[:], lhsT=gT[b][:, fc, t * 128:(t + 1) * 128],
                                                                                 rhs=w2b[b][:, fc, db * 512:(db + 1) * 512],
                                                                                 start=(fc == 0), stop=(fc == 3)),
                                     reads=[w2b[b], (gT[b].name, fc, t // 4)], writes=[py])
                            hs = HT[t][:, db * 512:(db + 1) * 512]
                            P.dve(lambda e, hs=hs, py=py: e.tensor_tensor(out=hs, in0=hs, in1=py[:], op=ALU.add),
                                  reads=[(HT[t].name, db)], writes=[py, (HT[t].name, db)])

                NFB = int(os.environ.get("MK_NFB", "16"))
                loadw(0)
                stage1(0)
                for fb in range(NFB):
                    if fb + 1 < NFB:
                        loadw(fb + 1)
                        stage1(fb + 1)
                    stage2(fb)
                for t in range(8):
                    r0 = (tb * 8 + t) * 128
                    P.dma("sp", H[r0:r0 + 128, :], HT[t][:], reads=hk(HT[t]))
                P.emit()


def ple_pass(kb, C, H, p_l, wple, wgate, nvcol):
    import contextlib
    for tb in range(2):
        with contextlib.ExitStack() as st:
            HT = [kb.sb(st, "ht", [128, D], F32) for _ in range(8)]
            hpT = kb.sb(st, "hpT", [128, KD, 1024], BF16)
            pT = kb.sb(st, "pT", [128, 2, 1024], BF16)
            with contextlib.ExitStack() as st1:
                P = kb.prog()
                for t in range(8):
                    r0 = (tb * 8 + t) * 128
                    P.dma("sp", HT[t][:], H[r0:r0 + 128, :], writes=hk(HT[t]))
                trr = tr_ring(kb, st1)
                norm_T(kb, P, st1, C, HT, nvcol, hpT, trr)
                pf = Ring([kb.sb(st1, "pf", [128, 256], F32) for _ in range(2)])
                pb = Ring([kb.sb(st1, "pb", [128, 256], BF16) for _ in range(2)])
                for t in range(8):
                    r0 = (tb * 8 + t) * 128
                    f, b_ = pf.next(), pb.next()
                    P.dma("sp", f[:], p_l[r0:r0 + 128, :], writes=[f])
                    P.act(lambda e, f=f, b_=b_: e.activation(out=b_[:], in_=f[:], func=AF.Copy), reads=[f], writes=[b_])
                    pt = trr.next()
                    for c in range(2):
                        P.pe(lambda e, pt=pt, c=c, b_=b_: e.transpose(out=pt[:, c * 128:(c + 1) * 128],
                                                                    in_=b_[:, c * 128:(c + 1) * 128], identity=C["identb"][:]),
                             reads=[b_, C["identb"]], writes=[pt])
                    for c in range(2):
                        P.dve(lambda e, pt=pt, c=c, t=t: e.tensor_copy(out=pT[:, c, t * 128:(t + 1) * 128],
                                                                     in_=pt[:, c * 128:(c + 1) * 128]),
                              reads=[], writes=[pt, (pT.name, c, t)])
                P.emit()
            with contextlib.ExitStack() as st2:
                P = kb.prog()
                wg = [kb.sb(st2, "wg", [128, KD, 512], BF16) for _ in range(2)]
                wp = kb.sb(st2, "wp", [128, 2, D], BF16)
                gs = Ring([kb.sb(st2, "gs", [128, 512], F32) for _ in range(2)])
                tm = Ring([kb.sb(st2, "tm", [128, 512], F32) for _ in range(2)])
                psg = Ring([kb.ps(st2, "psg", [128, 512]) for _ in range(3)])
                psp = Ring([kb.ps(st2, "psp", [128, 512]) for _ in range(3)])
                P.dma("pool", wp[:], wple.rearrange("(c p) n -> p c n", p=128), writes=[wp])
                wgv = w_view(wgate)
                P.dma("pool", wg[0][:], wgv[:, :, 0:512], writes=[wg[0]])
                for db in range(4):
                    b = db % 2
                    if db + 1 < 4:
                        P.dma("pool", wg[1 - b][:], wgv[:, :, (db + 1) * 512:(db + 2) * 512], writes=[wg[1 - b]])
                    for t in range(8):
                        pg, pp = psg.next(), psp.next()
                        for k in range(KD):
                            P.pe(lambda e, pg=pg, k=k, t=t, b=b: e.matmul(pg[:], lhsT=hpT[:, k, t * 128:(t + 1) * 128], rhs=wg[b][:, k, :],
                                                                        start=(k == 0), stop=(k == KD - 1)),
                                 reads=[wg[b], (hpT.name, k, t // 4)], writes=[pg])
                        for c in range(2):
                            P.pe(lambda e, pp=pp, c=c, t=t, db=db: e.matmul(pp[:], lhsT=pT[:, c, t * 128:(t + 1) * 128],
                                                                          rhs=wp[:, c, db * 512:(db + 1) * 512], start=(c == 0), stop=(c == 1)),
                                 reads=[wp, (pT.name, c, t)], writes=[pp])
                        g_, m_ = gs.next(), tm.next()
                        P.act(lambda e, g_=g_, pg=pg: e.activation(out=g_[:], in_=pg[:], func=AF.Sigmoid), reads=[], writes=[pg, g_])
                        P.dve(lambda e, m_=m_, pp=pp, g_=g_: e.tensor_tensor(out=m_[:], in0=pp[:], in1=g_[:], op=ALU.mult),
                              reads=[g_], writes=[pp, m_])
                        hs = HT[t][:, db * 512:(db + 1) * 512]
                        P.dve(lambda e, hs=hs, m_=m_: e.tensor_tensor(out=hs, in0=hs, in1=m_[:], op=ALU.add),
                               reads=[m_, (HT[t].name, db)], writes=[(HT[t].name, db)])
                for t in range(8):
                    r0 = (tb * 8 + t) * 128
                    P.dma("sp", H[r0:r0 + 128, :], HT[t][:], reads=hk(HT[t]))
                P.emit()


def final_pass(kb, C, H, out, fnorm):
    import contextlib
    with contextlib.ExitStack() as st:
        P = kb.prog()
        FN = kb.sb(st, "FN", [128, D], F32)
        P.dma("sp", FN[:], fnorm, writes=[FN])
        hts = Ring([kb.sb(st, "ht", [128, D], F32) for _ in range(3)])
        ots = Ring([kb.sb(st, "ot", [128, D], F32) for _ in range(2)])
        junk = kb.sb(st, "junk", [128, D], BF16)
        ss = kb.sb(st, "ss", [128, NT], F32)
        sd = kb.sb(st, "sd", [128, NT], F32)
        rs = kb.sb(st, "rs", [128, NT], F32)
        P.dve(lambda e: e.memset(ss[:], 0.0), writes=[ss])
        for t in range(NT):
            h, o = hts.next(), ots.next()
            P.dma("sp", h[:], H[t * 128:(t + 1) * 128, :], writes=[h])
            P.act(lambda e, h=h, t=t: e.activation(out=junk[:], in_=h[:], func=AF.Square, accum_out=ss[:, t:t + 1]),
                  reads=[h, ss], writes=[junk, (ss.name, t)])
            P.act(lambda e, t=t: e.activation(out=sd[:, t:t + 1], in_=ss[:, t:t + 1], func=AF.Sqrt, bias=C["eps"][:, 0:1], scale=1.0 / D),
                  reads=[(ss.name, t)], writes=[(sd.name, t)])
            P.dve(lambda e, t=t: e.reciprocal(out=rs[:, t:t + 1], in_=sd[:, t:t + 1]), reads=[(sd.name, t)], writes=[(rs.name, t)])
            P.dve(lambda e, h=h, o=o, t=t: e.scalar_tensor_tensor(out=o[:], in0=h[:], scalar=rs[:, t:t + 1], in1=FN[:],
                                                                op0=ALU.mult, op1=ALU.mult),
                  reads=[h, (rs.name, t), FN], writes=[o])
            P.dma("sp", out[t * 128:(t + 1) * 128, :], o[:], reads=[o])
        P.emit()


def build(stages):
    import contextlib
    kb = KB()
    nc = kb.nc
    x_own = kb.din("x_own", [TO, D])
    out = nc.dram_tensor("out", [TO, D], F32, kind="ExternalOutput").ap()
    H = kb.dscr("H", [TO, D], F32)
    cf = kb.din("cf", [128, 5 * 128])
    nv = kb.din("nv", [128, NVC])
    used = {"x_own", "cf", "nv"}
    ins = {}

    def inp(name, shape):
        if name not in ins:
            ins[name] = kb.din(name, shape)
            used.add(name)
        return ins[name]

    with contextlib.ExitStack() as gst:
        C = {}
        CF = kb.sb(gst, "CF", [128, 5, 128], F32)
        C["CF"] = CF
        C["identb"] = kb.sb(gst, "identb", [128, 128], BF16)
        C["onesb"] = kb.sb(gst, "onesb", [128, 128], BF16)
        C["NV"] = kb.sb(gst, "NV", [128, NVC], F32)
        C["eps"] = kb.sb(gst, "eps", [128, 1], F32)
        P = kb.prog()
        P.dma("sp", CF[:], cf.rearrange("p (a b) -> p a b", a=5), writes=[CF])
        P.dma("sp", C["NV"][:], nv, writes=[C["NV"]])
        P.dve(lambda e: e.tensor_copy(out=C["identb"][:], in_=CF[:, 0, :]), reads=[CF], writes=[C["identb"]])
        P.dve(lambda e: e.tensor_copy(out=C["onesb"][:], in_=CF[:, 4, :]), reads=[CF], writes=[C["onesb"]])
        P.dve(lambda e: e.memset(C["eps"][:], EPS), writes=[C["eps"]])
        for i in range(4):
            P.dma("sp", H[i * 512:(i + 1) * 512, :], x_own[i * 512:(i + 1) * 512, :], writes=[("H", i)])
        P.emit()

        C["one"] = kb.sb(gst, "one", [128, 1], F32)
        P = kb.prog()
        P.dve(lambda e: e.memset(C["one"][:], 1.0), writes=[C["one"]])
        P.emit()
        G = None
        S_ = {}

        def gdn_setup():
            nonlocal G
            if G is not None:
                return
            G = {}
            G["CW"] = kb.sb(gst, "CW", [128, 2, 32, 4], F32)
            G["AB"] = kb.sb(gst, "AB", [128, 2, 2, 16], F32)
            G["BG"] = kb.sb(gst, "BG", [128, 32, 32], F32)
            G["NEA"] = kb.sb(gst, "NEA", [128, 16], F32)
            G["SEL16"] = kb.sb(gst, "SEL16", [16, 16, 128], F32)
            G["SELC"] = kb.sb(gst, "SELC", [128, 2], F32)
            P = kb.prog()
            P.dma("sp", G["CW"][:], inp("convw", [128, 256]).rearrange("p (l c j) -> p l c j", l=2, c=32), writes=[G["CW"]])
            P.dma("sp", G["AB"][:], inp("ab", [128, 64]).rearrange("p (l a h) -> p l a h", l=2, a=2), writes=[G["AB"]])
            P.dma("sp", G["SEL16"][:], inp("sel16", [16, 2048]).rearrange("p (h m) -> p h m", h=16), writes=[G["SEL16"]])
            P.dma("sp", G["SELC"][:], inp("selc", [128, 2]), writes=[G["SELC"]])
            G["lvlmask"] = inp("lvlmask", [128, 7680])
            P.emit()
            S_["XNT_own"] = kb.dscr("XNT_own", [2048, TO], BF16)
            S_["XNT_all"] = kb.dscr("XNT_all", [8, 512, TO], BF16)
            S_["QKVZ"] = kb.dscr("QKVZ", [48, 128, T], BF16)
            S_["OT_lo"] = kb.dscr("OT_lo", [2048, TO], BF16)
            S_["OT_hi"] = kb.dscr("OT_hi", [2048, TO], BF16)
            S_["G_lo"] = kb.dscr("G_lo", [8, 512, TO], BF16)
            S_["G_hi"] = kb.dscr("G_hi", [8, 512, TO], BF16)

        A = {}

        def att_setup():
            if A:
                return
            A["LAMV"] = kb.sb(gst, "LAMV", [128, 2, 4, 128], F32)
            A["SUBW"] = kb.sb(gst, "SUBW", [128, 2, 256], F32)
            A["SW"] = kb.sb(gst, "SW", [128, 256], F32)
            A["nlam"] = kb.sb(gst, "nlam", [128, 1], F32)
            A["l2t"] = kb.sb(gst, "l2t", [128, 128], F32)
            A["lsum"] = kb.sb(gst, "lsum", [128, 2], F32)
            A["BT"] = kb.sb(gst, "BT", [128, 8, 32], F32)
            A["MA"] = kb.sb(gst, "MA", [128, 128], BF16)
            A["MB"] = kb.sb(gst, "MB", [128, 128], BF16)
            P = kb.prog()
            P.dma("sp", A["LAMV"][:], inp("lamv", [128, 1024]).rearrange("p (l a d) -> p l a d", l=2, a=4), writes=[A["LAMV"]])
            P.dma("sp", A["SUBW"][:], inp("sublnw", [128, 512]).rearrange("p (l d) -> p l d", l=2), writes=[A["SUBW"]])
            P.dma("sp", A["BT"][:], inp("bt", [128, 256]).rearrange("p (h d) -> p h d", h=8), writes=[A["BT"]])
            mab = inp("mab", [128, 256])
            P.dma("pool", A["MA"][:], mab[:, 0:128], writes=[A["MA"]])
            P.dma("pool", A["MB"][:], mab[:, 128:256], writes=[A["MB"]])
            P.emit()
            S_["KT_own"] = kb.dscr("KT_own", [2048, TO], BF16)
            S_["KT_all"] = kb.dscr("KT_all", [8, 512, TO], BF16)
            S_["V_own"] = kb.dscr("V_own", [TO, 2048], BF16)
            S_["V_all"] = kb.dscr("V_all", [8, 512, 2048], BF16)
            S_["QT"] = kb.dscr("QT", [16, 128, TO], BF16)

        for stg in stages:
            kind, li = stg[:-1], int(stg[-1]) if stg[-1].isdigit() else None
            if kind == "gdn":
                gdn_setup()
                S_["out"] = out
                gi = {"w_in": [inp(f"w_in_{l}", [D, NCOLS_IN]) if l == li else None for l in range(2)],
                      "w_out": [inp(f"w_out_{l}", [4096, D]) if l == li else None for l in range(2)]}
                gdn_layer(kb, C, G, li, H, S_, gi)
            elif stg == "kv":
                att_setup()
                kv_pass(kb, C, H, inp("w_kv", [D, 4096]), 12 * 16, S_["KT_own"], S_["V_own"])
                allgather(kb, S_["KT_own"], S_["KT_all"])
                allgather(kb, S_["V_own"], S_["V_all"])
            elif kind == "att":
                att_setup()
                attn_layer(kb, C, A, li, H, S_, inp(f"w_q_{li - 2}", [D, D]), inp(f"w_o_{li - 2}", [D, D]), li * 16)
            elif kind == "mlp":
                w1 = inp(f"mlp_w1_{li}", [D, FF])
                w2 = inp(f"mlp_w2_{li}", [FF, D])
                mlp_pass(kb, C, H, w1, w2, (4 + li) * 16)
            elif kind == "ple":
                p_own = inp("p_own", [DEPTH, TO, 256])
                wple = inp(f"ple_w_proj_{li}", [256, D])
                wgate = inp(f"ple_w_gate_{li}", [D, D])
                ple_pass(kb, C, H, p_own[li], wple, wgate, (8 + li) * 16)
            elif stg == "final":
                final_pass(kb, C, H, out, inp("fnorm", [128, D]))
            elif stg == "outh":
                P = kb.prog()
                for i in range(4):
                    P.dma("sp", out[i * 512:(i + 1) * 512, :], H[i * 512:(i + 1) * 512, :])
                P.emit()
            else:
                raise ValueError(stg)
    return nc, sorted(used)


def _fm(v):
    return np.ascontiguousarray(np.asarray(v, np.float32).reshape(16, 128).T)


def host_consts():
    i = np.arange(128)
    ident = np.eye(128, dtype=np.float32)
    U = (i[:, None] <= i[None, :]).astype(np.float32)
    maskl = np.where(i[:, None] > i[None, :], 0.0, BIG).astype(np.float32)
    masklt = np.where(i[None, :] >= i[:, None], 0.0, BIG).astype(np.float32)
    ones = np.ones((128, 128), np.float32)
    return np.concatenate([ident, U, maskl, masklt, ones], axis=1)


def prep_inputs(inputs, used):
    g = {k: np.asarray(v) for k, v in inputs.items()}
    cf = host_consts()
    nvl = [_fm(g["norm_mix"][i]) for i in range(4)] + [_fm(g["norm_mlp"][i]) for i in range(4)] + \
          [_fm(g["norm_ple"][i]) for i in range(4)] + [_fm(g["kv_norm"])]
    nvl.append(np.ascontiguousarray(g["gdn_norm_w"].astype(np.float32).T))
    nv = np.ascontiguousarray(np.concatenate(nvl, axis=1))
    fnorm = np.ascontiguousarray(np.broadcast_to(g["final_norm"].astype(np.float32)[None, :], (128, D)))
    maps = []
    for c in range(8):
        b, s = c // 2, c % 2
        m = {}
        m["x_own"] = np.ascontiguousarray(g["x"][b, s * TO:(s + 1) * TO])
        m["p_own"] = np.ascontiguousarray(g["p"][:, b, s * TO:(s + 1) * TO, :])
        m["cf"] = cf
        m["nv"] = nv
        m["fnorm"] = fnorm
        pp = np.arange(128, dtype=np.float32)
        bt = np.zeros((128, 8, 32), np.float32)
        for hh in range(8):
            slope = 2.0 ** -(hh + 1)
            for dd in range(32):
                dg = dd - 16 + 16 * s
                bt[:, hh, dd] = slope * (pp - 127.0 - 128.0 * dg) if dg >= 0 else -BIG
        m["bt"] = bt.reshape(128, 256)
        tri = (pp[None, :] >= pp[:, None]).astype(np.float32)
        mab = np.zeros((128, 256), np.float32)
        mab[:, 0:128] = tri if s == 0 else 1.0
        mab[:, 128:256] = tri if s == 1 else 0.0
        m["mab"] = mab
        lamv = np.stack([np.stack([g["diff_lambda_q1"][jj], g["diff_lambda_k1"][jj], g["diff_lambda_q2"][jj], g["diff_lambda_k2"][jj]]) for jj in range(2)])
        m["lamv"] = np.ascontiguousarray(np.broadcast_to(lamv.astype(np.float32).reshape(1, 1024), (128, 1024)))
        m["sublnw"] = np.ascontiguousarray(np.broadcast_to(g["diff_subln_w"].astype(np.float32).reshape(1, 512), (128, 512)))
        m["w_kv"] = g["w_kv"]
        for jj in range(2):
            m[f"w_q_{jj}"] = g["diff_w_q"][jj]
            m[f"w_o_{jj}"] = g["diff_w_o"][jj]
        sel16 = np.zeros((16, 16, 128), np.float32)
        for hh in range(16):
            sel16[hh, hh, :] = 1.0
        m["sel16"] = sel16.reshape(16, 2048)
        ii = np.arange(128)
        lm = np.zeros((128, 15, 4, 128), np.float32)
        for l in range(7):
            bsz = 2 ** l
            mk = (((ii[:, None] // bsz) % 2 == 1) & ((ii[None, :] // bsz) == (ii[:, None] // bsz) - 1)).astype(np.float32)
            lm[:, l, :, :] = mk[:, None, :]
            lm[:, 7 + l, :, :] = mk.T[:, None, :]
        lm[:, 14, :, :] = np.eye(128, dtype=np.float32)[:, None, :]
        m["lvlmask"] = lm.reshape(128, 7680)
        selc = np.zeros((128, 2), np.float32)
        selc[:, s] = 1.0
        m["selc"] = selc
        cw = np.zeros((128, 2, 32, 4), np.float32)
        ab = np.zeros((128, 2, 2, 16), np.float32)
        for l in range(2):
            if f"w_in_{l}" in used:
                wi = g["gdn_w_in"][l]
                m[f"w_in_{l}"] = np.ascontiguousarray(np.concatenate([
                    wi[:, s * 1024:(s + 1) * 1024], wi[:, 2048 + s * 1024:2048 + (s + 1) * 1024],
                    wi[:, 4096 + s * 2048:4096 + (s + 1) * 2048], wi[:, 8192 + s * 2048:8192 + (s + 1) * 2048],
                    wi[:, 12288 + s * 16:12288 + (s + 1) * 16], wi[:, 12320 + s * 16:12320 + (s + 1) * 16]], axis=1))
            if f"w_out_{l}" in used:
                m[f"w_out_{l}"] = g["gdn_w_out"][l]
            cwl = g["gdn_conv_w"][l]
            own = np.concatenate([cwl[:, s * 1024:(s + 1) * 1024], cwl[:, 2048 + s * 1024:2048 + (s + 1) * 1024],
                                  cwl[:, 4096 + s * 2048:4096 + (s + 1) * 2048]], axis=1)
            cw[:, l] = own.reshape(4, 32, 128).transpose(2, 1, 0)
            ab[:, l, 0, :] = g["gdn_a_log"][l, s * 16:(s + 1) * 16][None, :]
            ab[:, l, 1, :] = g["gdn_dt_bias"][l, s * 16:(s + 1) * 16][None, :]
        m["convw"] = cw.reshape(128, 256)
        m["ab"] = ab.reshape(128, 64)
        for k in ("mlp_w1", "mlp_w2", "ple_w_proj", "ple_w_gate"):
            for li in range(DEPTH):
                if f"{k}_{li}" in used:
                    m[f"{k}_{li}"] = g[k][li]
        maps.append({k: v for k, v in m.items() if k in used})
    return maps


ALL_STAGES = ["gdn0", "mlp0", "ple0", "gdn1", "mlp1", "ple1", "kv", "att2", "mlp2", "ple2", "att3", "mlp3", "ple3", "final"]
_CACHE = {}


def run_stages(inputs, stages):
    key = tuple(stages)
    if key not in _CACHE:
        _CACHE[key] = build(stages)
    nc, used = _CACHE[key]
    maps = prep_inputs(inputs, set(used))
    res = run_bass_kernel_spmd(nc, maps, core_ids=list(range(8)))
    outs = [r["out"] for r in res.results]
    full = np.empty((4, T, D), np.float32)
    for c in range(8):
        full[c // 2, (c % 2) * TO:(c % 2 + 1) * TO] = outs[c]
    return full


def kernel(**inputs):
    return run_stages(inputs, ALL_STAGES)


def gdn_norm_gather(kb, C, H, nvcol, XNT_own, XNT_all):
    import contextlib
    xv = XNT_own.rearrange("(k p) t -> p k t", p=128)
    for tb in range(2):
        with contextlib.ExitStack() as st:
            HT = [kb.sb(st, "ht", [128, D], F32) for _ in range(8)]
            xT = kb.sb(st, "xT", [128, KD, 1024], BF16)
            P = kb.prog()
            for t in range(8):
                r0 = (tb * 8 + t) * 128
                P.dma("sp", HT[t][:], H[r0:r0 + 128, :], writes=hk(HT[t]))
            norm_T(kb, P, st, C, HT, nvcol, xT, tr_ring(kb, st))
            for kh in range(2):
                P.dma("sp", xv[:, kh * 8:(kh + 1) * 8, tb * 1024:(tb + 1) * 1024], xT[:, kh * 8:(kh + 1) * 8, :],
                      reads=[(xT.name, k, g) for k in range(kh * 8, kh * 8 + 8) for g in range(2)])
            P.emit()
    allgather(kb, XNT_own, XNT_all)


PAIRS = [[0, 1], [2, 3], [4, 5], [6, 7]]


def allgather(kb, src, dst, nch=8):
    R = src.shape[0]
    rc = R // nch
    P = kb.prog()
    for j in range(nch):
        P.add("pool", lambda e, j=j: e.collective_compute("AllGather", ALU.bypass, replica_groups=PAIRS,
                                                        ins=[src[j * rc:(j + 1) * rc, :].opt()], outs=[dst[j].opt()]),
              dma=True, inc=1)
    P.emit()


def gdn_inproj(kb, C, G, li, XNT_all, w_in, QKVZ):
    import contextlib
    with contextlib.ExitStack() as st:
        XT = kb.sb(st, "XT", [128, KD, T], BF16)
        wb = [kb.sb(st, "wb", [128, KD, 256], BF16) for _ in range(2)]
        obuf = [kb.sb(st, "ob", [128, T], BF16) for _ in range(2)]
        xbuf = [kb.sb(st, "xb", [128, 515], F32) for _ in range(2)]
        cacc = Ring([kb.sb(st, "ca", [128, 512], F32) for _ in range(2)])
        sl = Ring([kb.sb(st, "sl", [128, 512], F32) for _ in range(2)])
        sq = Ring([kb.sb(st, "sq", [128, 512], BF16) for _ in range(2)])
        sd = Ring([kb.sb(st, "sd", [128, 512], F32) for _ in range(2)])
        ri = Ring([kb.sb(st, "ri", [128, 512], F32) for _ in range(2)])
        wba = kb.sb(st, "wba", [128, KD, 32], BF16)
        xa = kb.sb(st, "xa", [128, 16], F32)
        ea = kb.sb(st, "ea", [128, 16], F32)
        sp_ = kb.sb(st, "sp", [128, 16], F32)
        psa = Ring([kb.ps(st, "psa", [128, 512]) for _ in range(3)])
        pss = Ring([kb.ps(st, "pss", [128, 512]) for _ in range(2)])
        psb = Ring([kb.ps(st, "psb", [128, 512]) for _ in range(2)])
        CW, AB, BG, NEA = G["CW"], G["AB"], G["BG"], G["NEA"]
        onesb, eps = C["onesb"], C["eps"]
        P = kb.prog()
        for r in range(2):
            for j in range(8):
                P.dma("sp", XT[:, 2 * j:2 * j + 2, r * TO:(r + 1) * TO],
                      XNT_all[j, r * 256:(r + 1) * 256, :].rearrange("(k p) t -> p k t", p=128),
                      writes=[(XT.name, r, j)])
        P.act(lambda e: e.activation(out=NEA[:], in_=AB[:, li, 0, :], func=AF.Exp), reads=[AB], writes=[NEA])
        P.dve(lambda e: e.tensor_scalar(out=NEA[:], in0=NEA[:], scalar1=-1.0, scalar2=None, op0=ALU.mult), reads=[NEA], writes=[NEA])
        wv = w_view(w_in)
        P.dma("pool", wba[:], wv[:, :, 6144:6176], writes=[wba])

        def loadw(wbk):
            P.dma("pool", wb[wbk % 2][:], wv[:, :, wbk * 256:(wbk + 1) * 256], writes=[wb[wbk % 2]])

        loadw(0)
        for wbk in range(24):
            if wbk + 1 < 24:
                loadw(wbk + 1)
            b = wbk % 2
            for cti in range(2):
                ct = wbk * 2 + cti
                typ = "q" if ct < 8 else "k" if ct < 16 else "v" if ct < 32 else "z"
                ob = obuf[ct % 2]
                for blk in range(8):
                    pa = psa.next()
                    osl = ob[:, blk * 512:(blk + 1) * 512]
                    okey = (ob.name, blk)
                    for k in range(KD):
                        P.pe(lambda e, pa=pa, k=k, cti=cti, blk=blk, b=b: e.matmul(
                            pa[:], lhsT=wb[b][:, k, cti * 128:(cti + 1) * 128], rhs=XT[:, k, blk * 512:(blk + 1) * 512],
                            start=(k == 0), stop=(k == KD - 1)),
                            reads=[wb[b], (XT.name, blk // 4, k // 2)], writes=[pa])
                    if typ == "z":
                        P.act(lambda e, pa=pa, osl=osl: e.activation(out=osl, in_=pa[:], func=AF.Silu), writes=[pa, okey])
                        continue
                    xb_, prev = xbuf[blk % 2], xbuf[(blk + 1) % 2]
                    if blk == 0:
                        P.dve(lambda e, xb_=xb_: e.memset(xb_[:, 0:3], 0.0), writes=[(xb_.name, "h")])
                    else:
                        P.dve(lambda e, xb_=xb_, prev=prev: e.tensor_copy(out=xb_[:, 0:3], in_=prev[:, 512:515]),
                              reads=[prev], writes=[(xb_.name, "h")])
                    P.act(lambda e, pa=pa, xb_=xb_: e.activation(out=xb_[:, 3:515], in_=pa[:], func=AF.Copy), writes=[pa, xb_])
                    ca = cacc.next()
                    P.dve(lambda e, ca=ca, xb_=xb_, ct=ct: e.tensor_scalar(out=ca[:], in0=xb_[:, 3:515], scalar1=CW[:, li, ct, 3:4],
                                                                        scalar2=None, op0=ALU.mult),
                          reads=[xb_, CW], writes=[ca])
                    for j in (2, 1, 0):
                        P.dve(lambda e, ca=ca, xb_=xb_, ct=ct, j=j: e.scalar_tensor_tensor(
                            out=ca[:], in0=xb_[:, j:j + 512], scalar=CW[:, li, ct, j:j + 1], in1=ca[:], op0=ALU.mult, op1=ALU.add),
                            reads=[xb_, (xb_.name, "h"), CW], writes=[ca])
                    if typ == "v":
                        P.act(lambda e, ca=ca, osl=osl: e.activation(out=osl, in_=ca[:], func=AF.Silu), reads=[ca], writes=[okey])
                        continue
                    s_, q_, d_, r_ = sl.next(), sq.next(), sd.next(), ri.next()
                    P.act(lambda e, ca=ca, s_=s_: e.activation(out=s_[:], in_=ca[:], func=AF.Silu), reads=[ca], writes=[s_])
                    P.act(lambda e, s_=s_, q_=q_: e.activation(out=q_[:], in_=s_[:], func=AF.Square), reads=[s_], writes=[q_])
                    ps_ = pss.next()
                    P.pe(lambda e, ps_=ps_, q_=q_: e.matmul(ps_[:], lhsT=onesb[:], rhs=q_[:], start=True, stop=True),
                         reads=[q_, onesb], writes=[ps_])
                    P.act(lambda e, ps_=ps_, d_=d_: e.activation(out=d_[:], in_=ps_[:], func=AF.Sqrt, bias=eps[:, 0:1], scale=1.0),
                          reads=[eps], writes=[ps_, d_])
                    P.dve(lambda e, d_=d_, r_=r_: e.reciprocal(out=r_[:], in_=d_[:]), reads=[d_], writes=[r_])
                    qs = (128.0 ** -0.5) if typ == "q" else 1.0
                    P.dve(lambda e, s_=s_, r_=r_, osl=osl, qs=qs: e.scalar_tensor_tensor(out=osl, in0=s_[:], scalar=qs, in1=r_[:],
                                                                                     op0=ALU.mult, op1=ALU.mult),
                          reads=[s_, r_], writes=[okey])
                P.dma("sp", QKVZ[ct], ob[:], reads=[(ob.name, blk) for blk in range(8)])
        for tt in range(32):
            pb = psb.next()
            for k in range(KD):
                P.pe(lambda e, pb=pb, k=k, tt=tt: e.matmul(pb[:, 0:32], lhsT=XT[:, k, tt * 128:(tt + 1) * 128], rhs=wba[:, k, :],
                                                        start=(k == 0), stop=(k == KD - 1)),
                     reads=[wba, (XT.name, tt // 16, k // 2)], writes=[pb])
            P.act(lambda e, pb=pb, tt=tt: e.activation(out=BG[:, tt, 0:16], in_=pb[:, 0:16], func=AF.Sigmoid), writes=[pb, (BG.name, tt, 0)])
            P.dve(lambda e, pb=pb: e.tensor_tensor(out=xa[:], in0=pb[:, 16:32], in1=AB[:, li, 1, :], op=ALU.add), reads=[AB], writes=[pb, xa])
            P.act(lambda e: e.activation(out=ea[:], in_=xa[:], func=AF.Exp), reads=[xa], writes=[ea])
            P.act(lambda e: e.activation(out=sp_[:], in_=ea[:], func=AF.Ln, bias=C["one"][:, 0:1], scale=1.0), reads=[ea, C["one"]], writes=[sp_])
            P.dve(lambda e, tt=tt: e.tensor_tensor(out=BG[:, tt, 16:32], in0=sp_[:], in1=NEA[:], op=ALU.mult),
                  reads=[sp_, NEA], writes=[(BG.name, tt, 1)])
        P.emit()


def gdn_scan(kb, C, G, li, QKVZ, OT_lo, OT_hi):
    import contextlib
    CF, identb, onesb, NV, eps = C["CF"], C["identb"], C["onesb"], C["NV"], C["eps"]
    identf, U, MASKL, MASKLT = CF[:, 0, :], CF[:, 1, :], CF[:, 2, :], CF[:, 3, :]
    onesf = CF[:, 4, :]
    BG, SEL16 = G["BG"], G["SEL16"]
    qkvz_v = QKVZ.rearrange("c p t -> p c t")
    with contextlib.ExitStack() as st:
        S = kb.sb(st, "S", [128, HL, 128], F32)
        Sbf = kb.sb(st, "Sbf", [128, HL, 128], BF16)
        inb = [kb.sb(st, "inb", [128, 48, 512], BF16) for _ in range(1)]
        otst = [kb.sb(st, "otst", [128, HL, 512], BF16) for _ in range(1)]
        tk = {n: kb.sb(st, n, [128, 16], F32) for n in ("gc", "egc", "bege", "edl", "negb", "glb")}
        gcT = kb.sb(st, "gcT", [16, 128], F32)

        def gt(name, dt, depth=1):
            return PRing(Ring([kb.sb(st, name, [128, 4, 128], dt) for _ in range(depth)]),
                         Ring([kb.sb(st, name, [128, 4, 128], dt) for _ in range(depth)]))

        Lm, LT, ER = gt("Lm", F32), gt("LT", F32), gt("ER", F32)
        tmpA, tmpB = gt("tmpA", F32), gt("tmpB", F32)
        Yb = [gt("Y0", BF16)]
        Pb = [gt("P0", BF16)]
        ymr, pmr, w1r, w2r, tcr, ttr = gt("YM", BF16), gt("PM", BF16), gt("W1", BF16), gt("W2", BF16), gt("Tc", BF16, 2), gt("TTc", BF16, 2)
        t1, rr, vnew, MT, qd, kdec = gt("t1", F32), gt("rr", BF16), gt("vnew", BF16), gt("MT", BF16), gt("qd", BF16), gt("kdec", BF16)
        osb, osq, sdn, rsn, onn = gt("osb", F32), gt("osq", BF16), gt("sdn", F32), gt("rsn", F32), gt("onn", F32)
        psf = PRing(Ring([kb.ps(st, "psf", [128, 4, 128]) for _ in range(3)]), Ring([kb.ps(st, "psf", [128, 4, 128]) for _ in range(3)]))
        psh = PRing(Ring([kb.ps(st, "psh", [128, 8, 128], BF16) for _ in range(1)]), Ring([kb.ps(st, "psh", [128, 8, 128], BF16) for _ in range(1)]))

        G["LM"] = kb.sb(st, "LM", [128, 15, 4, 128], BF16)
        P = kb.prog()
        P.dma("pool", G["LM"][:], G["lvlmask"].rearrange("p (l a m) -> p l a m", l=15, a=4), writes=[G["LM"]])
        P.dve(lambda e: e.memset(S[:], 0.0), writes=[S])
        P.dve(lambda e: e.memset(Sbf[:], 0.0), writes=[Sbf])
        P.emit()

        for sc in range(8):
            P = kb.prog()
            ib = inb[0]
            ot = otst[0]
            for q4 in range(4):
                P.dma("sp", ib[:, q4 * 12:(q4 + 1) * 12, :], qkvz_v[:, q4 * 12:(q4 + 1) * 12, sc * 512:(sc + 1) * 512],
                      writes=[(ib.name, q4)])
            ibk = [(ib.name, q4) for q4 in range(4)]
            def chunk(cl, sc=sc, P=P):
                n = sc * 4 + cl
                cs = slice(cl * 128, (cl + 1) * 128)
                qT = lambda hq: ib[:, hq, cs]
                kT = lambda hq: ib[:, 8 + hq, cs]
                vT = lambda h: ib[:, 16 + h, cs]
                zT = lambda h: ib[:, 32 + h, cs]
                beta, g_ = BG[:, n, 0:16], BG[:, n, 16:32]
                bgk = [(BG.name, n, 0), (BG.name, n, 1)]
                pg = psf.next()
                P.pe(lambda e, pg=pg, g_=g_: e.matmul(pg[:, 0, 0:16], lhsT=U, rhs=g_, start=True, stop=True), reads=[CF] + bgk, writes=[pg])
                P.pe(lambda e, pg=pg, g_=g_: e.matmul(pg[:, 1, 0:16], lhsT=onesf, rhs=g_, start=True, stop=True), reads=[CF] + bgk, writes=[pg])
                P.dve(lambda e, pg=pg: e.tensor_copy(out=tk["gc"][:], in_=pg[:, 0, 0:16]), writes=[pg, tk["gc"]])
                P.dve(lambda e, pg=pg: e.tensor_copy(out=tk["glb"][:], in_=pg[:, 1, 0:16]), writes=[pg, tk["glb"]])
                P.act(lambda e: e.activation(out=tk["egc"][:], in_=tk["gc"][:], func=AF.Exp), reads=[tk["gc"]], writes=[tk["egc"]])
                P.dve(lambda e, beta=beta: e.tensor_tensor(out=tk["bege"][:], in0=tk["egc"][:], in1=beta, op=ALU.mult),
                      reads=[tk["egc"]] + bgk, writes=[tk["bege"]])
                P.dve(lambda e: e.tensor_tensor(out=tk["edl"][:], in0=tk["glb"][:], in1=tk["gc"][:], op=ALU.subtract),
                      reads=[tk["glb"], tk["gc"]], writes=[tk["edl"]])
                P.act(lambda e: e.activation(out=tk["edl"][:], in_=tk["edl"][:], func=AF.Exp), reads=[tk["edl"]], writes=[tk["edl"]])
                P.dve(lambda e, beta=beta: e.tensor_scalar(out=tk["negb"][:], in0=beta, scalar1=-1.0, scalar2=None, op0=ALU.mult),
                      reads=bgk, writes=[tk["negb"]])
                pt_ = psf.next()
                P.pe(lambda e, pt_=pt_: e.transpose(out=pt_[0:16, 0, :], in_=tk["gc"][:], identity=identf), reads=[tk["gc"], CF], writes=[pt_])
                P.dve(lambda e, pt_=pt_: e.tensor_copy(out=gcT[:], in_=pt_[0:16, 0, :]), writes=[pt_, gcT])

                def group(g4):
                    PAR[0] = g4 % 2
                    hs = [g4 * 4 + i for i in range(4)]
                    hqs = [g4 * 2, g4 * 2 + 1]
                    Lm_, LT_, ER_, tA, tB = Lm.next(), LT.next(), ER.next(), tmpA.next(), tmpB.next()
                    pgr = psf.next()
                    for i, h in enumerate(hs):
                        P.pe(lambda e, pgr=pgr, i=i, h=h: e.matmul(pgr[:, i, :], lhsT=SEL16[:, h, :], rhs=gcT[:], start=True, stop=True),
                             reads=[SEL16, gcT], writes=[pgr])
                    for i, h in enumerate(hs):
                        gcol = tk["gc"][:, h:h + 1]
                        P.dve(lambda e, pgr=pgr, i=i, gcol=gcol, tA=tA: e.scalar_tensor_tensor(
                            out=tA[:, i, :], in0=pgr[:, i, :], scalar=gcol, in1=MASKL, op0=ALU.subtract, op1=ALU.add),
                            reads=[tk["gc"], CF], writes=[pgr, (tA.name, i)])
                        P.dve(lambda e, pgr=pgr, i=i, gcol=gcol, tB=tB: e.scalar_tensor_tensor(
                            out=tB[:, i, :], in0=pgr[:, i, :], scalar=gcol, in1=MASKLT, op0=ALU.subtract, op1=ALU.subtract),
                            reads=[tk["gc"], CF], writes=[pgr, (tB.name, i)])
                    P.act(lambda e, pgr=pgr, ER_=ER_: e.activation(out=ER_[:], in_=pgr[:], func=AF.Exp), writes=[pgr] + [(ER_.name, i) for i in range(4)])
                    P.act(lambda e, tA=tA, Lm_=Lm_: e.activation(out=Lm_[:], in_=tA[:], func=AF.Exp, scale=-1.0),
                          reads=[(tA.name, i) for i in range(4)], writes=[(Lm_.name, i) for i in range(4)])
                    P.act(lambda e, tB=tB, LT_=LT_: e.activation(out=LT_[:], in_=tB[:], func=AF.Exp),
                          reads=[(tB.name, i) for i in range(4)], writes=[(LT_.name, i) for i in range(4)])
                    pkk = psf.next()
                    for a, hq in enumerate(hqs):
                        P.pe(lambda e, pkk=pkk, a=a, hq=hq: e.matmul(pkk[:, a, :], lhsT=kT(hq), rhs=kT(hq), start=True, stop=True),
                             reads=ibk, writes=[pkk])
                        P.pe(lambda e, pkk=pkk, a=a, hq=hq: e.matmul(pkk[:, 2 + a, :], lhsT=kT(hq), rhs=qT(hq), start=True, stop=True),
                             reads=ibk, writes=[pkk])
                    Y, Pm = Yb[0].next(), Pb[0].next()
                    MT_ = MT.next()
                    for i, h in enumerate(hs):
                        P.dve(lambda e, pkk=pkk, i=i, h=h, Y=Y, Lm_=Lm_: e.scalar_tensor_tensor(
                            out=Y[:, i, :], in0=pkk[:, i // 2, :], scalar=tk["negb"][:, h:h + 1], in1=Lm_[:, i, :], op0=ALU.mult, op1=ALU.mult),
                            reads=[tk["negb"], (Lm_.name, i)], writes=[pkk, (Y.name, i)])
                        P.dve(lambda e, pkk=pkk, i=i, MT_=MT_, LT_=LT_: e.tensor_tensor(
                            out=MT_[:, i, :], in0=pkk[:, 2 + i // 2, :], in1=LT_[:, i, :], op=ALU.mult),
                            reads=[(LT_.name, i)], writes=[pkk, (MT_.name, i)])
                    ph = psh.next()
                    for i in range(4):
                        P.pe(lambda e, ph=ph, i=i, Y=Y: e.transpose(out=ph[:, i, :], in_=Y[:, i, :], identity=identb[:]),
                             reads=[(Y.name, i), identb], writes=[ph])
                    P.act(lambda e, ph=ph, Pm=Pm: e.activation(out=Pm[:], in_=ph[:, 0:4, :], func=AF.Copy),
                          writes=[ph] + [(Pm.name, i) for i in range(4)])
                    k4_ = lambda t_: [(t_.name, i) for i in range(4)]
                    LMt = G["LM"]
                    Tc = TTc = None
                    for l in range(7):
                        YM, PM = ymr.next(), pmr.next()
                        P.dve(lambda e, YM=YM, Y=Y, l=l: e.tensor_tensor(out=YM[:], in0=Y[:], in1=LMt[:, l, :, :], op=ALU.mult),
                              reads=k4_(Y) + [LMt], writes=k4_(YM))
                        P.dve(lambda e, PM=PM, Pm=Pm, l=l: e.tensor_tensor(out=PM[:], in0=Pm[:], in1=LMt[:, 7 + l, :, :], op=ALU.mult),
                              reads=k4_(Pm) + [LMt], writes=k4_(PM))
                        Tn, TTn = tcr.next(), ttr.next()
                        if l == 0:
                            P.dve(lambda e, TTn=TTn, PM=PM: e.tensor_tensor(out=TTn[:], in0=PM[:], in1=LMt[:, 14, :, :], op=ALU.add),
                                  reads=k4_(PM) + [LMt], writes=k4_(TTn))
                            P.dve(lambda e, Tn=Tn, YM=YM: e.tensor_tensor(out=Tn[:], in0=YM[:], in1=LMt[:, 14, :, :], op=ALU.add),
                                  reads=k4_(YM) + [LMt], writes=k4_(Tn))
                        else:
                            W1 = w1r.next()
                            pw = psf.next()
                            for i in range(4):
                                P.pe(lambda e, pw=pw, i=i, YM=YM, TTc=TTc: e.matmul(pw[:, i, :], lhsT=YM[:, i, :], rhs=TTc[:, i, :], start=True, stop=True),
                                     reads=[(YM.name, i), (TTc.name, i)], writes=[pw])
                            if l < 6:
                                W2 = w2r.next()
                                pw2 = psf.next()
                                for i in range(4):
                                    P.pe(lambda e, pw2=pw2, i=i, PM=PM, Tc=Tc: e.matmul(pw2[:, i, :], lhsT=PM[:, i, :], rhs=Tc[:, i, :], start=True, stop=True),
                                         reads=[(PM.name, i), (Tc.name, i)], writes=[pw2])
                            P.act(lambda e, pw=pw, W1=W1: e.activation(out=W1[:], in_=pw[:], func=AF.Copy), writes=[pw] + k4_(W1))
                            if l < 6:
                                P.act(lambda e, pw2=pw2, W2=W2: e.activation(out=W2[:], in_=pw2[:], func=AF.Copy), writes=[pw2] + k4_(W2))
                            pt2 = psf.next()
                            for i in range(4):
                                P.pe(lambda e, pt2=pt2, i=i, Tc=Tc, W1=W1: e.matmul(pt2[:, i, :], lhsT=Tc[:, i, :], rhs=W1[:, i, :], start=True, stop=True),
                                     reads=[(Tc.name, i), (W1.name, i)], writes=[pt2])
                            if l < 6:
                                pt3 = psf.next()
                                for i in range(4):
                                    P.pe(lambda e, pt3=pt3, i=i, TTc=TTc, W2=W2: e.matmul(pt3[:, i, :], lhsT=TTc[:, i, :], rhs=W2[:, i, :], start=True, stop=True),
                                         reads=[(TTc.name, i), (W2.name, i)], writes=[pt3])
                            P.dve(lambda e, pt2=pt2, TTn=TTn, TTc=TTc: e.tensor_tensor(out=TTn[:], in0=pt2[:], in1=TTc[:], op=ALU.add),
                                  reads=k4_(TTc), writes=[pt2] + k4_(TTn))
                            if l < 6:
                                P.dve(lambda e, pt3=pt3, Tn=Tn, Tc=Tc: e.tensor_tensor(out=Tn[:], in0=pt3[:], in1=Tc[:], op=ALU.add),
                                      reads=k4_(Tc), writes=[pt3] + k4_(Tn))
                        Tc, TTc = Tn, TTn
                    R = TTc
                    TT = R
                    pks = psf.next()
                    for i, h in enumerate(hs):
                        P.pe(lambda e, pks=pks, i=i, h=h: e.matmul(pks[:, i, :], lhsT=kT(h // 2), rhs=Sbf[:, h, :], start=True, stop=True),
                             reads=ibk + [(Sbf.name, h)], writes=[pks])
                    pv = psh.next()
                    for i, h in enumerate(hs):
                        P.pe(lambda e, pv=pv, i=i, h=h: e.transpose(out=pv[:, i, :], in_=vT(h), identity=identb[:]), reads=ibk + [identb], writes=[pv])
                    for a, hq in enumerate(hqs):
                        P.pe(lambda e, pv=pv, a=a, hq=hq: e.transpose(out=pv[:, 4 + a, :], in_=kT(hq), identity=identb[:]), reads=ibk + [identb], writes=[pv])
                    t1_, rr_, vn_, qd_, kd_ = t1.next(), rr.next(), vnew.next(), qd.next(), kdec.next()
                    for i, h in enumerate(hs):
                        P.dve(lambda e, pks=pks, i=i, h=h, t1_=t1_: e.tensor_scalar(out=t1_[:, i, :], in0=pks[:, i, :], scalar1=tk["bege"][:, h:h + 1],
                                                                                 scalar2=None, op0=ALU.mult),
                              reads=[tk["bege"]], writes=[pks, (t1_.name, i)])
                        P.dve(lambda e, pv=pv, i=i, h=h, t1_=t1_, rr_=rr_: e.scalar_tensor_tensor(
                            out=rr_[:, i, :], in0=pv[:, i, :], scalar=BG[:, n, h:h + 1], in1=t1_[:, i, :], op0=ALU.mult, op1=ALU.subtract),
                            reads=bgk + [(t1_.name, i)], writes=[pv, (rr_.name, i)])
                        P.act(lambda e, pv=pv, i=i, h=h, kd_=kd_: e.activation(out=kd_[:, i, :], in_=pv[:, 4 + i // 2, :], func=AF.Copy,
                                                                             scale=tk["edl"][:, h:h + 1]),
                              reads=[tk["edl"]], writes=[pv, (kd_.name, i)])
                    pvn = psf.next()
                    for i in range(4):
                        P.pe(lambda e, pvn=pvn, i=i, TT=TT, rr_=rr_: e.matmul(pvn[:, i, :], lhsT=TT[:, i, :], rhs=rr_[:, i, :], start=True, stop=True),
                             reads=[(TT.name, i), (rr_.name, i)], writes=[pvn])
                    P.act(lambda e, pvn=pvn, vn_=vn_: e.activation(out=vn_[:], in_=pvn[:], func=AF.Copy),
                          writes=[pvn] + [(vn_.name, i) for i in range(4)])
                    for i, h in enumerate(hs):
                        P.dve(lambda e, i=i, h=h, qd_=qd_, ER_=ER_: e.tensor_tensor(out=qd_[:, i, :], in0=qT(h // 2), in1=ER_[:, i, :], op=ALU.mult),
                              reads=ibk + [(ER_.name, i)], writes=[(qd_.name, i)])
                    po = psf.next()
                    for i, h in enumerate(hs):
                        P.pe(lambda e, po=po, i=i, h=h, qd_=qd_: e.matmul(po[:, i, :], lhsT=Sbf[:, h, :], rhs=qd_[:, i, :], start=True, stop=False),
                             reads=[(Sbf.name, h), (qd_.name, i)], writes=[po])
                        P.pe(lambda e, po=po, i=i, vn_=vn_, MT_=MT_: e.matmul(po[:, i, :], lhsT=vn_[:, i, :], rhs=MT_[:, i, :], start=False, stop=True),
                             reads=[(vn_.name, i), (MT_.name, i)], writes=[po])
                    pds = psf.next()
                    for i in range(4):
                        P.pe(lambda e, pds=pds, i=i, kd_=kd_, vn_=vn_: e.matmul(pds[:, i, :], lhsT=kd_[:, i, :], rhs=vn_[:, i, :], start=True, stop=True),
                             reads=[(kd_.name, i), (vn_.name, i)], writes=[pds])
                    for i, h in enumerate(hs):
                        P.dve(lambda e, pds=pds, i=i, h=h, ER_=ER_: e.scalar_tensor_tensor(
                            out=S[:, h, :], in0=S[:, h, :], scalar=ER_[:, i, 127:128], in1=pds[:, i, :], op0=ALU.mult, op1=ALU.add),
                            reads=[(ER_.name, i), (S.name, h)], writes=[pds, (S.name, h)])
                        P.act(lambda e, h=h: e.activation(out=Sbf[:, h, :], in_=S[:, h, :], func=AF.Copy), reads=[(S.name, h)], writes=[(Sbf.name, h)])
                    ob_, oq_, sd_, rs_, on_ = osb.next(), osq.next(), sdn.next(), rsn.next(), onn.next()
                    k4 = lambda t_: [(t_.name, i) for i in range(4)]
                    P.act(lambda e, po=po, ob_=ob_: e.activation(out=ob_[:], in_=po[:], func=AF.Copy), writes=[po] + k4(ob_))
                    P.act(lambda e, ob_=ob_, oq_=oq_: e.activation(out=oq_[:], in_=ob_[:], func=AF.Square), reads=k4(ob_), writes=k4(oq_))
                    pss = psf.next()
                    for i in range(4):
                        P.pe(lambda e, pss=pss, i=i, oq_=oq_: e.matmul(pss[:, i, :], lhsT=onesb[:], rhs=oq_[:, i, :], start=True, stop=True),
                             reads=[onesb, (oq_.name, i)], writes=[pss])
                    P.act(lambda e, pss=pss, sd_=sd_: e.activation(out=sd_[:], in_=pss[:], func=AF.Sqrt, bias=eps[:, 0:1], scale=1.0 / 128),
                          reads=[eps], writes=[pss] + k4(sd_))
                    P.dve(lambda e, sd_=sd_, rs_=rs_: e.reciprocal(out=rs_[:], in_=sd_[:]), reads=k4(sd_), writes=k4(rs_))
                    P.dve(lambda e, ob_=ob_, rs_=rs_, on_=on_: e.tensor_tensor(out=on_[:], in0=ob_[:], in1=rs_[:], op=ALU.mult),
                          reads=k4(ob_) + k4(rs_), writes=k4(on_))
                    for i, h in enumerate(hs):
                        P.dve(lambda e, i=i, h=h, on_=on_: e.scalar_tensor_tensor(
                            out=ot[:, h, cs], in0=on_[:, i, :], scalar=NV[:, 208 + li:209 + li], in1=zT(h), op0=ALU.mult, op1=ALU.mult),
                            reads=[(on_.name, i), NV] + ibk, writes=[(ot.name, h, cl)])
                for ga, gb_ in ((0, 1), (2, 3)):
                    n0 = len(P.ops)
                    group(ga)
                    opsA = P.ops[n0:]
                    del P.ops[n0:]
                    group(gb_)
                    opsB = P.ops[n0:]
                    del P.ops[n0:]
                    PAR[0] = 0
                    for k_ in range(max(len(opsA), len(opsB))):
                        if k_ < len(opsA):
                            P.ops.append(opsA[k_])
                        if k_ < len(opsB):
                            P.ops.append(opsB[k_])

            for cl in range(4):
                chunk(cl)
            dst = OT_lo if sc < 4 else OT_hi
            dv = dst.rearrange("(h e) t -> e h t", e=128)
            P.dma("sp", dv[:, :, (sc % 4) * 512:(sc % 4 + 1) * 512], ot[:],
                  reads=[(ot.name, h, cl) for h in range(HL) for cl in range(4)])
            P.emit()


def gdn_outproj(kb, C, G, H, G_lo, G_hi, w_out):
    import contextlib
    SELC = G["SELC"]
    wv = w_out.rearrange("(c p) n -> p c n", p=128)
    for tb in range(4):
        with contextlib.ExitStack() as st:
            HT = [kb.sb(st, "ht", [128, D], F32) for _ in range(4)]
            oa = kb.sb(st, "oa", [128, 32, 512], BF16)
            ob = kb.sb(st, "ob", [128, 32, 512], BF16)
            wo = [kb.sb(st, "wo", [128, 32, 512], BF16) for _ in range(2)]
            psy = Ring([kb.ps(st, "psy", [128, 512]) for _ in range(4)])
            P = kb.prog()
            for t in range(4):
                r0 = (tb * 4 + t) * 128
                P.dma("sp", HT[t][:], H[r0:r0 + 128, :], writes=hk(HT[t]))
            for r in range(2):
                for j in range(8):
                    c0 = r * 16 + 2 * j
                    P.dma("sp", oa[:, c0:c0 + 2, :], G_lo[j, r * 256:(r + 1) * 256, :].rearrange("(k p) t -> p k t", p=128)[:, :, tb * 512:(tb + 1) * 512],
                          writes=[(oa.name, r, j)])
                    P.dma("sp", ob[:, c0:c0 + 2, :], G_hi[j, r * 256:(r + 1) * 256, :].rearrange("(k p) t -> p k t", p=128)[:, :, tb * 512:(tb + 1) * 512],
                          writes=[(ob.name, r, j)])
            for c2 in range(2):
                sl_ = slice(c2 * 16, (c2 + 1) * 16)
                P.dve(lambda e, sl_=sl_: e.tensor_scalar(out=oa[:, sl_, :], in0=oa[:, sl_, :], scalar1=SELC[:, 0:1], scalar2=None, op0=ALU.mult),
                      reads=[SELC] + [(oa.name, c2, j) for j in range(8)], writes=[(oa.name, c2)])
                P.dve(lambda e, sl_=sl_: e.scalar_tensor_tensor(out=oa[:, sl_, :], in0=ob[:, sl_, :], scalar=SELC[:, 1:2], in1=oa[:, sl_, :],
                                                              op0=ALU.mult, op1=ALU.add),
                      reads=[SELC] + [(ob.name, c2, j) for j in range(8)], writes=[(oa.name, c2)])
            P.dma("pool", wo[0][:], wv[:, :, 0:512], writes=[wo[0]])
            for db in range(4):
                b = db % 2
                if db + 1 < 4:
                    P.dma("pool", wo[1 - b][:], wv[:, :, (db + 1) * 512:(db + 2) * 512], writes=[wo[1 - b]])
                for t in range(4):
                    py = psy.next()
                    for c in range(32):
                        P.pe(lambda e, py=py, c=c, t=t, b=b: e.matmul(py[:], lhsT=oa[:, c, t * 128:(t + 1) * 128], rhs=wo[b][:, c, :],
                                                                    start=(c == 0), stop=(c == 31)),
                             reads=[wo[b], (oa.name, c // 16)], writes=[py])
                    hs_ = HT[t][:, db * 512:(db + 1) * 512]
                    P.dve(lambda e, hs_=hs_, py=py: e.tensor_tensor(out=hs_, in0=hs_, in1=py[:], op=ALU.add),
                          reads=[(HT[t].name, db)], writes=[py, (HT[t].name, db)])
            for t in range(4):
                r0 = (tb * 4 + t) * 128
                P.dma("sp", H[r0:r0 + 128, :], HT[t][:], reads=hk(HT[t]))
            P.emit()


def gdn_layer(kb, C, G, li, H, S_, ins):
    gdn_norm_gather(kb, C, H, li * 16, S_["XNT_own"], S_["XNT_all"])
    if DEBUG_CUT == 11:
        return
    gdn_inproj(kb, C, G, li, S_["XNT_all"], ins["w_in"][li], S_["QKVZ"])
    if DEBUG_CUT == 12:
        import contextlib
        ov = S_["out"].rearrange("(a b) c -> a (b c)", b=2)
        with contextlib.ExitStack() as st:
            tb_ = kb.sb(st, "dbb", [128, T], BF16)
            tf_ = kb.sb(st, "dbf", [128, T], F32)
            P = kb.prog()
            for i, ct in enumerate((0, 8, 16, 32, 7, 15, 31, 47)):
                P.dma("sp", tb_[:], S_["QKVZ"][ct], writes=[tb_])
                P.dve(lambda e: e.tensor_copy(out=tf_[:], in_=tb_[:]), reads=[tb_], writes=[tf_])
                P.dma("sp", ov[i * 128:(i + 1) * 128, :], tf_[:], reads=[tf_])
            P.dma("sp", ov[1024 - 128:1024, 0:1024], G["BG"][:].rearrange("p a b -> p (a b)"), reads=[])
            P.emit()
        return
    gdn_scan(kb, C, G, li, S_["QKVZ"], S_["OT_lo"], S_["OT_hi"])
    if DEBUG_CUT == 13:
        import contextlib
        with contextlib.ExitStack() as st:
            tb_ = kb.sb(st, "dbb", [128, TO], BF16)
            tf_ = kb.sb(st, "dbf", [128, TO], F32)
            P = kb.prog()
            for i in range(16):
                P.dma("sp", tb_[:], S_["OT_lo"][i * 128:(i + 1) * 128, :], writes=[tb_])
                P.dve(lambda e: e.tensor_copy(out=tf_[:], in_=tb_[:]), reads=[tb_], writes=[tf_])
                P.dma("sp", S_["out"][i * 128:(i + 1) * 128, :], tf_[:], reads=[tf_])
            P.emit()
        return
    allgather(kb, S_["OT_lo"], S_["G_lo"])
    allgather(kb, S_["OT_hi"], S_["G_hi"])
    gdn_outproj(kb, C, G, H, S_["G_lo"], S_["G_hi"], ins["w_out"][li])


def kv_pass(kb, C, H, w_kv, nvcol, KT_own, V_own):
    import contextlib
    wv = w_view(w_kv)
    for tb in range(2):
        with contextlib.ExitStack() as st:
            HT = [kb.sb(st, "ht", [128, D], F32) for _ in range(8)]
            xT = kb.sb(st, "xT", [128, KD, 1024], BF16)
            with contextlib.ExitStack() as st1:
                P = kb.prog()
                for t in range(8):
                    r0 = (tb * 8 + t) * 128
                    P.dma("sp", HT[t][:], H[r0:r0 + 128, :], writes=hk(HT[t]))
                norm_T(kb, P, st1, C, HT, nvcol, xT, tr_ring(kb, st1))
                P.emit()
            with contextlib.ExitStack() as st2:
                P = kb.prog()
                wb = [kb.sb(st2, "wkv", [128, KD, 512], BF16) for _ in range(2)]
                kst = Ring([kb.sb(st2, "kst", [128, 1024], BF16) for _ in range(2)])
                vst = Ring([kb.sb(st2, "vst", [128, 512], BF16) for _ in range(3)])
                psa = Ring([kb.ps(st2, "psa", [128, 512]) for _ in range(4)])
                P.dma("pool", wb[0][:], wv[:, :, 0:512], writes=[wb[0]])
                for blk in range(8):
                    b = blk % 2
                    if blk + 1 < 8:
                        P.dma("pool", wb[1 - b][:], wv[:, :, (blk + 1) * 512:(blk + 2) * 512], writes=[wb[1 - b]])
                    if blk < 4:
                        for ci in range(4):
                            hc = blk * 4 + ci
                            ks = kst.next()
                            for tg in range(2):
                                pa = psa.next()
                                for k in range(KD):
                                    P.pe(lambda e, pa=pa, k=k, ci=ci, tg=tg, b=b: e.matmul(pa[:], lhsT=wb[b][:, k, ci * 128:(ci + 1) * 128],
                                                                                         rhs=xT[:, k, tg * 512:(tg + 1) * 512],
                                                                                         start=(k == 0), stop=(k == KD - 1)),
                                         reads=[wb[b], (xT.name, k, tg)], writes=[pa])
                                P.act(lambda e, pa=pa, ks=ks, tg=tg: e.activation(out=ks[:, tg * 512:(tg + 1) * 512], in_=pa[:], func=AF.Copy),
                                      writes=[pa, (ks.name, tg)])
                            P.dma("sp", KT_own[hc * 128:(hc + 1) * 128, tb * 1024:(tb + 1) * 1024], ks[:], reads=[(ks.name, 0), (ks.name, 1)])
                    else:
                        db = blk - 4
                        for t in range(8):
                            pa = psa.next()
                            vs = vst.next()
                            for k in range(KD):
                                P.pe(lambda e, pa=pa, k=k, t=t, b=b: e.matmul(pa[:], lhsT=xT[:, k, t * 128:(t + 1) * 128], rhs=wb[b][:, k, :],
                                                                            start=(k == 0), stop=(k == KD - 1)),
                                     reads=[wb[b], (xT.name, k, t // 4)], writes=[pa])
                            P.act(lambda e, pa=pa, vs=vs: e.activation(out=vs[:], in_=pa[:], func=AF.Copy), writes=[pa, vs])
                            r0 = (tb * 8 + t) * 128
                            P.dma("sp", V_own[r0:r0 + 128, db * 512:(db + 1) * 512], vs[:], reads=[vs])
                P.emit()


def attn_layer(kb, C, A, li, H, S_, w_q, w_o, nvcol):
    import contextlib, math
    j = li - N_A
    lam_init = 0.8 - 0.6 * math.exp(-0.3 * li)
    QT, KT_all, V_all = S_["QT"], S_["KT_all"], S_["V_all"]
    identb, eps = C["identb"], C["eps"]
    P = kb.prog()
    LAMV, nlam, SW, SUBW = A["LAMV"], A["nlam"], A["SW"], A["SUBW"]
    l2t, lsum = A["l2t"], A["lsum"]
    P.dve(lambda e: e.memset(lsum[:], 0.0), writes=[lsum])
    for a in range(2):
        P.dve(lambda e, a=a: e.tensor_tensor(out=l2t[:], in0=LAMV[:, j, 2 * a, :], in1=LAMV[:, j, 2 * a + 1, :], op=ALU.mult),
              reads=[LAMV], writes=[l2t])
        P.act(lambda e, a=a: e.activation(out=l2t[:], in_=l2t[:], func=AF.Copy, accum_out=lsum[:, a:a + 1]), reads=[l2t, lsum], writes=[l2t, (lsum.name, a)])
    P.act(lambda e: e.activation(out=lsum[:], in_=lsum[:], func=AF.Exp), reads=[lsum, (lsum.name, 0), (lsum.name, 1)], writes=[lsum])
    P.dve(lambda e: e.tensor_tensor(out=nlam[:], in0=lsum[:, 1:2], in1=lsum[:, 0:1], op=ALU.subtract), reads=[lsum], writes=[nlam])
    P.dve(lambda e: e.tensor_scalar(out=nlam[:], in0=nlam[:], scalar1=-lam_init, scalar2=None, op0=ALU.add), reads=[nlam], writes=[nlam])
    P.dve(lambda e: e.tensor_scalar(out=SW[:], in0=SUBW[:, j, :], scalar1=1.0 - lam_init, scalar2=None, op0=ALU.mult), reads=[SUBW], writes=[SW])
    P.emit()
    wqv = w_view(w_q)
    for tb in range(2):
        with contextlib.ExitStack() as st:
            HT = [kb.sb(st, "ht", [128, D], F32) for _ in range(8)]
            xT = kb.sb(st, "xT", [128, KD, 1024], BF16)
            with contextlib.ExitStack() as st1:
                P = kb.prog()
                for t in range(8):
                    r0 = (tb * 8 + t) * 128
                    P.dma("sp", HT[t][:], H[r0:r0 + 128, :], writes=hk(HT[t]))
                norm_T(kb, P, st1, C, HT, nvcol, xT, tr_ring(kb, st1))
                P.emit()
            with contextlib.ExitStack() as st2:
                P = kb.prog()
                wb = [kb.sb(st2, "wq", [128, KD, 512], BF16) for _ in range(2)]
                qst = Ring([kb.sb(st2, "qst", [128, 1024], BF16) for _ in range(2)])
                psa = Ring([kb.ps(st2, "psa", [128, 512]) for _ in range(4)])
                P.dma("pool", wb[0][:], wqv[:, :, 0:512], writes=[wb[0]])
                for blk in range(4):
                    b = blk % 2
                    if blk + 1 < 4:
                        P.dma("pool", wb[1 - b][:], wqv[:, :, (blk + 1) * 512:(blk + 2) * 512], writes=[wb[1 - b]])
                    for ci in range(4):
                        hm = blk * 4 + ci
                        qs = qst.next()
                        for tg in range(2):
                            pa = psa.next()
                            for k in range(KD):
                                P.pe(lambda e, pa=pa, k=k, ci=ci, tg=tg, b=b: e.matmul(pa[:], lhsT=wb[b][:, k, ci * 128:(ci + 1) * 128],
                                                                                     rhs=xT[:, k, tg * 512:(tg + 1) * 512],
                                                                                     start=(k == 0), stop=(k == KD - 1)),
                                     reads=[wb[b], (xT.name, k, tg)], writes=[pa])
                            P.act(lambda e, pa=pa, qs=qs, tg=tg: e.activation(out=qs[:, tg * 512:(tg + 1) * 512], in_=pa[:], func=AF.Copy,
                                                                            scale=128.0 ** -0.5),
                                  writes=[pa, (qs.name, tg)])
                        P.dma("sp", QT[hm][:, tb * 1024:(tb + 1) * 1024], qs[:], reads=[(qs.name, 0), (qs.name, 1)])
                P.emit()
    with contextlib.ExitStack() as sto:
        OATT = kb.sb(sto, "OATT", [128, NT, D], BF16)
        with contextlib.ExitStack() as st:
            KTh = [kb.sb(st, "KTh", [128, 2, T], BF16) for _ in range(2)]
            Vh = [kb.sb(st, "Vh", [128, 32, 257], BF16) for _ in range(2)]
            QTh = [kb.sb(st, "QTh", [128, 2, TO], BF16) for _ in range(2)]
            pT = Ring([kb.sb(st, "pT", [128, 128], BF16) for _ in range(6)])
            o1 = Ring([kb.sb(st, "o1", [128, 256], F32) for _ in range(2)])
            oo = Ring([kb.sb(st, "oo", [128, 256], F32) for _ in range(2)])
            jk = kb.sb(st, "jk", [128, 256], BF16)
            sm = Ring([kb.sb(st, "sm", [128, 8], F32) for _ in range(4)])
            sring = Ring([kb.ps(st, "pss", [128, 512]) for _ in range(3)])
            av = [[kb.ps(st, "av", [128, 512]) for _ in range(2)] for _ in range(2)]
            BT, MA, MB = A["BT"], A["MA"], A["MB"]
            P = kb.prog()
            for b in range(2):
                P.dve(lambda e, b=b: e.memset(Vh[b][:, :, 256:257], 1.0), writes=[(Vh[b].name, "one")])
            P.emit()
            for h in range(8):
                b = h % 2
                P = kb.prog()
                for m in range(2):
                    for r in range(2):
                        P.dma("sp", KTh[b][:, m, r * TO:(r + 1) * TO], KT_all[h, r * 256 + m * 128:r * 256 + (m + 1) * 128, :], writes=[(KTh[b].name, m, r)])
                    P.dma("sp", QTh[b][:, m, :], QT[2 * h + m], writes=[(QTh[b].name, m)])
                for r in range(2):
                    for jj in range(8):
                        kt0 = r * 16 + 2 * jj
                        P.dma("sp", Vh[b][:, kt0:kt0 + 2, 0:256],
                              V_all[jj, r * 256:(r + 1) * 256, h * 256:(h + 1) * 256].rearrange("(a p) e -> p a e", p=128),
                              writes=[(Vh[b].name, kt0 // 2)])

                def qblock(qb, h=h, b=b, P=P):
                    i0 = 2 * qb
                    nkt = 18 + 2 * qb
                    its = [(kt, m) for kt in range(nkt) for m in range(2)]
                    psl = {}

                    def score(n_):
                        kt, m = its[n_]
                        ps = sring.next()
                        psl[n_] = ps
                        P.pe(lambda e, ps=ps, m=m, kt=kt: e.matmul(ps[:, 0:256], lhsT=KTh[b][:, m, kt * 128:(kt + 1) * 128],
                                                                 rhs=QTh[b][:, m, i0 * 128:(i0 + 2) * 128], start=True, stop=True),
                             reads=[(KTh[b].name, m, kt // 16), (QTh[b].name, m)], writes=[ps])

                    score(0)
                    score(1)
                    for n_ in range(len(its)):
                        if n_ + 2 < len(its):
                            score(n_ + 2)
                        kt, m = its[n_]
                        ps = psl.pop(n_)
                        for jj in range(2):
                            i = i0 + jj
                            if kt > 16 + i:
                                continue
                            p_ = pT.next()
                            dd = i - kt + 16
                            P.act(lambda e, ps=ps, p_=p_, jj=jj, dd=dd: e.activation(out=p_[:], in_=ps[:, jj * 128:(jj + 1) * 128], func=AF.Exp,
                                                                                   bias=BT[:, h, dd:dd + 1], scale=1.0),
                                  reads=[BT], writes=[ps, p_])
                            if kt == i:
                                P.dve(lambda e, p_=p_: e.tensor_tensor(out=p_[:], in0=p_[:], in1=MA[:], op=ALU.mult), reads=[MA], writes=[p_])
                            if kt == 16 + i:
                                P.dve(lambda e, p_=p_: e.tensor_tensor(out=p_[:], in0=p_[:], in1=MB[:], op=ALU.mult), reads=[MB], writes=[p_])
                            acc = av[jj][m]
                            P.pe(lambda e, acc=acc, p_=p_, kt=kt, i=i: e.matmul(acc[:, 0:257], lhsT=p_[:], rhs=Vh[b][:, kt, :],
                                                                              start=(kt == 0), stop=(kt == 16 + i)),
                                 reads=[p_, (Vh[b].name, kt // 2), (Vh[b].name, "one")], writes=[acc])
                    for jj in range(2):
                        i = i0 + jj
                        s_ = sm.next()
                        a0, a1 = av[jj][0], av[jj][1]
                        o1_, oo_ = o1.next(), oo.next()
                        P.dve(lambda e, s_=s_, a0=a0: e.reciprocal(out=s_[:, 0:1], in_=a0[:, 256:257]), writes=[a0, (s_.name, 0)])
                        P.dve(lambda e, s_=s_, a1=a1: e.reciprocal(out=s_[:, 1:2], in_=a1[:, 256:257]), writes=[a1, (s_.name, 1)])
                        P.dve(lambda e, s_=s_: e.tensor_tensor(out=s_[:, 2:3], in0=s_[:, 1:2], in1=nlam[:], op=ALU.mult),
                              reads=[(s_.name, 1), nlam], writes=[(s_.name, 2)])
                        P.dve(lambda e, s_=s_, a0=a0, o1_=o1_: e.tensor_scalar(out=o1_[:], in0=a0[:, 0:256], scalar1=s_[:, 0:1], scalar2=None, op0=ALU.mult),
                              reads=[(s_.name, 0)], writes=[a0, o1_])
                        P.dve(lambda e, s_=s_, a1=a1, o1_=o1_, oo_=oo_: e.scalar_tensor_tensor(out=oo_[:], in0=a1[:, 0:256], scalar=s_[:, 2:3], in1=o1_[:],
                                                                                             op0=ALU.mult, op1=ALU.add),
                              reads=[(s_.name, 2), o1_], writes=[a1, oo_])
                        P.dve(lambda e, s_=s_: e.memset(s_[:, 3:4], 0.0), writes=[(s_.name, 3)])
                        P.act(lambda e, s_=s_, oo_=oo_: e.activation(out=jk[:], in_=oo_[:], func=AF.Square, accum_out=s_[:, 3:4]),
                              reads=[oo_, (s_.name, 3)], writes=[jk, (s_.name, 4)])
                        P.act(lambda e, s_=s_: e.activation(out=s_[:, 5:6], in_=s_[:, 3:4], func=AF.Sqrt, bias=eps[:, 0:1], scale=1.0 / 256),
                              reads=[(s_.name, 4), eps], writes=[(s_.name, 5)])
                        P.dve(lambda e, s_=s_: e.reciprocal(out=s_[:, 6:7], in_=s_[:, 5:6]), reads=[(s_.name, 5)], writes=[(s_.name, 6)])
                        P.dve(lambda e, s_=s_, oo_=oo_, i=i: e.scalar_tensor_tensor(out=OATT[:, i, h * 256:(h + 1) * 256], in0=oo_[:], scalar=s_[:, 6:7],
                                                                                  in1=SW[:], op0=ALU.mult, op1=ALU.mult),
                              reads=[oo_, (s_.name, 6), SW], writes=[(OATT.name, i, h)])

                for qb in range(8):
                    qblock(qb)
                P.emit()
        wov = w_view(w_o)
        for tb in range(4):
            with contextlib.ExitStack() as st:
                HT = [kb.sb(st, "ht", [128, D], F32) for _ in range(4)]
                oT = kb.sb(st, "oT", [128, KD, 512], BF16)
                wo = [kb.sb(st, "wo", [128, KD, 512], BF16) for _ in range(2)]
                trr = Ring([kb.ps(st, "ptr3", [128, 8, 128], BF16) for _ in range(2)])
                psy = Ring([kb.ps(st, "psy", [128, 512]) for _ in range(4)])
                P = kb.prog()
                for t in range(4):
                    r0 = (tb * 4 + t) * 128
                    P.dma("sp", HT[t][:], H[r0:r0 + 128, :], writes=hk(HT[t]))
                for t in range(4):
                    i = tb * 4 + t
                    for c2 in range(2):
                        pt = trr.next()
                        for cc in range(8):
                            c = c2 * 8 + cc
                            P.pe(lambda e, pt=pt, cc=cc, c=c, i=i: e.transpose(out=pt[:, cc, :], in_=OATT[:, i, c * 128:(c + 1) * 128],
                                                                             identity=identb[:]),
                                 reads=[(OATT.name, i, c // 2), identb], writes=[pt])
                        P.act(lambda e, pt=pt, c2=c2, t=t: e.activation(out=oT[:, c2 * 8:(c2 + 1) * 8, t * 128:(t + 1) * 128],
                                                                      in_=pt[:], func=AF.Copy),
                              writes=[pt, (oT.name, c2, t)])
                P.dma("pool", wo[0][:], wov[:, :, 0:512], writes=[wo[0]])
                for db in range(4):
                    b = db % 2
                    if db + 1 < 4:
                        P.dma("pool", wo[1 - b][:], wov[:, :, (db + 1) * 512:(db + 2) * 512], writes=[wo[1 - b]])
                    for t in range(4):
                        py = psy.next()
                        for c in range(KD):
                            P.pe(lambda e, py=py, c=c, t=t, b=b: e.matmul(py[:], lhsT=oT[:, c, t * 128:(t + 1) * 128], rhs=wo[b][:, c, :],
                                                                        start=(c == 0), stop=(c == KD - 1)),
                                 reads=[wo[b], (oT.name, c // 8, t)], writes=[py])
                        hs_ = HT[t][:, db * 512:(db + 1) * 512]
                        P.dve(lambda e, hs_=hs_, py=py: e.tensor_tensor(out=hs_, in0=hs_, in1=py[:], op=ALU.add),
                              reads=[(HT[t].name, db)], writes=[py, (HT[t].name, db)])
                for t in range(4):
                    r0 = (tb * 4 + t) * 128
                    P.dma("sp", H[r0:r0 + 128, :], HT[t][:], reads=hk(HT[t]))
                P.emit()
```
